# Optimizing a Trainium2 kernel written in Bass

```python
import jax, jax.numpy as jnp
from jax import lax
import numpy as np

D_MODEL = 1024
BATCH = 4
SEQ = 4096
DEPTH = 4
DEC_BATCH = 128
DEC_SEQ = 8
PAST_LEN = 8192
PAGE_SIZE = 128

HEAD_DIM = 64
N_HEADS = 8
N_KV_HEADS = 2
GQA_GROUP = N_HEADS // N_KV_HEADS
D_ATTN = N_HEADS * HEAD_DIM
D_KV = N_KV_HEADS * HEAD_DIM
WINDOW = 128
BLOCK = 128
D_CONV = D_MODEL // 2
CONV_WIDTH = 31
POOL_WINDOWS = (2, 4, 8, 16)
N_POOL_GROUPS = len(POOL_WINDOWS)
POOL_GROUP_DIM = D_MODEL // N_POOL_GROUPS
POOL_BUF = max(POOL_WINDOWS) - 1
D_FF = -(-(8 * D_MODEL) // (3 * 256)) * 256
N_EVEN = (DEPTH + 1) // 2
N_ODD = DEPTH // 2
D_IN_EVEN = 2 * D_CONV + D_ATTN + 2 * D_KV
RMS_EPS = 1e-6
LN_EPS = 1e-5
NEG_INF = -1e30

kernel_name = 'hybrid_conv_swa_pool_decoder_step'


def _rmsnorm(x, g):
    xf = x.astype(jnp.float32)
    y = xf * lax.rsqrt(jnp.mean(xf * xf, axis=-1, keepdims=True) + RMS_EPS)
    return (y * g.astype(jnp.float32)).astype(x.dtype)


def _layernorm(x, g, b):
    xf = x.astype(jnp.float32)
    mu = jnp.mean(xf, axis=-1, keepdims=True)
    var = jnp.mean(jnp.square(xf - mu), axis=-1, keepdims=True)
    y = (xf - mu) * lax.rsqrt(var + LN_EPS)
    return (y * g.astype(jnp.float32) + b.astype(jnp.float32)).astype(x.dtype)


def _alibi_slopes():
    return jnp.asarray(np.array([2.0 ** (-8.0 * (h + 1) / N_HEADS) for h in range(N_HEADS)], dtype=np.float32))


def _sink_attend(q, k, v, dist, valid, sinks):
    s = jnp.einsum('...qkgd,...skd->...kgqs', q, k).astype(jnp.float32) * (HEAD_DIM ** -0.5)
    slopes = _alibi_slopes().reshape(N_KV_HEADS, GQA_GROUP, 1, 1)
    s = s - slopes * dist.astype(jnp.float32)[..., None, None, :, :]
    s = jnp.where(valid[..., None, None, :, :], s, NEG_INF)
    sink = sinks.astype(jnp.float32).reshape(N_KV_HEADS, GQA_GROUP, 1, 1)
    m = jnp.maximum(jnp.max(s, axis=-1, keepdims=True), sink)
    p = jnp.exp(s - m)
    denom = jnp.sum(p, axis=-1, keepdims=True) + jnp.exp(sink - m)
    probs = (p / denom).astype(v.dtype)
    return jnp.einsum('...kgqs,...skd->...qkgd', probs, v)


def _swa_prompt(q, k, v, sinks):
    n, t = q.shape[0], q.shape[1]
    nb = t // BLOCK
    qb = q.reshape(n, nb, BLOCK, N_KV_HEADS, GQA_GROUP, HEAD_DIM)
    kb = k.reshape(n, nb, BLOCK, N_KV_HEADS, HEAD_DIM)
    vb = v.reshape(n, nb, BLOCK, N_KV_HEADS, HEAD_DIM)
    pad = ((0, 0), (1, 0), (0, 0), (0, 0), (0, 0))
    kk = jnp.concatenate([jnp.pad(kb, pad)[:, :-1], kb], axis=2)
    vv = jnp.concatenate([jnp.pad(vb, pad)[:, :-1], vb], axis=2)
    i = jnp.arange(BLOCK)[:, None]
    j = jnp.arange(2 * BLOCK)[None, :]
    dist = BLOCK + i - j
    blk = jnp.arange(nb)[:, None, None]
    valid = (dist >= 0) & (dist < WINDOW) & ((blk > 0) | (j >= BLOCK))
    out = _sink_attend(qb, kk, vv, dist, valid, sinks)
    keep = min(WINDOW, t)
    return out.reshape(n, t, D_ATTN), k[:, -keep:], v[:, -keep:]


def _swa_sample(q, k, v, k_buf, v_buf, sinks):
    n, t = q.shape[0], q.shape[1]
    buf_len = k_buf.shape[1]
    kk = jnp.concatenate([k_buf.astype(k.dtype), k], axis=1)
    vv = jnp.concatenate([v_buf.astype(v.dtype), v], axis=1)
    dist = buf_len + jnp.arange(t)[:, None] - jnp.arange(buf_len + t)[None, :]
    valid = (dist >= 0) & (dist < WINDOW)
    out = _sink_attend(q, kk, vv, dist, valid, sinks)
    return out.reshape(n, t, D_ATTN), kk[:, -buf_len:], vv[:, -buf_len:]


def _conformer_conv(u, buf, w_dw, b_dw, g, b):
    a, gate = jnp.split(u, 2, axis=-1)
    glu = a * jax.nn.sigmoid(gate)
    ext = jnp.concatenate([buf.astype(glu.dtype), glu], axis=1)
    y = lax.conv_general_dilated(ext, w_dw.astype(ext.dtype)[:, None, :], window_strides=(1,), padding='VALID',
                                 dimension_numbers=('NWC', 'WIO', 'NWC'), feature_group_count=D_CONV)
    y = y + b_dw.astype(y.dtype)
    y = jax.nn.silu(_layernorm(y, g, b))
    return y, ext[:, -(CONV_WIDTH - 1):]


def _even_layer(x, conv_buf, k_buf, v_buf, norm_g, w_in, q_norm, k_norm, sinks, w_dw, b_dw, cn_g, cn_b, w_out):
    n, t, _ = x.shape
    xn = _rmsnorm(x, norm_g)
    u = xn @ w_in
    o_q = 2 * D_CONV
    o_k = o_q + D_ATTN
    o_v = o_k + D_KV
    q = _rmsnorm(u[..., o_q:o_k].reshape(n, t, N_KV_HEADS, GQA_GROUP, HEAD_DIM), q_norm)
    k = _rmsnorm(u[..., o_k:o_v].reshape(n, t, N_KV_HEADS, HEAD_DIM), k_norm)
    v = u[..., o_v:].reshape(n, t, N_KV_HEADS, HEAD_DIM)
    conv_out, new_conv = _conformer_conv(u[..., :o_q], conv_buf, w_dw, b_dw, cn_g, cn_b)
    if k_buf is None:
        attn, new_k, new_v = _swa_prompt(q, k, v, sinks)
    else:
        attn, new_k, new_v = _swa_sample(q, k, v, k_buf, v_buf, sinks)
    y = jnp.concatenate([conv_out, attn.astype(conv_out.dtype)], axis=-1) @ w_out
    return x + y, new_conv, new_k, new_v


def _pool_mixer(xn, buf, pos0, w_pool, pool_scale):
    n, t, _ = xn.shape
    ext = jnp.concatenate([buf.astype(xn.dtype), xn], axis=1)
    cs = jnp.pad(jnp.cumsum(ext.astype(jnp.float32), axis=1), ((0, 0), (1, 0), (0, 0)))
    pos = pos0 + jnp.arange(t)
    xf = xn.astype(jnp.float32)
    groups = []
    for g, w in enumerate(POOL_WINDOWS):
        c0, c1 = g * POOL_GROUP_DIM, (g + 1) * POOL_GROUP_DIM
        hi = cs[:, POOL_BUF + 1:POOL_BUF + 1 + t, c0:c1]
        lo = cs[:, POOL_BUF + 1 - w:POOL_BUF + 1 - w + t, c0:c1]
        cnt = jnp.minimum(w, pos + 1).astype(jnp.float32)[:, None]
        groups.append((hi - lo) / cnt - xf[..., c0:c1])
    d = jnp.stack(groups, axis=2).astype(xn.dtype)
    y = jnp.einsum('btgc,gcd->btgd', d, w_pool).reshape(n, t, D_MODEL)
    return y * pool_scale.astype(y.dtype), ext[:, -POOL_BUF:]


def _odd_layer(x, pool_buf, pos0, norm_g, w_pool, pool_scale):
    y, new_buf = _pool_mixer(_rmsnorm(x, norm_g), pool_buf, pos0, w_pool, pool_scale)
    return x + y, new_buf


def _swiglu_ffn(x, g, w_gate, w_up, w_down):
    h = _rmsnorm(x, g)
    return x + (jax.nn.silu(h @ w_gate) * (h @ w_up)) @ w_down


def _trunk(x, conv_bufs, k_bufs, v_bufs, pool_bufs, pos0, norm_mix, norm_ffn, w_in, q_norm, k_norm, sinks,
           w_dw, b_dw, conv_norm_g, conv_norm_b, w_out, w_pool, pool_scale, w_gate, w_up, w_down):
    new_conv, new_k, new_v, new_pool = [], [], [], []
    for layer in range(DEPTH):
        i = layer // 2
        if layer % 2 == 0:
            kb = None if k_bufs is None else k_bufs[i]
            vb = None if v_bufs is None else v_bufs[i]
            x, c, kk, vv = _even_layer(x, conv_bufs[i], kb, vb, norm_mix[layer], w_in[i], q_norm[i], k_norm[i],
                                       sinks[i], w_dw[i], b_dw[i], conv_norm_g[i], conv_norm_b[i], w_out[i])
            new_conv.append(c)
            new_k.append(kk)
            new_v.append(vv)
        else:
            x, pb = _odd_layer(x, pool_bufs[i], pos0, norm_mix[layer], w_pool[i], pool_scale[i])
            new_pool.append(pb)
        x = _swiglu_ffn(x, norm_ffn[layer], w_gate[layer], w_up[layer], w_down[layer])
    return x, jnp.stack(new_conv), jnp.stack(new_k), jnp.stack(new_v), jnp.stack(new_pool)


def setup_inputs(seed: int = 0) -> dict:
    key = jax.random.key(seed)
    ks = jax.random.split(key, 24)
    f32 = jnp.float32

    def nrm(k, shape, scale):
        return jax.random.normal(k, shape, f32) * scale

    win_buf = min(WINDOW, PAST_LEN)
    return {
        'x_prompt': nrm(ks[0], (BATCH, SEQ, D_MODEL), 1.0),
        'x_sample': nrm(ks[1], (DEC_BATCH, DEC_SEQ, D_MODEL), 1.0),
        'cache_conv': nrm(ks[2], (N_EVEN, DEC_BATCH, CONV_WIDTH - 1, D_CONV), 0.5),
        'cache_k': nrm(ks[3], (N_EVEN, DEC_BATCH, win_buf, N_KV_HEADS, HEAD_DIM), 1.0),
        'cache_v': nrm(ks[4], (N_EVEN, DEC_BATCH, win_buf, N_KV_HEADS, HEAD_DIM), 1.0),
        'state_pool': nrm(ks[5], (N_ODD, DEC_BATCH, POOL_BUF, D_MODEL), 1.0),
        'norm_mix': 1.0 + nrm(ks[6], (DEPTH, D_MODEL), 0.02),
        'norm_ffn': 1.0 + nrm(ks[7], (DEPTH, D_MODEL), 0.02),
        'w_in': nrm(ks[8], (N_EVEN, D_MODEL, D_IN_EVEN), D_MODEL ** -0.5),
        'q_norm': 1.0 + nrm(ks[9], (N_EVEN, HEAD_DIM), 0.02),
        'k_norm': 1.0 + nrm(ks[10], (N_EVEN, HEAD_DIM), 0.02),
        'sinks': nrm(ks[11], (N_EVEN, N_HEADS), 0.5),
        'w_dw': nrm(ks[12], (N_EVEN, CONV_WIDTH, D_CONV), CONV_WIDTH ** -0.5),
        'b_dw': nrm(ks[13], (N_EVEN, D_CONV), 0.02),
        'conv_norm_g': 1.0 + nrm(ks[14], (N_EVEN, D_CONV), 0.02),
        'conv_norm_b': nrm(ks[15], (N_EVEN, D_CONV), 0.02),
        'w_out': nrm(ks[16], (N_EVEN, D_CONV + D_ATTN, D_MODEL), (D_CONV + D_ATTN) ** -0.5),
        'w_pool': nrm(ks[17], (N_ODD, N_POOL_GROUPS, POOL_GROUP_DIM, POOL_GROUP_DIM), POOL_GROUP_DIM ** -0.5),
        'pool_scale': 1.0 + nrm(ks[18], (N_ODD, D_MODEL), 0.02),
        'w_gate': nrm(ks[19], (DEPTH, D_MODEL, D_FF), D_MODEL ** -0.5),
        'w_up': nrm(ks[20], (DEPTH, D_MODEL, D_FF), D_MODEL ** -0.5),
        'w_down': nrm(ks[21], (DEPTH, D_FF, D_MODEL), D_FF ** -0.5),
    }


def reference(x_prompt, x_sample, cache_conv, cache_k, cache_v, state_pool, norm_mix, norm_ffn, w_in, q_norm,
              k_norm, sinks, w_dw, b_dw, conv_norm_g, conv_norm_b, w_out, w_pool, pool_scale, w_gate, w_up, w_down):
    n_p = x_prompt.shape[0]
    zero_conv = jnp.zeros((n_p, CONV_WIDTH - 1, D_CONV), x_prompt.dtype)
    zero_pool = jnp.zeros((n_p, POOL_BUF, D_MODEL), x_prompt.dtype)
    y_prompt, conv_p, k_p, v_p, pool_p = _trunk(
        x_prompt, [zero_conv] * N_EVEN, None, None, [zero_pool] * N_ODD, 0,
        norm_mix, norm_ffn, w_in, q_norm, k_norm, sinks, w_dw, b_dw, conv_norm_g, conv_norm_b, w_out,
        w_pool, pool_scale, w_gate, w_up, w_down)
    y_sample, conv_s, k_s, v_s, pool_s = _trunk(
        x_sample, cache_conv, cache_k, cache_v, state_pool, PAST_LEN,
        norm_mix, norm_ffn, w_in, q_norm, k_norm, sinks, w_dw, b_dw, conv_norm_g, conv_norm_b, w_out,
        w_pool, pool_scale, w_gate, w_up, w_down)
    return (y_prompt, y_sample, conv_p, k_p, v_p, pool_p, conv_s, k_s, v_s, pool_s)
```

```python
import numpy as np
from contextlib import ExitStack
import concourse.bass as bass
import concourse.mybir as mybir
from concourse.bass_utils import run_bass_kernel_spmd

F32 = mybir.dt.float32
BF16 = mybir.dt.bfloat16
AF = mybir.ActivationFunctionType
ALU = mybir.AluOpType

ENGS = ("pe", "act", "dve", "pool", "sp")

D = 1024
NCH = 8
DFF = 2816
NFC = 22
DIN = 1792
NBLK = 20
TOK = NBLK * 128
HALO = 384
OWN1 = 2432
NOUT = TOK - HALO
RMS_EPS = 1e-6
LN_EPS = 1e-5
POOL_W = (2, 4, 8, 16)
SBUF_BASE = 16512
SBUF_LIMIT = 229312

V_NMIX = 0
V_NFFN = 32
V_BDW = 64
V_CNG = 72
V_CNB = 80
V_PSC = 88
V_QN = 104
V_KN = 106
V_SINK = 108
V_CMASK = 116
NV = 120


class Op:
    __slots__ = ("eng", "fn", "deps", "signal", "tsem", "tick", "dma", "batch", "known", "waits", "idx")


class Sched:
    def __init__(self, nc, stack):
        self.nc = nc
        self.stack = stack
        self.all = []
        self.last_w = {}
        self.readers = {}
        self.esem = {e: stack.enter_context(nc.semaphore("s_" + e)) for e in ENGS}
        self.dsem = {}
        self.dcount = {}
        self.sems = dict(self.esem)
        self.phase_deps = []
        self.nalloc = 0

    PERSIST = ("x", "vecs", "cmat", "identf", "onesf", "small", "P")

    def _persistent(self, k):
        b = self._bname(k)
        return b in self.PERSIST or b.startswith("o_")

    def new_phase(self):
        retired = {}
        for k in [k for k in self.last_w if not self._persistent(k)]:
            w = self.last_w.pop(k)
            retired[w.idx] = w
        for k in [k for k in self.readers if not self._persistent(k)]:
            for r in self.readers.pop(k).values():
                retired[r.idx] = r
        for r in self.phase_deps:
            retired[r.idx] = r
        keep = {}
        for r in retired.values():
            key = r.eng if r.dma is None else ("dma", r.idx)
            if key not in keep or keep[key].idx < r.idx:
                keep[key] = r
        self.phase_deps = list(keep.values())

    def buf(self, name, shape, dtype, off):
        size = int(np.prod(shape[1:])) * (2 if dtype == BF16 else 4)
        assert off + size <= SBUF_LIMIT, (name, off, size)
        self.nalloc += 1
        return self.nc.alloc_sbuf_tensor_at("%s_%d" % (name, self.nalloc), list(shape), dtype, offset=off)

    @staticmethod
    def _bname(k):
        return k if isinstance(k, str) else k[0]

    def add(self, eng, fn, reads=(), writes=(), dma=None, batch=0):
        op = Op()
        op.eng, op.fn, op.signal, op.dma, op.batch = eng, fn, False, dma, batch
        op.idx = len(self.all)
        deps = {}
        for k in reads:
            w = self.last_w.get(k)
            if w is not None:
                deps[w.idx] = w
        for k in writes:
            w = self.last_w.get(k)
            if w is not None:
                deps[w.idx] = w
            rd = self.readers.get(k)
            if rd:
                for r in rd.values():
                    deps[r.idx] = r
        if self.phase_deps:
            for k in list(reads) + list(writes):
                if not self._persistent(k):
                    for r in self.phase_deps:
                        deps[r.idx] = r
                    break
        op.deps = list(deps.values())
        if fn is not None:
            for k in reads:
                rd = self.readers.setdefault(k, {})
                rd[eng if dma is None else ("dma", op.idx)] = op
            for k in writes:
                self.last_w[k] = op
                self.readers[k] = {}
        if dma is not None:
            if dma not in self.dsem:
                h = self.stack.enter_context(self.nc.semaphore("d_" + dma))
                self.dsem[dma] = h
                self.sems[dma] = h
                self.dcount[dma] = {}
            c = self.dcount[dma]
            if batch is None:
                batch = op.batch = len(c) + 1
            c[batch] = c.get(batch, 0) + 1
        for p in op.deps:
            if p.dma is None and not (p.eng == "pe" and eng == "pe" and dma is None):
                p.signal = True
        self.all.append(op)
        return op

    def finalize(self):
        count = {e: 0 for e in ENGS}
        seen = {e: {} for e in ENGS}
        bend = {}
        for name, c in self.dcount.items():
            tot = 0
            bend[name] = {}
            for b in sorted(c):
                tot += 16 * c[b]
                bend[name][b] = tot
        started = {}
        self.per_eng = {e: [] for e in ENGS}
        for op in self.all:
            E = op.eng
            sE = seen[E]
            waits = {}
            for p in sorted(op.deps, key=lambda p: -p.idx):
                if p.dma is None and p.eng == "pe" and E == "pe" and op.dma is None:
                    continue
                if p.dma is not None and p.dma == op.dma and p.batch == op.batch:
                    raise AssertionError("intra-batch DMA dependency on " + str(p.dma))
                if sE.get(p.tsem, 0) >= p.tick:
                    continue
                waits[p.tsem] = max(waits.get(p.tsem, 0), p.tick)
                for k, v in p.known.items():
                    if sE.get(k, 0) < v:
                        sE[k] = v
            if op.dma is not None:
                name = op.dma
                prev = started.get(name)
                if prev is not None and prev != op.batch:
                    assert op.batch > prev, (name, prev, op.batch)
                    v = bend[name][prev]
                    if sE.get(name, 0) < v:
                        waits[name] = max(waits.get(name, 0), v)
                        sE[name] = v
                started[name] = op.batch
                op.tsem, op.tick = name, bend[name][op.batch]
            elif op.signal:
                count[E] += 1
                op.tsem, op.tick = E, count[E]
            else:
                op.tsem, op.tick = E, count[E] + 1
            op.waits = list(waits.items())
            kn = dict(sE)
            if op.dma is not None or op.signal:
                kn[op.tsem] = max(kn.get(op.tsem, 0), op.tick)
            op.known = kn
            self.per_eng[E].append(op)

    def emit_engine(self, name, eng):
        for op in self.per_eng[name]:
            for (k, v) in op.waits:
                eng.wait_ge(self.sems[k], v)
            if op.fn is None:
                continue
            inst = op.fn(eng)
            if op.dma is not None:
                inst.then_inc(self.dsem[op.dma], 16)
            elif op.signal:
                inst.then_inc(self.esem[name], 1)

    def emit(self):
        self.finalize()
        with self.nc.Block() as block:
            @block.tensor
            def _(e):
                self.emit_engine("pe", e)

            @block.scalar
            def _(e):
                self.emit_engine("act", e)

            @block.vector
            def _(e):
                self.emit_engine("dve", e)

            @block.gpsimd
            def _(e):
                self.emit_engine("pool", e)

            @block.sync
            def _(e):
                self.emit_engine("sp", e)


class Arena:
    def __init__(self, base, limit=SBUF_LIMIT):
        self.p = base
        self.limit = limit

    def take(self, shape, dtype):
        size = int(np.prod(shape[1:])) * (2 if dtype == BF16 else 4)
        off = self.p
        self.p = (off + size + 63) // 64 * 64
        assert self.p <= self.limit, ("arena overflow", self.p)
        return off


def xkeys(c0, n, chs=range(NCH)):
    return [("x", ch, b) for ch in chs for b in range(c0 // 128, (c0 + n + 127) // 128)]


def MM(out, lhsT, rhs, start=True, stop=True):
    return lambda e: e.matmul(out, lhsT=lhsT, rhs=rhs, start=start, stop=stop)


def TR(out, in_, identity):
    return lambda e: e.transpose(out=out, in_=in_, identity=identity)


def ACT(out, in_, func, bias=None, scale=None):
    kw = {}
    if bias is not None:
        kw["bias"] = bias
    if scale is not None:
        kw["scale"] = scale
    return lambda e: e.activation(out=out, in_=in_, func=func, **kw)


def ACOPY(out, in_):
    return lambda e: e.copy(out=out, in_=in_)


def TT(out, in0, in1, op):
    return lambda e: e.tensor_tensor(out=out, in0=in0, in1=in1, op=op)


def STT(out, in0, scalar, in1, op0, op1):
    return lambda e: e.scalar_tensor_tensor(out=out, in0=in0, scalar=scalar, in1=in1, op0=op0, op1=op1)


def TS(out, in0, scalar1, op0, scalar2=None, op1=None):
    if op1 is None:
        return lambda e: e.tensor_scalar(out=out, in0=in0, scalar1=scalar1, scalar2=None, op0=op0)
    return lambda e: e.tensor_scalar(out=out, in0=in0, scalar1=scalar1, scalar2=scalar2, op0=op0, op1=op1)


def TCOPY(out, in_):
    return lambda e: e.tensor_copy(out=out, in_=in_)


def RECIP(out, in_):
    return lambda e: e.reciprocal(out=out, in_=in_)


def MSET(ap, v):
    return lambda e: e.memset(ap, v)


def TRED(out, in_, op):
    return lambda e: e.tensor_reduce(out=out, in_=in_, axis=mybir.AxisListType.X, op=op)


def DMA(out, in_):
    return lambda e: e.dma_start(out=out, in_=in_)


def build_program(n_layers=4):
    nc = bass.Bass("TRN2", target_bir_lowering=False)

    def din(name, shape):
        return nc.dram_tensor(name, list(shape), F32, kind="ExternalInput").ap()

    def dout(name, shape):
        return nc.dram_tensor(name, list(shape), F32, kind="ExternalOutput").ap()

    xT = din("xT", [D, TOK])
    vecs_d = din("vecs", [128, NV])
    cmat_d = din("cmat", [128, 5 * 128])
    emask_d = din("emask", [128, 2 * 2 * 512])
    esamp_d = din("esamp", [128, 2 * 32])
    esnew_d = din("esnew", [8, 2 * 32])
    invcnt_d = din("invcnt", [128, 4 * 128])
    w_in_d = din("w_in", [2, 128, NCH * DIN])
    w_out_d = din("w_out", [2, 128, NCH * D])
    w_dw_d = din("w_dwT", [2, 128, 4 * 31])
    w_pool_d = din("w_pool", [2, 128, 4 * 2 * 256])
    w_ffn_d = din("w_ffn", [4, NFC, 128, 3072])
    cconvT_d = din("cconvT", [2, 128, 4 * 16 * 30])
    ckT_d = din("ckT", [2, 128, 16 * 128])
    cv_d = din("cv_nat", [2, 16, 128, 128])
    ck_d = din("ck_nat", [2, 16, 128, 128])
    cconv_d = din("cconv_nat", [2, 16, 30, 512])
    cpool_d = din("cpool_nat", [2, 16, 15, 1024])
    cpoolT_d = din("cpoolT", [2, 128, 8 * 16 * 15])

    yT = dout("yT", [D, NOUT])
    o_gluT = dout("o_gluT", [2, 128, 4 * 30])
    o_kT = dout("o_kT", [2, 128, 128])
    o_vtok = dout("o_vtok", [2, 128, 128])
    o_xnT = dout("o_xnT", [2, 128, 8 * 15])
    o_convs_old = dout("o_convs_old", [2, 16, 22, 512])
    o_gluT_s = dout("o_gluT_s", [2, 128, 4 * 128])
    o_ks_old = dout("o_ks_old", [2, 16, 120, 128])
    o_kT_s = dout("o_kT_s", [2, 128, 128])
    o_vs_old = dout("o_vs_old", [2, 16, 120, 128])
    o_vtok_s = dout("o_vtok_s", [2, 128, 128])
    o_pools_old = dout("o_pools_old", [2, 16, 7, 1024])
    o_xnT_s = dout("o_xnT_s", [2, 128, 8 * 128])

    with ExitStack() as st:
        S = Sched(nc, st)
        A = S.add
        P = [nc.alloc_psum_tensor("P%d" % i, [128, 512], F32) for i in range(8)]

        def pk(i):
            return ("P", i)

        x = S.buf("x", [128, NCH, TOK], F32, SBUF_BASE)
        CB = SBUF_BASE + 81920
        vecs = S.buf("vecs", [128, NV], F32, CB)
        cmat = S.buf("cmat", [128, 5, 128], BF16, CB + 512)
        identf = S.buf("identf", [128, 128], F32, CB + 512 + 1280)
        onesf = S.buf("onesf", [128, 128], F32, CB + 512 + 1280 + 512)
        small = S.buf("small", [128, 16], F32, CB + 512 + 1280 + 1024)
        ABASE = CB + 512 + 1280 + 1024 + 64
        ONES_RMS, ONES_LN, BDIAG, ONES1, IDENT = range(5)
        out_ops = []

        def vcol(c):
            return vecs[:, c:c + 1]

        def b16(ap):
            return ap.rearrange("p (b t) -> p b t", b=16)

        def two(ap):
            return ap.rearrange("p (a c) -> p a c", a=2)

        for ch in range(NCH):
            A("sp", DMA(x[:, ch, :], xT[ch * 128:(ch + 1) * 128, :]), writes=xkeys(0, TOK, [ch]), dma="xin")
        A("sp", DMA(vecs[:], vecs_d[:, :]), writes=["vecs"], dma="cst")
        A("pool", DMA(cmat[:].rearrange("p a b -> p (a b)"), cmat_d[:, :]), writes=["cmat"], dma="cst2")
        A("sp", DMA(identf[:], cmat_d[:, 4 * 128:5 * 128]), writes=["identf"], dma="cst")
        A("sp", DMA(onesf[:], cmat_d[:, 3 * 128:4 * 128]), writes=["onesf"], dma="cst")

        for i in range(2):
            A("sp", DMA(o_convs_old[i], cconv_d[i, :, 8:30, :]), writes=[("o_shift", 0, i)], dma="shift")
            A("sp", DMA(o_ks_old[i], ck_d[i, :, 8:128, :]), writes=[("o_shift", 1, i)], dma="shift")
            A("sp", DMA(o_vs_old[i], cv_d[i, :, 8:128, :]), writes=[("o_shift", 2, i)], dma="shift")
            A("sp", DMA(o_pools_old[i], cpool_d[i, :, 8:15, :]), writes=[("o_shift", 3, i)], dma="shift")
            out_ops.extend([("o_shift", q, i) for q in range(4)])

        def rstd_from(ps_ap, dst_ap, eps, rkeys, wkeys):
            A("act", ACT(dst_ap, ps_ap, AF.Ln, bias=epsap[eps], scale=1.0), reads=list(rkeys) + ["small"], writes=wkeys)
            A("act", ACT(dst_ap, dst_ap, AF.Exp, scale=-0.5), reads=wkeys, writes=wkeys)

        A("dve", MSET(small[:, 8:9], RMS_EPS), writes=["small"])
        A("dve", MSET(small[:, 9:10], LN_EPS), writes=["small"])
        epsap = {RMS_EPS: small[:, 8:9], LN_EPS: small[:, 9:10]}

        def norm_group(c0, n, gbase, sq, rstd, dst_fn, dst_keys, view=None):
            A("act", ACT(sq[:, :, 0:n], x[:, :, c0:c0 + n], AF.Square), reads=xkeys(c0, n), writes=["sq"])
            for ch in range(NCH):
                A("pe", MM(P[4][:, 0:n], cmat[:, ONES_RMS, :], sq[:, ch, 0:n], ch == 0, ch == NCH - 1), reads=["sq", "cmat"], writes=[pk(4)])
            rstd_from(P[4][:, 0:n], rstd[:, 0:n], RMS_EPS, [pk(4)], ["rstd"])
            for ch in range(NCH):
                i0, i1 = x[:, ch, c0:c0 + n], rstd[:, 0:n]
                if view is not None:
                    i0, i1 = view(i0), view(i1)
                A("dve", STT(dst_fn(ch), i0, vcol(gbase + ch), i1, ALU.mult, ALU.mult),
                  reads=xkeys(c0, n, [ch]) + ["rstd", "vecs"], writes=dst_keys)

        def yps(dc, n):
            return P[dc // 2][:, (dc % 2) * 256:(dc % 2) * 256 + n]

        def add_y_to_x(c0, n):
            for b in range(4):
                xs = x[:, 2 * b:2 * b + 2, c0:c0 + n]
                A("dve", TT(xs, two(P[b][:, :])[:, :, 0:n], xs, ALU.add),
                  reads=[pk(b)] + xkeys(c0, n, [2 * b, 2 * b + 1]), writes=xkeys(c0, n, [2 * b, 2 * b + 1]))

        def split_groups(c_start, c_end):
            gs = []
            c = c_end
            while c > c_start:
                n = min(256, c - c_start)
                gs.append((c - n, n))
                c -= n
            return gs[::-1]

        def ffn(L):
            start_blk = (1, 1, 2, 3)[L]
            groups = split_groups(start_blk * 128, TOK)
            halves = [groups[:5], groups[5:]]
            S.new_phase()
            ar = Arena(ABASE)
            h = S.buf("h", [128, NCH, 1280], BF16, ar.take([128, NCH, 1280], BF16))
            wp = [S.buf("wp%d" % s, [128, 4, 3072], BF16, ar.take([128, 4, 3072], BF16)) for s in range(2)]
            sg = [S.buf("sg%d" % s, [128, 256], F32, ar.take([128, 256], F32)) for s in range(2)]
            abuf = [S.buf("a%d" % s, [128, 4, 256], BF16, ar.take([128, 4, 256], BF16)) for s in range(2)]
            sq = S.buf("sq", [128, NCH, 256], BF16, ar.take([128, NCH, 256], BF16))
            rstd = S.buf("rstd", [128, 256], F32, ar.take([128, 256], F32))
            pieces = [(f, min(4, NFC - f)) for f in range(0, NFC, 4)]
            gbase = V_NFFN + L * 8
            bcount = [0]

            def load_piece(pi, slot):
                f0, nf = pieces[pi]
                bcount[0] += 1
                A("pool", DMA(wp[slot][:, 0:nf, :], w_ffn_d[L, f0:f0 + nf].rearrange("f p n -> p f n")),
                  writes=["wp%d" % slot], dma="wffn%d_%d" % (L, slot), batch=bcount[0])

            def emit_Y(slot, nf, c0, n, aslot):
                for dc in range(NCH):
                    for fl in range(nf):
                        A("pe", MM(yps(dc, n), wp[slot][:, fl, 2048 + dc * 128:2048 + (dc + 1) * 128], abuf[aslot][:, fl, 0:n], fl == 0, fl == nf - 1),
                          reads=["wp%d" % slot, "a%d" % aslot], writes=[pk(dc // 2)])
                add_y_to_x(c0, n)

            cnt = 0
            for hi, half in enumerate(halves):
                if not half:
                    continue
                hc0 = half[0][0]
                load_piece(0, 0)
                for gi, (c0, n) in enumerate(half):
                    norm_group(c0, n, gbase, sq, rstd, lambda ch, o=c0 - hc0, n=n: h[:, ch, o:o + n], [("h", gi)])
                for pi, (f0, nf) in enumerate(pieces):
                    slot = pi % 2
                    if pi + 1 < len(pieces):
                        load_piece(pi + 1, 1 - slot)
                    wk = "wp%d" % slot
                    prev = None
                    for gi, (c0, n) in enumerate(half):
                        aslot = gi % 2
                        hoff = c0 - hc0
                        for fl in range(nf):
                            bank = 4 + fl
                            for kc in range(NCH):
                                A("pe", MM(P[bank][:, 0:n], wp[slot][:, fl, kc * 128:(kc + 1) * 128], h[:, kc, hoff:hoff + n], kc == 0, kc == NCH - 1),
                                  reads=[wk, ("h", gi)], writes=[pk(bank)])
                            for kc in range(NCH):
                                A("pe", MM(P[bank][:, 256:256 + n], wp[slot][:, fl, 1024 + kc * 128:1024 + (kc + 1) * 128], h[:, kc, hoff:hoff + n], kc == 0, kc == NCH - 1),
                                  reads=[wk, ("h", gi)], writes=[pk(bank)])
                            ss = cnt % 2
                            cnt += 1
                            A("act", ACT(sg[ss][:, 0:n], P[bank][:, 0:n], AF.Silu), reads=[pk(bank)], writes=["sg%d" % ss])
                            A("dve", TT(abuf[aslot][:, fl, 0:n], sg[ss][:, 0:n], P[bank][:, 256:256 + n], ALU.mult),
                              reads=["sg%d" % ss, pk(bank)], writes=["a%d" % aslot])
                        if prev is not None:
                            emit_Y(*prev)
                        prev = (slot, nf, c0, n, aslot)
                    emit_Y(*prev)

        def mixer_even(L):
            i = L // 2
            start_blk = (0, 0, 1, 0)[L]
            S.new_phase()
            ar = Arena(ABASE)

            def mk(name, shape, dt):
                return S.buf(name, shape, dt, ar.take(shape, dt))

            w_in = mk("w_in", [128, NCH, DIN], BF16)
            w_out = mk("w_out", [128, NCH, D], BF16)
            diag = mk("diag", [128, 4, 31, 128], BF16)
            wdw = mk("wdw", [128, 4, 32], F32)
            xn = mk("xn", [128, NCH, 256], BF16)
            sq = mk("sq", [128, NCH, 256], BF16)
            rstd = mk("rstd", [128, 512], F32)
            tmpA = mk("tmpA", [128, 4, 256], F32)
            vf = mk("vf", [128, 4, 256], F32)
            rhso = mk("rhso", [128, NCH, 256], BF16)
            qT = mk("qT", [128, 4, 256], BF16)
            o32 = mk("o32", [128, 4, 128], F32)
            k32 = mk("k32", [128, 128], F32)
            v32 = mk("v32", [128, 128], F32)
            esamp = mk("esamp", [128, 2, 32], F32)
            esnew = mk("esnew", [8, 2, 32], F32)
            rbase = ar.p
            emask = mk("emask", [128, 2, 2, 512], F32)
            kT = mk("kT", [128, 128 + OWN1], BF16)
            vall = mk("vall", [128, 20, 128], BF16)
            glu = mk("glu", [128, 4, 288], BF16)
            rend = ar.p
            ar.p = rbase
            ext = mk("ext", [128, 4, 16, 38], BF16)
            ckT = mk("ckT", [128, 16, 128], BF16)
            cv = mk("cv", [128, 16, 128], BF16)
            kTs = mk("kTs", [128, 128], BF16)
            vnew = mk("vnew", [8, 16, 128], BF16)
            assert ar.p <= rend
            meanb = two(rstd[:, :])
            alias_keys = ["emask", "glu"] + [("kT", b) for b in range(-1, 19)] + [("vall", b) for b in range(-1, 19)]
            pexp = tmpA[:, 0:2, :].rearrange("p a c -> p (a c)")
            denr = tmpA[:, 2:4, :].rearrange("p a c -> p (a c)")
            pTb = vf[:].rearrange("p a c -> p (a c)").bitcast(BF16)
            vbf = sq[:, 0:4, :]
            sq2 = sq[:, 4:8, :]
            tmpk = [("tmpA", ch) for ch in range(4)]

            wsem, csem = "wmix%d" % L, "cmix%d" % L
            A("pool", DMA(w_in[:].rearrange("p a b -> p (a b)"), w_in_d[i]), writes=["w_in"], dma=wsem)
            A("pool", DMA(w_out[:].rearrange("p a b -> p (a b)"), w_out_d[i]), writes=["w_out"], dma=wsem)
            A("sp", DMA(wdw[:, :, 0:31], w_dw_d[i].rearrange("p (a b) -> p a b", a=4)), writes=["wdw"], dma=csem)
            A("sp", DMA(emask[:].rearrange("p a b c -> p (a b c)"), emask_d[:, :]), writes=["emask"], dma=csem)
            A("sp", DMA(esamp[:].rearrange("p a b -> p (a b)"), esamp_d[:, :]), writes=["esamp"], dma=csem)
            A("sp", DMA(esnew[:].rearrange("p a b -> p (a b)"), esnew_d[:, :]), writes=["esnew"], dma=csem)
            for ch in range(4):
                for j in range(31):
                    A("dve", TS(diag[:, ch, j, :], cmat[:, IDENT, :], wdw[:, ch, j:j + 1], ALU.mult), reads=["cmat", "wdw"], writes=[("diag", ch, j)])
            A("dve", MSET(kT[:, :], 0.0), writes=[("kT", b) for b in range(-1, 19)])
            A("dve", MSET(vall[:].rearrange("p a b -> p (a b)"), 0.0), writes=[("vall", b) for b in range(-1, 19)])
            A("dve", MSET(glu[:].rearrange("p a b -> p (a b)"), 0.0), writes=["glu"])
            qn, kn = vcol(V_QN + i), vcol(V_KN + i)
            A("dve", TT(small[:, 0:1], qn, kn, ALU.mult), reads=["vecs"], writes=["small"])
            A("dve", TS(small[:, 1:2], small[:, 0:1], -1.0, ALU.mult), reads=["small"], writes=["small"])
            A("dve", TT(small[:, 0:1], small[:, 0:1], small[:, 1:2], ALU.max), reads=["small"], writes=["small"])
            A("pe", TR(P[7][0:1, 0:128], small[:, 0:1], identf[:]), reads=["small", "identf"], writes=[pk(7)])
            A("dve", TRED(small[0:1, 2:3], P[7][0:1, 0:128], ALU.max), reads=[pk(7)], writes=["small"])
            A("pe", MM(P[7][:, 128:129], onesf[0:1, :], small[0:1, 2:3]), reads=["small", "onesf"], writes=[pk(7)])
            A("dve", TS(small[:, 3:4], P[7][:, 128:129], -8.0, ALU.mult), reads=[pk(7)], writes=["small"])
            negM = small[:, 3:4]
            A("act", ACT(small[:, 4:8], vecs[:, V_SINK + 4 * i:V_SINK + 4 * i + 4], AF.Exp, bias=negM, scale=1.0), reads=["small", "vecs"], writes=["small"])
            sinkexp = small[:, 4:8]

            gbase = V_NMIX + L * 8
            pgroups = split_groups(start_blk * 128, OWN1)
            allgroups = [(c0, n, False) for (c0, n) in pgroups] + [(OWN1, 128, True)]
            spi = [0]

            def nbank():
                b = 5 + (spi[0] % 2)
                spi[0] += 1
                return b

            def proj(dst_ps, col0, n, keyw):
                for kc in range(NCH):
                    A("pe", MM(dst_ps, w_in[:, kc, col0:col0 + 128], xn[:, kc, 0:n], kc == 0, kc == NCH - 1), reads=["w_in", "xn"], writes=[keyw])

            for (c0, n, samp) in allgroups:
                last = (c0 + n == OWN1) and not samp
                if samp:
                    A("pool", None, writes=alias_keys)
                    A("pool", DMA(ext[:, :, :, 0:30], cconvT_d[i].rearrange("p (a b c) -> p a b c", a=4, b=16)), writes=["ext"], dma="csamp%d" % L)
                    A("pool", DMA(ckT[:].rearrange("p a b -> p (a b)"), ckT_d[i]), writes=["ckT"], dma="csamp%d" % L)
                    A("pool", DMA(cv[:], cv_d[i].rearrange("b k f -> k b f")), writes=["cv"], dma="csamp%d" % L)
                nb = n // 128
                norm_group(c0, n, gbase, sq, rstd, lambda ch, n=n: xn[:, ch, 0:n], ["xn"])
                for ch in range(4):
                    bank = nbank()
                    proj(P[bank][:, 0:n], ch * 128, n, pk(bank))
                    proj(P[bank][:, 256:256 + n], 512 + ch * 128, n, pk(bank))
                    A("act", ACT(tmpA[:, ch, 0:n], P[bank][:, 256:256 + n], AF.Sigmoid), reads=[pk(bank)], writes=[("tmpA", ch)])
                    if samp:
                        A("dve", TT(ext[:, ch, :, 30:38], b16(P[bank][:, 0:128]), b16(tmpA[:, ch, 0:128]), ALU.mult), reads=[pk(bank), ("tmpA", ch)], writes=["ext"])
                    else:
                        A("dve", TT(glu[:, ch, 32:32 + n], P[bank][:, 0:n], tmpA[:, ch, 0:n], ALU.mult), reads=[pk(bank), ("tmpA", ch)], writes=["glu"])
                    if samp or last:
                        n0 = n - 128
                        A("dve", TT(o32[:, ch, :], P[bank][:, n0:n0 + 128], tmpA[:, ch, n0:n0 + 128], ALU.mult), reads=[pk(bank), ("tmpA", ch)], writes=["o32"])
                if samp:
                    A("sp", DMA(o_gluT_s[i], o32[:].rearrange("p a b -> p (a b)")), reads=["o32"], writes=[("o_glus", i)], dma="outs", batch=None)
                    out_ops.append(("o_glus", i))
                elif last:
                    A("sp", DMA(o_gluT[i].rearrange("p (a b) -> p a b", a=4), o32[:, :, 98:128]), reads=["o32"], writes=[("o_glu", i)], dma="outs", batch=None)
                    out_ops.append(("o_glu", i))
                for pair in range(2):
                    bank = nbank()
                    for jj in range(2):
                        proj(P[bank][:, jj * 256:jj * 256 + n], 1024 + (pair * 2 + jj) * 128, n, pk(bank))
                    qps = two(P[bank][:, :])[:, :, 0:n]
                    A("act", ACT(sq[:, 0:2, 0:n], qps, AF.Square), reads=[pk(bank)], writes=["sq"])
                    for jj in range(2):
                        A("pe", MM(P[4][:, jj * 256:jj * 256 + n], cmat[:, BDIAG, :], sq[:, jj, 0:n]), reads=["sq", "cmat"], writes=[pk(4)])
                    r3 = two(rstd[:, :])[:, :, 0:n]
                    rstd_from(two(P[4][:, :])[:, :, 0:n], r3, RMS_EPS, [pk(4)], ["rstd"])
                    A("dve", STT(qT[:, 2 * pair:2 * pair + 2, 0:n], qps, qn, r3, ALU.mult, ALU.mult), reads=[pk(bank), "rstd", "vecs"], writes=["qT"])
                bank = nbank()
                proj(P[bank][:, 0:n], 1536, n, pk(bank))
                A("act", ACT(sq[:, 0, 0:n], P[bank][:, 0:n], AF.Square), reads=[pk(bank)], writes=["sq"])
                A("pe", MM(P[4][:, 0:n], cmat[:, BDIAG, :], sq[:, 0, 0:n]), reads=["sq", "cmat"], writes=[pk(4)])
                rstd_from(P[4][:, 0:n], rstd[:, 0:n], RMS_EPS, [pk(4)], ["rstd"])
                if samp:
                    kdst, kkeys = kTs[:, 0:128], ["kTs"]
                else:
                    kdst, kkeys = kT[:, 128 + c0:128 + c0 + n], [("kT", b) for b in range(c0 // 128, (c0 + n) // 128)]
                A("dve", STT(kdst, P[bank][:, 0:n], kn, rstd[:, 0:n], ALU.mult, ALU.mult), reads=[pk(bank), "rstd", "vecs"], writes=kkeys)
                if samp or last:
                    n0 = n - 128
                    A("dve", STT(k32[:, :], P[bank][:, n0:n0 + 128], kn, rstd[:, n0:n0 + 128], ALU.mult, ALU.mult), reads=[pk(bank), "rstd", "vecs"], writes=["k32"])
                    okk = ("o_k", samp, i)
                    A("sp", DMA((o_kT_s if samp else o_kT)[i], k32[:, :]), reads=["k32"], writes=[okk], dma="outs", batch=None)
                    out_ops.append(okk)
                for bl in range(nb):
                    blk = c0 // 128 + bl
                    for kc in range(NCH):
                        A("pe", MM(P[7][:, 0:128], xn[:, kc, bl * 128:(bl + 1) * 128], w_in[:, kc, 1664:1792], kc == 0, kc == NCH - 1), reads=["w_in", "xn"], writes=[pk(7)])
                    if not samp:
                        A("act", ACOPY(vall[:, blk + 1, :], P[7][:, 0:128]), reads=[pk(7)], writes=[("vall", blk)])
                    if samp or (last and bl == nb - 1):
                        A("act", ACOPY(v32[:, :], P[7][:, 0:128]), reads=[pk(7)], writes=["v32"])
                        ovk = ("o_v", samp, i)
                        A("sp", DMA((o_vtok_s if samp else o_vtok)[i], v32[:, :]), reads=["v32"], writes=[ovk], dma="outs", batch=None)
                        out_ops.append(ovk)
                if samp:
                    for r in range(4):
                        for bb in range(4):
                            b = r * 4 + bb
                            for kc in range(NCH):
                                A("pe", MM(P[7][0:8, bb * 128:(bb + 1) * 128], xn[:, kc, b * 8:(b + 1) * 8], w_in[:, kc, 1664:1792], kc == 0, kc == NCH - 1), reads=["w_in", "xn"], writes=[pk(7)])
                        A("act", ACOPY(vnew[0:8, r * 4:(r + 1) * 4, :], P[7][0:8, :].rearrange("p (a b) -> p a b", a=4)), reads=[pk(7)], writes=["vnew"])
                for ch in range(4):
                    cps = P[ch // 2][:, (ch % 2) * 256:(ch % 2) * 256 + n]
                    for j in range(31):
                        rhs = ext[:, ch, :, j:j + 8] if samp else glu[:, ch, 2 + j:2 + j + n]
                        A("pe", MM(cps, diag[:, ch, j, :], rhs, j == 0, j == 30), reads=[("diag", ch, j), "ext" if samp else "glu"], writes=[pk(ch // 2)])
                    bcol = vcol(V_BDW + 4 * i + ch)
                    A("act", ACT(vf[:, ch, 0:n], cps, AF.Identity, bias=bcol, scale=1.0), reads=[pk(ch // 2), "vecs"], writes=[("vf", ch)])
                    A("act", ACT(vbf[:, ch, 0:n], cps, AF.Identity, bias=bcol, scale=1.0), reads=[pk(ch // 2), "vecs"], writes=["sq"])
                    A("act", ACT(sq2[:, ch, 0:n], cps, AF.Square, bias=bcol, scale=1.0), reads=[pk(ch // 2), "vecs"], writes=["sq"])
                if not samp:
                    A("pool", TCOPY(glu[:, :, 0:32], glu[:, :, n:n + 32]), reads=["glu"], writes=["glu"])
                for ch in range(4):
                    A("pe", MM(P[4][:, 0:n], cmat[:, ONES_LN, :], vbf[:, ch, 0:n], ch == 0, ch == 3), reads=["sq", "cmat"], writes=[pk(4)])
                for ch in range(4):
                    A("pe", MM(P[4][:, 256:256 + n], cmat[:, ONES_LN, :], sq2[:, ch, 0:n], ch == 0, ch == 3), reads=["sq", "cmat"], writes=[pk(4)])
                A("act", ACOPY(meanb[:, 0, 0:n], P[4][:, 0:n]), reads=[pk(4)], writes=["rstd"])
                A("dve", STT(meanb[:, 1, 0:n], meanb[:, 0, 0:n], -1.0, meanb[:, 0, 0:n], ALU.mult, ALU.mult), reads=["rstd"], writes=["rstd"])
                A("dve", TT(meanb[:, 1, 0:n], P[4][:, 256:256 + n], meanb[:, 1, 0:n], ALU.add), reads=[pk(4), "rstd"], writes=["rstd"])
                rstd_from(meanb[:, 1, 0:n], meanb[:, 1, 0:n], LN_EPS, ["rstd"], ["rstd"])
                for ch in range(4):
                    A("dve", TT(tmpA[:, ch, 0:n], vf[:, ch, 0:n], meanb[:, 0, 0:n], ALU.subtract), reads=[("vf", ch), "rstd"], writes=[("tmpA", ch)])
                    A("dve", TT(tmpA[:, ch, 0:n], tmpA[:, ch, 0:n], meanb[:, 1, 0:n], ALU.mult), reads=[("tmpA", ch), "rstd"], writes=[("tmpA", ch)])
                    A("act", ACT(rhso[:, ch, 0:n], tmpA[:, ch, 0:n], AF.Silu, bias=vcol(V_CNB + 4 * i + ch), scale=vcol(V_CNG + 4 * i + ch)),
                      reads=[("tmpA", ch), "vecs"], writes=[("rhso", ch)])
                if not samp:
                    for bl in range(nb):
                        blk = c0 // 128 + bl
                        qo = bl * 128
                        slot = 0
                        for kv in range(2):
                            ps_ = slice(kv * 64, (kv + 1) * 64)
                            for kb in range(2):
                                kblk = blk - 1 + kb
                                sb = nbank()
                                A("pe", MM(P[sb][:, :], kT[ps_, 128 + kblk * 128:128 + (kblk + 1) * 128], qT[ps_, :, qo:qo + 128]), reads=[("kT", kblk), "qT"], writes=[pk(sb)])
                                A("act", ACT(pexp, P[sb][:, :], AF.Exp, bias=negM, scale=0.125), reads=[pk(sb), "small"], writes=tmpk[0:2])
                                kind = kb
                                pslot = pTb[:, slot * 512:(slot + 1) * 512]
                                pkey = ("vf", slot)
                                slot += 1
                                A("dve", TT(pslot, pexp, emask[:, kind, kv, :], ALU.mult), reads=tmpk[0:2] + ["emask"], writes=[pkey])
                                if blk == 3 and kb == 0:
                                    A("dve", TS(pslot, pslot, vcol(V_CMASK), ALU.mult), reads=[pkey, "vecs"], writes=[pkey])
                                A("pe", MM(P[2][ps_, :], vall[:, kblk + 1, ps_], pslot, kb == 0, kb == 1), reads=[("vall", kblk), pkey], writes=[pk(2)])
                                A("pe", MM(P[3][ps_, :], cmat[:, ONES1, 0:64], pslot, kb == 0, kb == 1), reads=["cmat", pkey], writes=[pk(3)])
                        for j in range(4):
                            A("dve", TS(denr[:, j * 128:(j + 1) * 128], P[3][:, j * 128:(j + 1) * 128], sinkexp[:, j:j + 1], ALU.add), reads=[pk(3), "small"], writes=tmpk[2:4])
                        A("dve", RECIP(denr, denr), reads=tmpk[2:4], writes=tmpk[2:4])
                        A("dve", TT(rhso[:, 4:8, qo:qo + 128], P[2][:, :].rearrange("p (a c) -> p a c", a=4), denr.rearrange("p (a c) -> p a c", a=4), ALU.mult),
                          reads=[pk(2)] + tmpk[2:4], writes=[("rhso", 4 + j) for j in range(4)])
                else:
                    slot = 0
                    for kv in range(2):
                        ps_ = slice(kv * 64, (kv + 1) * 64)
                        for b in range(16):
                            A("pe", MM(P[5][:, b * 32:(b + 1) * 32], ckT[ps_, b, :], qT[ps_, :, b * 8:(b + 1) * 8]), reads=["ckT", "qT"], writes=[pk(5)])
                        for b in range(16):
                            A("pe", MM(P[6][0:8, b * 32:(b + 1) * 32], kTs[ps_, b * 8:(b + 1) * 8], qT[ps_, :, b * 8:(b + 1) * 8]), reads=["kTs", "qT"], writes=[pk(6)])
                        A("act", ACT(pexp, P[5][:, :], AF.Exp, bias=negM, scale=0.125), reads=[pk(5), "small"], writes=tmpk[0:2])
                        A("act", ACT(denr[0:8, :], P[6][0:8, :], AF.Exp, bias=small[0:8, 3:4], scale=0.125), reads=[pk(6), "small"], writes=tmpk[2:4])
                        p1 = pTb[:, slot * 512:(slot + 1) * 512]
                        p2 = pTb[0:8, (slot + 1) * 512:(slot + 2) * 512]
                        k1, k2 = ("vf", slot), ("vf", slot + 1)
                        slot += 2
                        A("dve", TT(b16(p1), b16(pexp), esamp[:, kv:kv + 1, :].to_broadcast([128, 16, 32]), ALU.mult), reads=tmpk[0:2] + ["esamp"], writes=[k1])
                        A("dve", TT(b16(p2), b16(denr[0:8, :]), esnew[0:8, kv:kv + 1, :].to_broadcast([8, 16, 32]), ALU.mult), reads=tmpk[2:4] + ["esnew"], writes=[k2])
                        for b in range(16):
                            cs = slice(b * 32, (b + 1) * 32)
                            A("pe", MM(P[2][ps_, cs], cv[:, b, ps_], p1[:, cs], True, False), reads=["cv", k1], writes=[pk(2)])
                            A("pe", MM(P[2][ps_, cs], vnew[0:8, b, ps_], p2[:, cs], False, True), reads=["vnew", k2], writes=[pk(2)])
                            A("pe", MM(P[3][ps_, cs], cmat[:, ONES1, 0:64], p1[:, cs], True, False), reads=["cmat", k1], writes=[pk(3)])
                            A("pe", MM(P[3][ps_, cs], cmat[0:8, ONES1, 0:64], p2[:, cs], False, True), reads=["cmat", k2], writes=[pk(3)])
                    den4 = denr.rearrange("p (b j t) -> p j b t", b=16, j=4)
                    d34 = P[3][:, :].rearrange("p (b j t) -> p j b t", b=16, j=4)
                    o24 = P[2][:, :].rearrange("p (b j t) -> p j b t", b=16, j=4)
                    for j in range(4):
                        A("dve", TS(den4[:, j], d34[:, j], sinkexp[:, j:j + 1], ALU.add), reads=[pk(3), "small"], writes=tmpk[2:4])
                    A("dve", RECIP(denr, denr), reads=tmpk[2:4], writes=tmpk[2:4])
                    for j in range(4):
                        A("dve", TT(b16(rhso[:, 4 + j, 0:128]), o24[:, j], den4[:, j], ALU.mult), reads=[pk(2)] + tmpk[2:4], writes=[("rhso", 4 + j)])
                for dc in range(NCH):
                    for kc in range(NCH):
                        A("pe", MM(yps(dc, n), w_out[:, kc, dc * 128:(dc + 1) * 128], rhso[:, kc, 0:n], kc == 0, kc == NCH - 1), reads=["w_out", ("rhso", kc)], writes=[pk(dc // 2)])
                add_y_to_x(c0, n)

        def mixer_odd(L):
            i = L // 2
            start_blk = (0, 1, 0, 2)[L]
            S.new_phase()
            ar = Arena(ABASE)

            def mk(name, shape, dt):
                return S.buf(name, shape, dt, ar.take(shape, dt))

            wpl = mk("wpl", [128, 4, 2, 256], BF16)
            invc = mk("invc", [128, 4, 128], F32)
            sq = mk("sq", [128, NCH, 272], BF16)
            rstd = mk("rstd", [128, 272], F32)
            xw = mk("xw", [128, NCH, 272], F32)
            sA = mk("sA", [128, NCH, 272], F32)
            sB = mk("sB", [128, NCH, 272], F32)
            dd = mk("dd", [128, NCH, 256], BF16)
            tmp = mk("tmp", [128, 128], F32)
            ep = mk("ep", [128, NCH, 16, 23], F32)
            eA = mk("eA", [128, NCH, 16, 23], F32)
            eB = mk("eB", [128, NCH, 16, 23], F32)
            A("pool", DMA(wpl[:].rearrange("p a b c -> p (a b c)"), w_pool_d[i]), writes=["wpl"], dma="wmix%d" % L)
            A("sp", DMA(invc[:].rearrange("p a b -> p (a b)"), invcnt_d[:, :]), writes=["invc"], dma="cmix%d" % L)
            A("sp", DMA(ep[:, :, :, 0:15], cpoolT_d[i].rearrange("p (a b c) -> p a b c", a=8, b=16)), writes=["ep"], dma="cmix%d" % L)
            gbase = V_NMIX + L * 8

            def pool_out(c0, n):
                for g in range(4):
                    for oc in range(2):
                        dc = 2 * g + oc
                        for kc in range(2):
                            A("pe", MM(yps(dc, n), wpl[:, g, kc, oc * 128:(oc + 1) * 128], dd[:, 2 * g + kc, 0:n], kc == 0, kc == 1), reads=["wpl", "dd"], writes=[pk(dc // 2)])
                for dc in range(NCH):
                    xs = x[:, dc, c0:c0 + n]
                    A("dve", STT(xs, yps(dc, n), vcol(V_PSC + 8 * i + dc), xs, ALU.mult, ALU.add),
                      reads=[pk(dc // 2), "vecs"] + xkeys(c0, n, [dc]), writes=xkeys(c0, n, [dc]))

            prev_n = None
            for (c0, n) in split_groups(start_blk * 128, OWN1):
                last = (c0 + n == OWN1)
                m = n + 16
                if prev_n is None:
                    norm_group(c0 - 16, m, gbase, sq, rstd, lambda ch, m=m: xw[:, ch, 0:m], ["xw"])
                else:
                    A("dve", TCOPY(xw[:, :, 0:16], xw[:, :, prev_n:prev_n + 16]), reads=["xw"], writes=["xw"])
                    norm_group(c0, n, gbase, sq, rstd, lambda ch, n=n: xw[:, ch, 16:16 + n], ["xw"])
                prev_n = n
                A("dve", TT(sA[:, :, 1:m], xw[:, :, 1:m], xw[:, :, 0:m - 1], ALU.add), reads=["xw"], writes=["sA"])
                A("dve", TT(sB[:, 2:8, 3:m], sA[:, 2:8, 3:m], sA[:, 2:8, 1:m - 2], ALU.add), reads=["sA"], writes=["sB"])
                A("dve", TT(sA[:, 4:8, 7:m], sB[:, 4:8, 7:m], sB[:, 4:8, 3:m - 4], ALU.add), reads=["sB"], writes=["sA"])
                A("dve", TT(sB[:, 6:8, 15:m], sA[:, 6:8, 15:m], sA[:, 6:8, 7:m - 8], ALU.add), reads=["sA"], writes=["sB"])
                for g in range(4):
                    src = sA if g in (0, 2) else sB
                    A("dve", STT(dd[:, 2 * g:2 * g + 2, 0:n], src[:, 2 * g:2 * g + 2, 16:16 + n], 1.0 / POOL_W[g], xw[:, 2 * g:2 * g + 2, 16:16 + n], ALU.mult, ALU.subtract),
                      reads=["sA", "sB", "xw"], writes=["dd"])
                    if c0 <= HALO < c0 + n:
                        o = HALO - c0
                        for cc in range(2):
                            ch = 2 * g + cc
                            A("dve", TT(tmp[:, :], src[:, ch, 16 + o:16 + o + 128], invc[:, g, :], ALU.mult), reads=["sA", "sB", "invc"], writes=["tmp"])
                            A("dve", TT(dd[:, ch, o:o + 128], tmp[:, :], xw[:, ch, 16 + o:16 + o + 128], ALU.subtract), reads=["tmp", "xw"], writes=["dd"])
                if last:
                    A("sp", DMA(o_xnT[i].rearrange("p (a b) -> p a b", a=8), xw[:, :, m - 15:m]), reads=["xw"], writes=[("o_xn", i)], dma="outs", batch=None)
                    out_ops.append(("o_xn", i))
                pool_out(c0, n)
            c0, n = OWN1, 128
            norm_group(c0, n, gbase, sq, rstd, lambda ch: ep[:, ch, :, 15:23], ["ep"], view=b16)
            A("dve", TT(eA[:, :, :, 1:23], ep[:, :, :, 1:23], ep[:, :, :, 0:22], ALU.add), reads=["ep"], writes=["eA"])
            A("dve", TT(eB[:, 2:8, :, 3:23], eA[:, 2:8, :, 3:23], eA[:, 2:8, :, 1:21], ALU.add), reads=["eA"], writes=["eB"])
            A("dve", TT(eA[:, 4:8, :, 7:23], eB[:, 4:8, :, 7:23], eB[:, 4:8, :, 3:19], ALU.add), reads=["eB"], writes=["eA"])
            A("dve", TT(eB[:, 6:8, :, 15:23], eA[:, 6:8, :, 15:23], eA[:, 6:8, :, 7:15], ALU.add), reads=["eA"], writes=["eB"])
            for g in range(4):
                src = eA if g in (0, 2) else eB
                for cc in range(2):
                    ch = 2 * g + cc
                    A("dve", STT(b16(dd[:, ch, 0:128]), src[:, ch, :, 15:23], 1.0 / POOL_W[g], ep[:, ch, :, 15:23], ALU.mult, ALU.subtract), reads=["eA", "eB", "ep"], writes=["dd"])
            A("sp", DMA(o_xnT_s[i].rearrange("p (a b t) -> p a b t", a=8, b=16), ep[:, :, :, 15:23]), reads=["ep"], writes=[("o_xns", i)], dma="outs", batch=None)
            out_ops.append(("o_xns", i))
            pool_out(c0, n)

        for L in range(n_layers):
            if L >= 1:
                xs = x[:, :, 0:HALO]
                A("dve", TS(xs, xs, vcol(V_CMASK), ALU.mult), reads=xkeys(0, HALO) + ["vecs"], writes=xkeys(0, HALO))
            if L % 2 == 0:
                mixer_even(L)
            else:
                mixer_odd(L)
            ffn(L)

        for ch in range(NCH):
            A("sp", DMA(yT[ch * 128:(ch + 1) * 128, :], x[:, ch, HALO:TOK]), reads=xkeys(HALO, NOUT, [ch]), writes=[("o_y", ch)], dma="outy")
            out_ops.append(("o_y", ch))
        A("sp", None, reads=list(dict.fromkeys(out_ops)))
        S.emit()
    return nc


_PROG = {}


def _alibi_slopes():
    return np.array([2.0 ** (-8.0 * (h + 1) / 8) for h in range(8)], dtype=np.float64)


def _const_inputs():
    cm = np.zeros((128, 5, 128), np.float32)
    cm[:, 0] = 1.0 / 1024
    cm[:, 1] = 1.0 / 512
    cm[0:64, 2, 0:64] = 1.0 / 64
    cm[64:128, 2, 64:128] = 1.0 / 64
    cm[:, 3] = 1.0
    cm[:, 4] = np.eye(128, dtype=np.float32)
    sl = _alibi_slopes()
    s = np.arange(128)[:, None]
    q = np.arange(128)[None, :]
    em = np.zeros((128, 3, 2, 4, 128), np.float64)
    for kv in range(2):
        for j in range(4):
            h = kv * 4 + j
            dprev = 128 + q - s
            em[:, 0, kv, j] = np.where(dprev < 128, np.exp(-sl[h] * dprev), 0.0)
            dcur = q - s
            em[:, 1, kv, j] = np.where(dcur >= 0, np.exp(-sl[h] * dcur), 0.0)
    em[:, 2] = em[:, 0]
    es = np.zeros((128, 2, 4, 8), np.float64)
    en = np.zeros((8, 2, 4, 8), np.float64)
    ii = np.arange(8)[None, :]
    for kv in range(2):
        for j in range(4):
            h = kv * 4 + j
            d1 = 128 + ii - np.arange(128)[:, None]
            es[:, kv, j] = np.where(d1 < 128, np.exp(-sl[h] * d1), 0.0)
            d2 = ii - np.arange(8)[:, None]
            en[:, kv, j] = np.where(d2 >= 0, np.exp(-sl[h] * d2), 0.0)
    return cm.reshape(128, 640), em.astype(np.float32), es.astype(np.float32).reshape(128, 64), en.astype(np.float32).reshape(8, 64)


def _prep_shared(norm_mix, norm_ffn, w_in, q_norm, k_norm, sinks, w_dw, b_dw, conv_norm_g, conv_norm_b, w_out,
                 w_pool, pool_scale, w_gate, w_up, w_down):
    f = np.float32
    vecs = np.zeros((128, NV), f)
    vecs[:, V_NMIX:V_NMIX + 32] = np.asarray(norm_mix, f).reshape(4, 8, 128).transpose(2, 0, 1).reshape(128, 32)
    vecs[:, V_NFFN:V_NFFN + 32] = np.asarray(norm_ffn, f).reshape(4, 8, 128).transpose(2, 0, 1).reshape(128, 32)
    vecs[:, V_BDW:V_BDW + 8] = np.asarray(b_dw, f).reshape(2, 4, 128).transpose(2, 0, 1).reshape(128, 8)
    vecs[:, V_CNG:V_CNG + 8] = np.asarray(conv_norm_g, f).reshape(2, 4, 128).transpose(2, 0, 1).reshape(128, 8)
    vecs[:, V_CNB:V_CNB + 8] = np.asarray(conv_norm_b, f).reshape(2, 4, 128).transpose(2, 0, 1).reshape(128, 8)
    vecs[:, V_PSC:V_PSC + 16] = np.asarray(pool_scale, f).reshape(2, 8, 128).transpose(2, 0, 1).reshape(128, 16)
    pidx = np.arange(128)
    vecs[:, V_QN:V_QN + 2] = np.asarray(q_norm, f)[:, pidx % 64].T
    vecs[:, V_KN:V_KN + 2] = np.asarray(k_norm, f)[:, pidx % 64].T
    sk = np.asarray(sinks, f)
    for i in range(2):
        for j in range(4):
            vecs[:, V_SINK + 4 * i + j] = sk[i, (pidx // 64) * 4 + j]
    w_in = np.asarray(w_in, f)
    cols = list(range(1024))
    for j in range(4):
        for kv in range(2):
            h = kv * 4 + j
            cols.extend(range(1024 + h * 64, 1024 + (h + 1) * 64))
    cols.extend(range(1536, 1792))
    w_in_p = w_in[:, :, cols]
    w_in_l = np.ascontiguousarray(w_in_p.reshape(2, 8, 128, DIN).transpose(0, 2, 1, 3)).reshape(2, 128, 8 * DIN)
    w_out = np.asarray(w_out, f)
    rows = list(range(512))
    for j in range(4):
        for kv in range(2):
            h = kv * 4 + j
            rows.extend(range(512 + h * 64, 512 + (h + 1) * 64))
    w_out_l = np.ascontiguousarray(w_out[:, rows, :].reshape(2, 8, 128, D).transpose(0, 2, 1, 3)).reshape(2, 128, 8 * D)
    w_dwT = np.ascontiguousarray(np.asarray(w_dw, f).reshape(2, 31, 4, 128).transpose(0, 3, 2, 1)).reshape(2, 128, 4 * 31)
    w_pool_l = np.ascontiguousarray(np.asarray(w_pool, f).reshape(2, 4, 2, 128, 256).transpose(0, 3, 1, 2, 4)).reshape(2, 128, 2048)
    wg = np.asarray(w_gate, f).reshape(4, 8, 128, NFC, 128).transpose(0, 3, 2, 1, 4).reshape(4, NFC, 128, 1024)
    wu = np.asarray(w_up, f).reshape(4, 8, 128, NFC, 128).transpose(0, 3, 2, 1, 4).reshape(4, NFC, 128, 1024)
    wd = np.asarray(w_down, f).reshape(4, NFC, 128, 1024)
    w_ffn = np.ascontiguousarray(np.concatenate([wg, wu, wd], axis=3))
    return dict(vecs=vecs, w_in=w_in_l, w_out=w_out_l, w_dwT=w_dwT, w_pool=w_pool_l, w_ffn=w_ffn)


def kernel(x_prompt, x_sample, cache_conv, cache_k, cache_v, state_pool, norm_mix, norm_ffn, w_in, q_norm, k_norm,
           sinks, w_dw, b_dw, conv_norm_g, conv_norm_b, w_out, w_pool, pool_scale, w_gate, w_up, w_down, _n_layers=4):
    f = np.float32
    x_prompt = np.asarray(x_prompt, f)
    x_sample = np.asarray(x_sample, f)
    cache_conv = np.asarray(cache_conv, f)
    cache_k = np.asarray(cache_k, f).reshape(2, 128, 128, 128)
    cache_v = np.asarray(cache_v, f).reshape(2, 128, 128, 128)
    state_pool = np.asarray(state_pool, f)
    shared = _prep_shared(norm_mix, norm_ffn, w_in, q_norm, k_norm, sinks, w_dw, b_dw, conv_norm_g, conv_norm_b,
                          w_out, w_pool, pool_scale, w_gate, w_up, w_down)
    cmat, emask, esamp, esnew = _const_inputs()
    if _n_layers not in _PROG:
        _PROG[_n_layers] = build_program(_n_layers)
    nc = _PROG[_n_layers]
    in_maps = []
    for c in range(8):
        s, half = c // 2, c % 2
        xt = np.zeros((TOK, D), f)
        t0 = half * 2048
        if half == 1:
            xt[0:HALO] = x_prompt[s, t0 - HALO:t0]
        xt[HALO:OWN1] = x_prompt[s, t0:t0 + 2048]
        xt[OWN1:TOK] = x_sample[16 * c:16 * c + 16].reshape(128, D)
        bs = slice(16 * c, 16 * c + 16)
        vecs = shared["vecs"].copy()
        vecs[:, V_CMASK] = float(half)
        em = emask[:, 0:2]
        invc = np.zeros((128, 4, 128), f)
        for g, w in enumerate(POOL_W):
            if half == 0:
                invc[:, g, :] = 1.0 / np.minimum(w, np.arange(128) + 1)
            else:
                invc[:, g, :] = 1.0 / w
        m = dict(
            xT=np.ascontiguousarray(xt.T), vecs=vecs, cmat=cmat, emask=np.ascontiguousarray(em).reshape(128, 2048), esamp=esamp, esnew=esnew,
            invcnt=invc.reshape(128, 512), w_in=shared["w_in"], w_out=shared["w_out"], w_dwT=shared["w_dwT"],
            w_pool=shared["w_pool"], w_ffn=shared["w_ffn"],
            cconvT=np.ascontiguousarray(cache_conv[:, bs].reshape(2, 16, 30, 4, 128).transpose(0, 4, 3, 1, 2)).reshape(2, 128, 1920),
            ckT=np.ascontiguousarray(cache_k[:, bs].transpose(0, 3, 1, 2)).reshape(2, 128, 2048),
            cv_nat=np.ascontiguousarray(cache_v[:, bs]), ck_nat=np.ascontiguousarray(cache_k[:, bs]),
            cconv_nat=np.ascontiguousarray(cache_conv[:, bs]), cpool_nat=np.ascontiguousarray(state_pool[:, bs]),
            cpoolT=np.ascontiguousarray(state_pool[:, bs].reshape(2, 16, 15, 8, 128).transpose(0, 4, 3, 1, 2)).reshape(2, 128, 1920),
        )
        in_maps.append(m)
    res = run_bass_kernel_spmd(nc, in_maps, core_ids=list(range(8)))
    return _assemble(res.results)


def _assemble(R):
    f = np.float32
    y_prompt = np.zeros((4, 4096, D), f)
    y_sample = np.zeros((128, 8, D), f)
    conv_p = np.zeros((2, 4, 30, 512), f)
    k_p = np.zeros((2, 4, 128, 2, 64), f)
    v_p = np.zeros((2, 4, 128, 2, 64), f)
    pool_p = np.zeros((2, 4, 15, D), f)
    conv_s = np.zeros((2, 128, 30, 512), f)
    k_s = np.zeros((2, 128, 128, 2, 64), f)
    v_s = np.zeros((2, 128, 128, 2, 64), f)
    pool_s = np.zeros((2, 128, 15, D), f)
    for c in range(8):
        r = R[c]
        s, half = c // 2, c % 2
        yt = r["yT"].T
        y_prompt[s, half * 2048:(half + 1) * 2048] = yt[0:2048]
        y_sample[16 * c:16 * c + 16] = yt[2048:].reshape(16, 8, D)
        bs = slice(16 * c, 16 * c + 16)
        if half == 1:
            conv_p[:, s] = r["o_gluT"].reshape(2, 128, 4, 30).transpose(0, 3, 2, 1).reshape(2, 30, 512)
            k_p[:, s] = r["o_kT"].transpose(0, 2, 1).reshape(2, 128, 2, 64)
            v_p[:, s] = r["o_vtok"].reshape(2, 128, 2, 64)
            pool_p[:, s] = r["o_xnT"].reshape(2, 128, 8, 15).transpose(0, 3, 2, 1).reshape(2, 15, D)
        conv_s[:, bs, 0:22] = r["o_convs_old"]
        conv_s[:, bs, 22:30] = r["o_gluT_s"].reshape(2, 128, 4, 16, 8).transpose(0, 3, 4, 2, 1).reshape(2, 16, 8, 512)
        k_s[:, bs, 0:120] = r["o_ks_old"].reshape(2, 16, 120, 2, 64)
        k_s[:, bs, 120:128] = r["o_kT_s"].transpose(0, 2, 1).reshape(2, 16, 8, 2, 64)
        v_s[:, bs, 0:120] = r["o_vs_old"].reshape(2, 16, 120, 2, 64)
        v_s[:, bs, 120:128] = r["o_vtok_s"].reshape(2, 16, 8, 2, 64)
        pool_s[:, bs, 0:7] = r["o_pools_old"]
        pool_s[:, bs, 7:15] = r["o_xnT_s"].reshape(2, 128, 8, 16, 8).transpose(0, 3, 4, 2, 1).reshape(2, 16, 8, D)
    return (y_prompt, y_sample, conv_p, k_p, v_p, pool_p, conv_s, k_s, v_s, pool_s)
```

```python
import numpy as np
from contextlib import ExitStack
import concourse.bass as bass
import concourse.mybir as mybir
from concourse.bass_utils import run_bass_kernel_spmd

F32 = mybir.dt.float32
BF16 = mybir.dt.bfloat16
AF = mybir.ActivationFunctionType
ALU = mybir.AluOpType

ENGS = ("pe", "act", "dve", "pool", "sp")

D = 1024
NCH = 8
DFF = 2816
NFC = 22
DIN = 1792
NBLK = 20
TOK = NBLK * 128
HALO = 384
OWN1 = 2432
NOUT = TOK - HALO
RMS_EPS = 1e-6
LN_EPS = 1e-5
POOL_W = (2, 4, 8, 16)
SBUF_BASE = 16512
SBUF_LIMIT = 229312

V_NMIX = 0
V_NFFN = 32
V_BDW = 64
V_CNG = 72
V_CNB = 80
V_PSC = 88
V_QN = 104
V_KN = 106
V_SINK = 108
V_CMASK = 116
NV = 120


class Op:
    __slots__ = ("eng", "fn", "deps", "signal", "tsem", "tick", "dma", "batch", "known", "waits", "idx")


class Sched:
    def __init__(self, nc, stack):
        self.nc = nc
        self.stack = stack
        self.all = []
        self.last_w = {}
        self.readers = {}
        self.esem = {e: stack.enter_context(nc.semaphore("s_" + e)) for e in ENGS}
        self.dsem = {}
        self.dcount = {}
        self.sems = dict(self.esem)
        self.phase_deps = []
        self._cap = None
        self.nalloc = 0

    PERSIST = ("x", "vecs", "cmat", "identf", "onesf", "small", "P")

    def _persistent(self, k):
        b = self._bname(k)
        return b in self.PERSIST or b.startswith("o_")

    def new_phase(self):
        retired = {}
        for k in [k for k in self.last_w if not self._persistent(k)]:
            w = self.last_w.pop(k)
            retired[w.idx] = w
        for k in [k for k in self.readers if not self._persistent(k)]:
            for r in self.readers.pop(k).values():
                retired[r.idx] = r
        for r in self.phase_deps:
            retired[r.idx] = r
        keep = {}
        for r in retired.values():
            key = r.eng if r.dma is None else ("dma", r.idx)
            if key not in keep or keep[key].idx < r.idx:
                keep[key] = r
        self.phase_deps = list(keep.values())

    def buf(self, name, shape, dtype, off):
        size = int(np.prod(shape[1:])) * (2 if dtype == BF16 else 4)
        assert off + size <= SBUF_LIMIT, (name, off, size)
        self.nalloc += 1
        return self.nc.alloc_sbuf_tensor_at("%s_%d" % (name, self.nalloc), list(shape), dtype, offset=off)

    @staticmethod
    def _bname(k):
        return k if isinstance(k, str) else k[0]

    def capture(self, f):
        self._cap = []
        f()
        lst, self._cap = self._cap, None
        return lst

    def replay_merged(self, la, lb):
        i = j = 0
        while i < len(la) or j < len(lb):
            if j >= len(lb) or (i < len(la) and i * len(lb) <= j * len(la)):
                self.add(*la[i])
                i += 1
            else:
                self.add(*lb[j])
                j += 1

    def add(self, eng, fn, reads=(), writes=(), dma=None, batch=0):
        if self._cap is not None:
            self._cap.append((eng, fn, list(reads), list(writes), dma, batch))
            return None
        op = Op()
        op.eng, op.fn, op.signal, op.dma, op.batch = eng, fn, False, dma, batch
        op.idx = len(self.all)
        deps = {}
        for k in reads:
            w = self.last_w.get(k)
            if w is not None:
                deps[w.idx] = w
        for k in writes:
            w = self.last_w.get(k)
            if w is not None:
                deps[w.idx] = w
            rd = self.readers.get(k)
            if rd:
                for r in rd.values():
                    deps[r.idx] = r
        if self.phase_deps:
            for k in list(reads) + list(writes):
                if not self._persistent(k):
                    for r in self.phase_deps:
                        deps[r.idx] = r
                    break
        op.deps = list(deps.values())
        if fn is not None:
            for k in reads:
                rd = self.readers.setdefault(k, {})
                rd[eng if dma is None else ("dma", op.idx)] = op
            for k in writes:
                self.last_w[k] = op
                self.readers[k] = {}
        if dma is not None:
            if dma not in self.dsem:
                h = self.stack.enter_context(self.nc.semaphore("d_" + dma))
                self.dsem[dma] = h
                self.sems[dma] = h
                self.dcount[dma] = {}
            c = self.dcount[dma]
            if batch is None:
                batch = op.batch = len(c) + 1
            c[batch] = c.get(batch, 0) + 1
        for p in op.deps:
            if p.dma is None and not (p.eng == "pe" and eng == "pe" and dma is None):
                p.signal = True
        self.all.append(op)
        return op

    def finalize(self):
        count = {e: 0 for e in ENGS}
        seen = {e: {} for e in ENGS}
        bend = {}
        for name, c in self.dcount.items():
            tot = 0
            bend[name] = {}
            for b in sorted(c):
                tot += 16 * c[b]
                bend[name][b] = tot
        started = {}
        self.per_eng = {e: [] for e in ENGS}
        for op in self.all:
            E = op.eng
            sE = seen[E]
            waits = {}
            for p in sorted(op.deps, key=lambda p: -p.idx):
                if p.dma is None and p.eng == "pe" and E == "pe" and op.dma is None:
                    continue
                if p.dma is not None and p.dma == op.dma and p.batch == op.batch:
                    raise AssertionError("intra-batch DMA dependency on " + str(p.dma))
                if sE.get(p.tsem, 0) >= p.tick:
                    continue
                waits[p.tsem] = max(waits.get(p.tsem, 0), p.tick)
                for k, v in p.known.items():
                    if sE.get(k, 0) < v:
                        sE[k] = v
            if op.dma is not None:
                name = op.dma
                prev = started.get(name)
                if prev is not None and prev != op.batch:
                    assert op.batch > prev, (name, prev, op.batch)
                    v = bend[name][prev]
                    if sE.get(name, 0) < v:
                        waits[name] = max(waits.get(name, 0), v)
                        sE[name] = v
                started[name] = op.batch
                op.tsem, op.tick = name, bend[name][op.batch]
            elif op.signal:
                count[E] += 1
                op.tsem, op.tick = E, count[E]
            else:
                op.tsem, op.tick = E, count[E] + 1
            op.waits = list(waits.items())
            kn = dict(sE)
            if op.dma is not None or op.signal:
                kn[op.tsem] = max(kn.get(op.tsem, 0), op.tick)
            op.known = kn
            self.per_eng[E].append(op)

    def emit_engine(self, name, eng):
        for op in self.per_eng[name]:
            for (k, v) in op.waits:
                eng.wait_ge(self.sems[k], v)
            if op.fn is None:
                continue
            inst = op.fn(eng)
            if op.dma is not None:
                inst.then_inc(self.dsem[op.dma], 16)
            elif op.signal:
                inst.then_inc(self.esem[name], 1)

    def emit(self):
        self.finalize()
        with self.nc.Block() as block:
            @block.tensor
            def _(e):
                self.emit_engine("pe", e)

            @block.scalar
            def _(e):
                self.emit_engine("act", e)

            @block.vector
            def _(e):
                self.emit_engine("dve", e)

            @block.gpsimd
            def _(e):
                self.emit_engine("pool", e)

            @block.sync
            def _(e):
                self.emit_engine("sp", e)


class Arena:
    def __init__(self, base, limit=SBUF_LIMIT):
        self.p = base
        self.limit = limit

    def take(self, shape, dtype):
        size = int(np.prod(shape[1:])) * (2 if dtype == BF16 else 4)
        off = self.p
        self.p = (off + size + 63) // 64 * 64
        assert self.p <= self.limit, ("arena overflow", self.p)
        return off


def xkeys(c0, n, chs=range(NCH)):
    return [("x", ch, b) for ch in chs for b in range(c0 // 128, (c0 + n + 127) // 128)]


def MM(out, lhsT, rhs, start=True, stop=True):
    return lambda e: e.matmul(out, lhsT=lhsT, rhs=rhs, start=start, stop=stop)


def TR(out, in_, identity):
    return lambda e: e.transpose(out=out, in_=in_, identity=identity)


def ACT(out, in_, func, bias=None, scale=None):
    kw = {}
    if bias is not None:
        kw["bias"] = bias
    if scale is not None:
        kw["scale"] = scale
    return lambda e: e.activation(out=out, in_=in_, func=func, **kw)


def ACOPY(out, in_):
    return lambda e: e.copy(out=out, in_=in_)


def TT(out, in0, in1, op):
    return lambda e: e.tensor_tensor(out=out, in0=in0, in1=in1, op=op)


def STT(out, in0, scalar, in1, op0, op1):
    return lambda e: e.scalar_tensor_tensor(out=out, in0=in0, scalar=scalar, in1=in1, op0=op0, op1=op1)


def TS(out, in0, scalar1, op0, scalar2=None, op1=None):
    if op1 is None:
        return lambda e: e.tensor_scalar(out=out, in0=in0, scalar1=scalar1, scalar2=None, op0=op0)
    return lambda e: e.tensor_scalar(out=out, in0=in0, scalar1=scalar1, scalar2=scalar2, op0=op0, op1=op1)


def TCOPY(out, in_):
    return lambda e: e.tensor_copy(out=out, in_=in_)


def RECIP(out, in_):
    return lambda e: e.reciprocal(out=out, in_=in_)


def MSET(ap, v):
    return lambda e: e.memset(ap, v)


def TRED(out, in_, op):
    return lambda e: e.tensor_reduce(out=out, in_=in_, axis=mybir.AxisListType.X, op=op)


def DMA(out, in_):
    return lambda e: e.dma_start(out=out, in_=in_)


def build_program(n_layers=4):
    nc = bass.Bass("TRN2", target_bir_lowering=False)

    def din(name, shape):
        return nc.dram_tensor(name, list(shape), F32, kind="ExternalInput").ap()

    def dout(name, shape):
        return nc.dram_tensor(name, list(shape), F32, kind="ExternalOutput").ap()

    xT = din("xT", [D, TOK])
    vecs_d = din("vecs", [128, NV])
    cmat_d = din("cmat", [128, 5 * 128])
    emask_d = din("emask", [128, 2 * 2 * 512])
    esamp_d = din("esamp", [128, 2 * 32])
    esnew_d = din("esnew", [8, 2 * 32])
    invcnt_d = din("invcnt", [128, 4 * 128])
    w_in_d = din("w_in", [2, 128, NCH * DIN])
    w_out_d = din("w_out", [2, 128, NCH * D])
    w_dw_d = din("w_dwT", [2, 128, 4 * 31])
    w_pool_d = din("w_pool", [2, 128, 4 * 2 * 256])
    w_ffn_d = din("w_ffn", [4, NFC, 128, 3072])
    cconvT_d = din("cconvT", [2, 128, 4 * 16 * 30])
    ckT_d = din("ckT", [2, 128, 16 * 128])
    cv_d = din("cv_nat", [2, 16, 128, 128])
    ck_d = din("ck_nat", [2, 16, 128, 128])
    cconv_d = din("cconv_nat", [2, 16, 30, 512])
    cpool_d = din("cpool_nat", [2, 16, 15, 1024])
    cpoolT_d = din("cpoolT", [2, 128, 8 * 16 * 15])

    yT = dout("yT", [D, NOUT])
    o_gluT = dout("o_gluT", [2, 128, 4 * 30])
    o_kT = dout("o_kT", [2, 128, 128])
    o_vtok = dout("o_vtok", [2, 128, 128])
    o_xnT = dout("o_xnT", [2, 128, 8 * 15])
    o_convs_old = dout("o_convs_old", [2, 16, 22, 512])
    o_gluT_s = dout("o_gluT_s", [2, 128, 4 * 128])
    o_ks_old = dout("o_ks_old", [2, 16, 120, 128])
    o_kT_s = dout("o_kT_s", [2, 128, 128])
    o_vs_old = dout("o_vs_old", [2, 16, 120, 128])
    o_vtok_s = dout("o_vtok_s", [2, 128, 128])
    o_pools_old = dout("o_pools_old", [2, 16, 7, 1024])
    o_xnT_s = dout("o_xnT_s", [2, 128, 8 * 128])

    with ExitStack() as st:
        S = Sched(nc, st)
        A = S.add
        P = [nc.alloc_psum_tensor("P%d" % i, [128, 512], F32) for i in range(8)]

        def pk(i):
            return ("P", i)

        x = S.buf("x", [128, NCH, TOK], F32, SBUF_BASE)
        CB = SBUF_BASE + 81920
        vecs = S.buf("vecs", [128, NV], F32, CB)
        cmat = S.buf("cmat", [128, 5, 128], BF16, CB + 512)
        identf = S.buf("identf", [128, 128], F32, CB + 512 + 1280)
        onesf = S.buf("onesf", [128, 128], F32, CB + 512 + 1280 + 512)
        small = S.buf("small", [128, 16], F32, CB + 512 + 1280 + 1024)
        ABASE = CB + 512 + 1280 + 1024 + 64
        ONES_RMS, ONES_LN, BDIAG, ONES1, IDENT = range(5)
        out_ops = []

        def vcol(c):
            return vecs[:, c:c + 1]

        def b16(ap):
            return ap.rearrange("p (b t) -> p b t", b=16)

        def two(ap):
            return ap.rearrange("p (a c) -> p a c", a=2)

        for ch in range(NCH):
            A("sp", DMA(x[:, ch, :], xT[ch * 128:(ch + 1) * 128, :]), writes=xkeys(0, TOK, [ch]), dma="xin")
        A("sp", DMA(vecs[:], vecs_d[:, :]), writes=["vecs"], dma="cst")
        A("pool", DMA(cmat[:].rearrange("p a b -> p (a b)"), cmat_d[:, :]), writes=["cmat"], dma="cst2")
        A("sp", DMA(identf[:], cmat_d[:, 4 * 128:5 * 128]), writes=["identf"], dma="cst")
        A("sp", DMA(onesf[:], cmat_d[:, 3 * 128:4 * 128]), writes=["onesf"], dma="cst")

        for i in range(2):
            A("sp", DMA(o_convs_old[i], cconv_d[i, :, 8:30, :]), writes=[("o_shift", 0, i)], dma="shift")
            A("sp", DMA(o_ks_old[i], ck_d[i, :, 8:128, :]), writes=[("o_shift", 1, i)], dma="shift")
            A("sp", DMA(o_vs_old[i], cv_d[i, :, 8:128, :]), writes=[("o_shift", 2, i)], dma="shift")
            A("sp", DMA(o_pools_old[i], cpool_d[i, :, 8:15, :]), writes=[("o_shift", 3, i)], dma="shift")
            out_ops.extend([("o_shift", q, i) for q in range(4)])

        def rstd_from(ps_ap, dst_ap, eps, rkeys, wkeys):
            A("act", ACT(dst_ap, ps_ap, AF.Ln, bias=epsap[eps], scale=1.0), reads=list(rkeys) + ["small"], writes=wkeys)
            A("act", ACT(dst_ap, dst_ap, AF.Exp, scale=-0.5), reads=wkeys, writes=wkeys)

        A("dve", MSET(small[:, 8:9], RMS_EPS), writes=["small"])
        A("dve", MSET(small[:, 9:10], LN_EPS), writes=["small"])
        epsap = {RMS_EPS: small[:, 8:9], LN_EPS: small[:, 9:10]}

        def norm_group(c0, n, gbase, sq, rstd, dst_fn, dst_keys, view=None, bank=4, sqk="sq", rk="rstd"):
            A("act", ACT(sq[:, :, 0:n], x[:, :, c0:c0 + n], AF.Square), reads=xkeys(c0, n), writes=[sqk])
            for ch in range(NCH):
                A("pe", MM(P[bank][:, 0:n], cmat[:, ONES_RMS, :], sq[:, ch, 0:n], ch == 0, ch == NCH - 1), reads=[sqk, "cmat"], writes=[pk(bank)])
            rstd_from(P[bank][:, 0:n], rstd[:, 0:n], RMS_EPS, [pk(bank)], [rk])
            for ch in range(NCH):
                i0, i1 = x[:, ch, c0:c0 + n], rstd[:, 0:n]
                if view is not None:
                    i0, i1 = view(i0), view(i1)
                A("dve", STT(dst_fn(ch), i0, vcol(gbase + ch), i1, ALU.mult, ALU.mult),
                  reads=xkeys(c0, n, [ch]) + [rk, "vecs"], writes=dst_keys)

        def yps(dc, n):
            return P[dc // 2][:, (dc % 2) * 256:(dc % 2) * 256 + n]

        def add_y_to_x(c0, n):
            for b in range(4):
                xs = x[:, 2 * b:2 * b + 2, c0:c0 + n]
                A("dve", TT(xs, two(P[b][:, :])[:, :, 0:n], xs, ALU.add),
                  reads=[pk(b)] + xkeys(c0, n, [2 * b, 2 * b + 1]), writes=xkeys(c0, n, [2 * b, 2 * b + 1]))

        def split_groups(c_start, c_end):
            gs = []
            c = c_end
            while c > c_start:
                n = min(256, c - c_start)
                gs.append((c - n, n))
                c -= n
            return gs[::-1]

        def ffn(L):
            start_blk = (1, 1, 2, 3)[L]
            groups = split_groups(start_blk * 128, TOK)
            halves = [groups[:5], groups[5:]]
            S.new_phase()
            ar = Arena(ABASE)
            h = S.buf("h", [128, NCH, TOK], BF16, ar.take([128, NCH, TOK], BF16))
            wp = [S.buf("wp%d" % s, [128, 4, 3072], BF16, ar.take([128, 4, 3072], BF16)) for s in range(2)]
            sg = [S.buf("sg%d" % s, [128, 256], F32, ar.take([128, 256], F32)) for s in range(2)]
            abuf = [S.buf("a%d" % s, [128, 4, 256], BF16, ar.take([128, 4, 256], BF16)) for s in range(2)]
            sqs = [S.buf("sq%d" % s, [128, NCH, 256], BF16, ar.take([128, NCH, 256], BF16)) for s in range(2)]
            rstds = [S.buf("rstd%d" % s, [128, 256], F32, ar.take([128, 256], F32)) for s in range(2)]
            pieces = [(f, min(4, NFC - f)) for f in range(0, NFC, 4)]
            gbase = V_NFFN + L * 8
            bcount = [0]

            def load_piece(pi, slot):
                f0, nf = pieces[pi]
                bcount[0] += 1
                A("pool", DMA(wp[slot][:, 0:nf, :], w_ffn_d[L, f0:f0 + nf].rearrange("f p n -> p f n")),
                  writes=["wp%d" % slot], dma="wffn%d_%d" % (L, slot), batch=bcount[0])

            def norm_h(c0, n, bank, st):
                norm_group(c0, n, gbase, sqs[st], rstds[st], lambda ch: h[:, ch, c0:c0 + n], [("h", c0)], bank=bank, sqk="sq%d" % st, rk="rstd%d" % st)

            def emit_Y(slot, nf, c0, n, aslot):
                for dc in range(NCH):
                    for fl in range(nf):
                        A("pe", MM(yps(dc, n), wp[slot][:, fl, 2048 + dc * 128:2048 + (dc + 1) * 128], abuf[aslot][:, fl, 0:n], fl == 0, fl == nf - 1),
                          reads=["wp%d" % slot, "a%d" % aslot], writes=[pk(dc // 2)])
                add_y_to_x(c0, n)

            cnt = 0
            for hi, half in enumerate(halves):
                if not half:
                    continue
                other = halves[1] if hi == 0 else []
                load_piece(0, 0)
                if hi == 0:
                    norm_h(half[0][0], half[0][1], 3, 0)
                for pi, (f0, nf) in enumerate(pieces):
                    slot = pi % 2
                    if pi + 1 < len(pieces):
                        load_piece(pi + 1, 1 - slot)
                    wk = "wp%d" % slot
                    prev = None
                    for gi, (c0, n) in enumerate(half):
                        if hi == 0 and pi == 0 and gi + 1 < len(half):
                            norm_h(half[gi + 1][0], half[gi + 1][1], 3, (gi + 1) % 2)
                        if hi == 0 and pi == len(pieces) - 1 and gi < len(other):
                            norm_h(other[gi][0], other[gi][1], 7, gi % 2)
                        aslot = gi % 2
                        for fl in range(nf):
                            bank = 4 + fl
                            for kc in range(NCH):
                                A("pe", MM(P[bank][:, 0:n], wp[slot][:, fl, kc * 128:(kc + 1) * 128], h[:, kc, c0:c0 + n], kc == 0, kc == NCH - 1),
                                  reads=[wk, ("h", c0)], writes=[pk(bank)])
                            for kc in range(NCH):
                                A("pe", MM(P[bank][:, 256:256 + n], wp[slot][:, fl, 1024 + kc * 128:1024 + (kc + 1) * 128], h[:, kc, c0:c0 + n], kc == 0, kc == NCH - 1),
                                  reads=[wk, ("h", c0)], writes=[pk(bank)])
                            ss = cnt % 2
                            cnt += 1
                            A("act", ACT(sg[ss][:, 0:n], P[bank][:, 0:n], AF.Silu), reads=[pk(bank)], writes=["sg%d" % ss])
                            A("dve", TT(abuf[aslot][:, fl, 0:n], sg[ss][:, 0:n], P[bank][:, 256:256 + n], ALU.mult),
                              reads=["sg%d" % ss, pk(bank)], writes=["a%d" % aslot])
                        if prev is not None:
                            emit_Y(*prev)
                        prev = (slot, nf, c0, n, aslot)
                    emit_Y(*prev)

        def mixer_even(L):
            i = L // 2
            start_blk = (0, 0, 1, 0)[L]
            S.new_phase()
            ar = Arena(ABASE)

            def mk(name, shape, dt):
                return S.buf(name, shape, dt, ar.take(shape, dt))

            w_in = mk("w_in", [128, NCH, DIN], BF16)
            w_out = mk("w_out", [128, NCH, D], BF16)
            diag = mk("diag", [128, 4, 31, 128], BF16)
            wdw = mk("wdw", [128, 4, 32], F32)
            xn = mk("xn", [128, NCH, 256], BF16)
            sq = mk("sq", [128, NCH, 256], BF16)
            rstd = mk("rstd", [128, 512], F32)
            tmpA = mk("tmpA", [128, 4, 256], F32)
            sqq = mk("sqq", [128, 2, 256], BF16)
            rstdq = mk("rstdq", [128, 512], F32)
            pexp = mk("pexp", [128, 512], F32)
            denr = mk("denr", [128, 512], F32)
            pTb = mk("pT", [128, 2, 512], BF16)
            rhso = mk("rhso", [128, NCH, 256], BF16)
            qT = mk("qT", [128, 4, 256], BF16)
            o32 = mk("o32", [128, 4, 128], F32)
            k32 = mk("k32", [128, 128], F32)
            v32 = mk("v32", [128, 128], F32)
            esamp = mk("esamp", [128, 2, 32], F32)
            esnew = mk("esnew", [8, 2, 32], F32)
            kTs = mk("kTs", [128, 128], BF16)
            vnew = mk("vnew", [8, 16, 128], BF16)
            rbase = ar.p
            emask = mk("emask", [128, 2, 2, 512], F32)
            kT = mk("kT", [128, 5 * 128], BF16)
            vall = mk("vall", [128, 5, 128], BF16)
            glu = mk("glu", [128, 4, 288], BF16)
            rend = ar.p
            ar.p = rbase
            ext = mk("ext", [128, 4, 16, 38], BF16)
            ckT = mk("ckT", [128, 16, 128], BF16)
            cv = mk("cv", [128, 16, 128], BF16)
            assert ar.p <= rend, (ar.p, rend)
            meanb = two(rstd[:, :])
            vbf = sq[:, 0:4, :]
            sq2 = sq[:, 4:8, :]
            alias_keys = ["emask", "glu"] + [("kT", b) for b in range(-1, 19)] + [("vall", b) for b in range(-1, 19)]

            def kslot(b):
                return (b + 1) % 5

            wsem, csem = "wmix%d" % L, "cmix%d" % L
            A("pool", DMA(w_in[:].rearrange("p a b -> p (a b)"), w_in_d[i]), writes=["w_in"], dma=wsem)
            A("pool", DMA(w_out[:].rearrange("p a b -> p (a b)"), w_out_d[i]), writes=["w_out"], dma=wsem)
            A("sp", DMA(wdw[:, :, 0:31], w_dw_d[i].rearrange("p (a b) -> p a b", a=4)), writes=["wdw"], dma=csem)
            A("sp", DMA(emask[:].rearrange("p a b c -> p (a b c)"), emask_d[:, :]), writes=["emask"], dma=csem)
            A("sp", DMA(esamp[:].rearrange("p a b -> p (a b)"), esamp_d[:, :]), writes=["esamp"], dma=csem)
            A("sp", DMA(esnew[:].rearrange("p a b -> p (a b)"), esnew_d[:, :]), writes=["esnew"], dma=csem)
            for ch in range(4):
                for j in range(31):
                    A("dve", TS(diag[:, ch, j, :], cmat[:, IDENT, :], wdw[:, ch, j:j + 1], ALU.mult), reads=["cmat", "wdw"], writes=[("diag", ch, j)])
            A("dve", MSET(kT[:, :], 0.0), writes=[("kT", b) for b in range(-1, 19)])
            A("dve", MSET(vall[:].rearrange("p a b -> p (a b)"), 0.0), writes=[("vall", b) for b in range(-1, 19)])
            A("dve", MSET(glu[:].rearrange("p a b -> p (a b)"), 0.0), writes=["glu"])
            qn, kn = vcol(V_QN + i), vcol(V_KN + i)
            A("dve", TT(small[:, 0:1], qn, kn, ALU.mult), reads=["vecs"], writes=["small"])
            A("dve", TS(small[:, 1:2], small[:, 0:1], -1.0, ALU.mult), reads=["small"], writes=["small"])
            A("dve", TT(small[:, 0:1], small[:, 0:1], small[:, 1:2], ALU.max), reads=["small"], writes=["small"])
            A("pe", TR(P[7][0:1, 0:128], small[:, 0:1], identf[:]), reads=["small", "identf"], writes=[pk(7)])
            A("dve", TRED(small[0:1, 2:3], P[7][0:1, 0:128], ALU.max), reads=[pk(7)], writes=["small"])
            A("pe", MM(P[7][:, 128:129], onesf[0:1, :], small[0:1, 2:3]), reads=["small", "onesf"], writes=[pk(7)])
            A("dve", TS(small[:, 3:4], P[7][:, 128:129], -8.0, ALU.mult), reads=[pk(7)], writes=["small"])
            negM = small[:, 3:4]
            A("act", ACT(small[:, 4:8], vecs[:, V_SINK + 4 * i:V_SINK + 4 * i + 4], AF.Exp, bias=negM, scale=1.0), reads=["small", "vecs"], writes=["small"])
            sinkexp = small[:, 4:8]

            gbase = V_NMIX + L * 8
            pgroups = split_groups(start_blk * 128, OWN1)
            allgroups = [(c0, n, False) for (c0, n) in pgroups] + [(OWN1, 128, True)]
            spi = [0]

            def abank():
                b = 6 + (spi[0] % 2)
                spi[0] += 1
                return b

            def proj(dst_ps, col0, n, keyw):
                for kc in range(NCH):
                    A("pe", MM(dst_ps, w_in[:, kc, col0:col0 + 128], xn[:, kc, 0:n], kc == 0, kc == NCH - 1), reads=["w_in", "xn"], writes=[keyw])

            def chain_conv(c0, n, samp, last):
                for ch in range(4):
                    bank = abank()
                    proj(P[bank][:, 0:n], ch * 128, n, pk(bank))
                    proj(P[bank][:, 256:256 + n], 512 + ch * 128, n, pk(bank))
                    A("act", ACT(tmpA[:, ch, 0:n], P[bank][:, 256:256 + n], AF.Sigmoid), reads=[pk(bank)], writes=[("tmpA", ch)])
                    if samp:
                        A("dve", TT(ext[:, ch, :, 30:38], b16(P[bank][:, 0:128]), b16(tmpA[:, ch, 0:128]), ALU.mult), reads=[pk(bank), ("tmpA", ch)], writes=["ext"])
                    else:
                        A("dve", TT(glu[:, ch, 32:32 + n], P[bank][:, 0:n], tmpA[:, ch, 0:n], ALU.mult), reads=[pk(bank), ("tmpA", ch)], writes=["glu"])
                    if samp or last:
                        n0 = n - 128
                        A("dve", TT(o32[:, ch, :], P[bank][:, n0:n0 + 128], tmpA[:, ch, n0:n0 + 128], ALU.mult), reads=[pk(bank), ("tmpA", ch)], writes=["o32"])
                if samp:
                    A("sp", DMA(o_gluT_s[i], o32[:].rearrange("p a b -> p (a b)")), reads=["o32"], writes=[("o_glus", i)], dma="outs", batch=None)
                    out_ops.append(("o_glus", i))
                elif last:
                    A("sp", DMA(o_gluT[i].rearrange("p (a b) -> p a b", a=4), o32[:, :, 98:128]), reads=["o32"], writes=[("o_glu", i)], dma="outs", batch=None)
                    out_ops.append(("o_glu", i))
                for ch in range(4):
                    cps = P[ch // 2][:, (ch % 2) * 256:(ch % 2) * 256 + n]
                    for j in range(31):
                        rhs = ext[:, ch, :, j:j + 8] if samp else glu[:, ch, 2 + j:2 + j + n]
                        A("pe", MM(cps, diag[:, ch, j, :], rhs, j == 0, j == 30), reads=[("diag", ch, j), "ext" if samp else "glu"], writes=[pk(ch // 2)])
                    bcol = vcol(V_BDW + 4 * i + ch)
                    A("act", ACT(vbf[:, ch, 0:n], cps, AF.Identity, bias=bcol, scale=1.0), reads=[pk(ch // 2), "vecs"], writes=["sq"])
                    A("act", ACT(sq2[:, ch, 0:n], cps, AF.Square, bias=bcol, scale=1.0), reads=[pk(ch // 2), "vecs"], writes=["sq"])
                if not samp:
                    A("pool", TCOPY(glu[:, :, 0:32], glu[:, :, n:n + 32]), reads=["glu"], writes=["glu"])
                for ch in range(4):
                    A("pe", MM(P[7][:, 0:n], cmat[:, ONES_LN, :], vbf[:, ch, 0:n], ch == 0, ch == 3), reads=["sq", "cmat"], writes=[pk(7)])
                for ch in range(4):
                    A("pe", MM(P[7][:, 256:256 + n], cmat[:, ONES_LN, :], sq2[:, ch, 0:n], ch == 0, ch == 3), reads=["sq", "cmat"], writes=[pk(7)])
                A("act", ACOPY(meanb[:, 0, 0:n], P[7][:, 0:n]), reads=[pk(7)], writes=["rstd"])
                A("dve", STT(meanb[:, 1, 0:n], meanb[:, 0, 0:n], -1.0, meanb[:, 0, 0:n], ALU.mult, ALU.mult), reads=["rstd"], writes=["rstd"])
                A("dve", TT(meanb[:, 1, 0:n], P[7][:, 256:256 + n], meanb[:, 1, 0:n], ALU.add), reads=[pk(7), "rstd"], writes=["rstd"])
                rstd_from(meanb[:, 1, 0:n], meanb[:, 1, 0:n], LN_EPS, ["rstd"], ["rstd"])
                for ch in range(4):
                    cps = P[ch // 2][:, (ch % 2) * 256:(ch % 2) * 256 + n]
                    bcol = vcol(V_BDW + 4 * i + ch)
                    A("dve", STT(tmpA[:, ch, 0:n], cps, bcol, meanb[:, 0, 0:n], ALU.add, ALU.subtract), reads=[pk(ch // 2), "vecs", "rstd"], writes=[("tmpA", ch)])
                    A("dve", TT(tmpA[:, ch, 0:n], tmpA[:, ch, 0:n], meanb[:, 1, 0:n], ALU.mult), reads=[("tmpA", ch), "rstd"], writes=[("tmpA", ch)])
                    A("act", ACT(rhso[:, ch, 0:n], tmpA[:, ch, 0:n], AF.Silu, bias=vcol(V_CNB + 4 * i + ch), scale=vcol(V_CNG + 4 * i + ch)),
                      reads=[("tmpA", ch), "vecs"], writes=[("rhso", ch)])

            def chain_attn(c0, n, samp, last):
                nb = n // 128
                for pair in range(2):
                    bank = 5
                    for jj in range(2):
                        proj(P[bank][:, jj * 256:jj * 256 + n], 1024 + (pair * 2 + jj) * 128, n, pk(bank))
                    qps = two(P[bank][:, :])[:, :, 0:n]
                    A("act", ACT(sqq[:, 0:2, 0:n], qps, AF.Square), reads=[pk(bank)], writes=["sqq"])
                    for jj in range(2):
                        A("pe", MM(P[4][:, jj * 256:jj * 256 + n], cmat[:, BDIAG, :], sqq[:, jj, 0:n]), reads=["sqq", "cmat"], writes=[pk(4)])
                    r3 = two(rstdq[:, :])[:, :, 0:n]
                    rstd_from(two(P[4][:, :])[:, :, 0:n], r3, RMS_EPS, [pk(4)], ["rstdq"])
                    A("dve", STT(qT[:, 2 * pair:2 * pair + 2, 0:n], qps, qn, r3, ALU.mult, ALU.mult), reads=[pk(bank), "rstdq", "vecs"], writes=["qT"])
                bank = 5
                proj(P[bank][:, 0:n], 1536, n, pk(bank))
                A("act", ACT(sqq[:, 0, 0:n], P[bank][:, 0:n], AF.Square), reads=[pk(bank)], writes=["sqq"])
                A("pe", MM(P[4][:, 0:n], cmat[:, BDIAG, :], sqq[:, 0, 0:n]), reads=["sqq", "cmat"], writes=[pk(4)])
                rstd_from(P[4][:, 0:n], rstdq[:, 0:n], RMS_EPS, [pk(4)], ["rstdq"])
                if samp:
                    A("dve", STT(kTs[:, 0:128], P[bank][:, 0:n], kn, rstdq[:, 0:n], ALU.mult, ALU.mult), reads=[pk(bank), "rstdq", "vecs"], writes=["kTs"])
                else:
                    for bl in range(nb):
                        blk = c0 // 128 + bl
                        sl = kslot(blk)
                        A("dve", STT(kT[:, sl * 128:(sl + 1) * 128], P[bank][:, bl * 128:(bl + 1) * 128], kn, rstdq[:, bl * 128:(bl + 1) * 128], ALU.mult, ALU.mult),
                          reads=[pk(bank), "rstdq", "vecs"], writes=[("kT", blk), ("kT", blk - 5)])
                if samp or last:
                    n0 = n - 128
                    A("dve", STT(k32[:, :], P[bank][:, n0:n0 + 128], kn, rstdq[:, n0:n0 + 128], ALU.mult, ALU.mult), reads=[pk(bank), "rstdq", "vecs"], writes=["k32"])
                    okk = ("o_k", samp, i)
                    A("sp", DMA((o_kT_s if samp else o_kT)[i], k32[:, :]), reads=["k32"], writes=[okk], dma="outs", batch=None)
                    out_ops.append(okk)
                for bl in range(nb):
                    blk = c0 // 128 + bl
                    bank = 5
                    for kc in range(NCH):
                        A("pe", MM(P[bank][:, 0:128], xn[:, kc, bl * 128:(bl + 1) * 128], w_in[:, kc, 1664:1792], kc == 0, kc == NCH - 1), reads=["w_in", "xn"], writes=[pk(bank)])
                    if not samp:
                        A("act", ACOPY(vall[:, kslot(blk), :], P[bank][:, 0:128]), reads=[pk(bank)], writes=[("vall", blk), ("vall", blk - 5)])
                    if samp or (last and bl == nb - 1):
                        A("act", ACOPY(v32[:, :], P[bank][:, 0:128]), reads=[pk(bank)], writes=["v32"])
                        ovk = ("o_v", samp, i)
                        A("sp", DMA((o_vtok_s if samp else o_vtok)[i], v32[:, :]), reads=["v32"], writes=[ovk], dma="outs", batch=None)
                        out_ops.append(ovk)
                if samp:
                    for r in range(4):
                        bank = 5
                        for bb in range(4):
                            b = r * 4 + bb
                            for kc in range(NCH):
                                A("pe", MM(P[bank][0:8, bb * 128:(bb + 1) * 128], xn[:, kc, b * 8:(b + 1) * 8], w_in[:, kc, 1664:1792], kc == 0, kc == NCH - 1), reads=["w_in", "xn"], writes=[pk(bank)])
                        A("act", ACOPY(vnew[0:8, r * 4:(r + 1) * 4, :], P[bank][0:8, :].rearrange("p (a b) -> p a b", a=4)), reads=[pk(bank)], writes=["vnew"])
                if not samp:
                    slot = 0
                    for bl in range(nb):
                        blk = c0 // 128 + bl
                        qo = bl * 128
                        for kv in range(2):
                            ps_ = slice(kv * 64, (kv + 1) * 64)
                            for kb in range(2):
                                kblk = blk - 1 + kb
                                ks = kslot(kblk)
                                sb = 5
                                A("pe", MM(P[sb][:, :], kT[ps_, ks * 128:(ks + 1) * 128], qT[ps_, :, qo:qo + 128]), reads=[("kT", kblk), "qT"], writes=[pk(sb)])
                                A("act", ACT(pexp[:, :], P[sb][:, :], AF.Exp, bias=negM, scale=0.125), reads=[pk(sb), "small"], writes=["pexp"])
                                pslot = pTb[:, slot % 2, :]
                                pkey = ("pT", slot % 2)
                                slot += 1
                                A("dve", TT(pslot, pexp[:, :], emask[:, kb, kv, :], ALU.mult), reads=["pexp", "emask"], writes=[pkey])
                                if blk == 3 and kb == 0:
                                    A("dve", TS(pslot, pslot, vcol(V_CMASK), ALU.mult), reads=[pkey, "vecs"], writes=[pkey])
                                A("pe", MM(P[2][ps_, :], vall[:, ks, ps_], pslot, kb == 0, kb == 1), reads=[("vall", kblk), pkey], writes=[pk(2)])
                                A("pe", MM(P[3][ps_, :], cmat[:, ONES1, 0:64], pslot, kb == 0, kb == 1), reads=["cmat", pkey], writes=[pk(3)])
                        for j in range(4):
                            A("dve", TS(denr[:, j * 128:(j + 1) * 128], P[3][:, j * 128:(j + 1) * 128], sinkexp[:, j:j + 1], ALU.add), reads=[pk(3), "small"], writes=["denr"])
                        A("dve", RECIP(denr[:, :], denr[:, :]), reads=["denr"], writes=["denr"])
                        A("dve", TT(rhso[:, 4:8, qo:qo + 128], P[2][:, :].rearrange("p (a c) -> p a c", a=4), denr[:, :].rearrange("p (a c) -> p a c", a=4), ALU.mult),
                          reads=[pk(2), "denr"], writes=[("rhso", 4 + j) for j in range(4)])
                else:
                    for kv in range(2):
                        ps_ = slice(kv * 64, (kv + 1) * 64)
                        for b in range(16):
                            A("pe", MM(P[5][:, b * 32:(b + 1) * 32], ckT[ps_, b, :], qT[ps_, :, b * 8:(b + 1) * 8]), reads=["ckT", "qT"], writes=[pk(5)])
                        for b in range(16):
                            A("pe", MM(P[4][0:8, b * 32:(b + 1) * 32], kTs[ps_, b * 8:(b + 1) * 8], qT[ps_, :, b * 8:(b + 1) * 8]), reads=["kTs", "qT"], writes=[pk(4)])
                        A("act", ACT(pexp[:, :], P[5][:, :], AF.Exp, bias=negM, scale=0.125), reads=[pk(5), "small"], writes=["pexp"])
                        A("act", ACT(denr[0:8, :], P[4][0:8, :], AF.Exp, bias=small[0:8, 3:4], scale=0.125), reads=[pk(4), "small"], writes=["denr"])
                        p1 = pTb[:, 0, :]
                        p2 = pTb[0:8, 1, :]
                        k1, k2 = ("pT", 0), ("pT", 1)
                        A("dve", TT(b16(p1), b16(pexp[:, :]), esamp[:, kv:kv + 1, :].to_broadcast([128, 16, 32]), ALU.mult), reads=["pexp", "esamp"], writes=[k1])
                        A("dve", TT(b16(p2), b16(denr[0:8, :]), esnew[0:8, kv:kv + 1, :].to_broadcast([8, 16, 32]), ALU.mult), reads=["denr", "esnew"], writes=[k2])
                        for b in range(16):
                            cs = slice(b * 32, (b + 1) * 32)
                            A("pe", MM(P[2][ps_, cs], cv[:, b, ps_], p1[:, cs], True, False), reads=["cv", k1], writes=[pk(2)])
                            A("pe", MM(P[2][ps_, cs], vnew[0:8, b, ps_], p2[:, cs], False, True), reads=["vnew", k2], writes=[pk(2)])
                            A("pe", MM(P[3][ps_, cs], cmat[:, ONES1, 0:64], p1[:, cs], True, False), reads=["cmat", k1], writes=[pk(3)])
                            A("pe", MM(P[3][ps_, cs], cmat[0:8, ONES1, 0:64], p2[:, cs], False, True), reads=["cmat", k2], writes=[pk(3)])
                    den4 = denr[:, :].rearrange("p (b j t) -> p j b t", b=16, j=4)
                    d34 = P[3][:, :].rearrange("p (b j t) -> p j b t", b=16, j=4)
                    o24 = P[2][:, :].rearrange("p (b j t) -> p j b t", b=16, j=4)
                    for j in range(4):
                        A("dve", TS(den4[:, j], d34[:, j], sinkexp[:, j:j + 1], ALU.add), reads=[pk(3), "small"], writes=["denr"])
                    A("dve", RECIP(denr[:, :], denr[:, :]), reads=["denr"], writes=["denr"])
                    for j in range(4):
                        A("dve", TT(b16(rhso[:, 4 + j, 0:128]), o24[:, j], den4[:, j], ALU.mult), reads=[pk(2), "denr"], writes=[("rhso", 4 + j)])

            for (c0, n, samp) in allgroups:
                last = (c0 + n == OWN1) and not samp
                if samp:
                    A("pool", None, writes=alias_keys)
                    A("pool", DMA(ext[:, :, :, 0:30], cconvT_d[i].rearrange("p (a b c) -> p a b c", a=4, b=16)), writes=["ext"], dma="csamp%d" % L)
                    A("pool", DMA(ckT[:].rearrange("p a b -> p (a b)"), ckT_d[i]), writes=["ckT"], dma="csamp%d" % L)
                    A("pool", DMA(cv[:], cv_d[i].rearrange("b k f -> k b f")), writes=["cv"], dma="csamp%d" % L)
                norm_group(c0, n, gbase, sq, rstd, lambda ch, n=n: xn[:, ch, 0:n], ["xn"])
                la = S.capture(lambda: chain_conv(c0, n, samp, last))
                lb = S.capture(lambda: chain_attn(c0, n, samp, last))
                S.replay_merged(la, lb)
                for dc in range(NCH):
                    for kc in range(NCH):
                        A("pe", MM(yps(dc, n), w_out[:, kc, dc * 128:(dc + 1) * 128], rhso[:, kc, 0:n], kc == 0, kc == NCH - 1), reads=["w_out", ("rhso", kc)], writes=[pk(dc // 2)])
                add_y_to_x(c0, n)

        def mixer_odd(L):
            i = L // 2
            start_blk = (0, 1, 0, 2)[L]
            S.new_phase()
            ar = Arena(ABASE)

            def mk(name, shape, dt):
                return S.buf(name, shape, dt, ar.take(shape, dt))

            wpl = mk("wpl", [128, 4, 2, 256], BF16)
            invc = mk("invc", [128, 4, 128], F32)
            sq = mk("sq", [128, NCH, 272], BF16)
            rstd = mk("rstd", [128, 272], F32)
            xw = mk("xw", [128, NCH, 272], F32)
            sA = mk("sA", [128, NCH, 272], F32)
            sB = mk("sB", [128, NCH, 272], F32)
            dd = mk("dd", [128, NCH, 256], BF16)
            tmp = mk("tmp", [128, 128], F32)
            ep = mk("ep", [128, NCH, 16, 23], F32)
            eA = mk("eA", [128, NCH, 16, 23], F32)
            eB = mk("eB", [128, NCH, 16, 23], F32)
            A("pool", DMA(wpl[:].rearrange("p a b c -> p (a b c)"), w_pool_d[i]), writes=["wpl"], dma="wmix%d" % L)
            A("sp", DMA(invc[:].rearrange("p a b -> p (a b)"), invcnt_d[:, :]), writes=["invc"], dma="cmix%d" % L)
            A("sp", DMA(ep[:, :, :, 0:15], cpoolT_d[i].rearrange("p (a b c) -> p a b c", a=8, b=16)), writes=["ep"], dma="cmix%d" % L)
            gbase = V_NMIX + L * 8

            def pool_out(c0, n):
                for g in range(4):
                    for oc in range(2):
                        dc = 2 * g + oc
                        for kc in range(2):
                            A("pe", MM(yps(dc, n), wpl[:, g, kc, oc * 128:(oc + 1) * 128], dd[:, 2 * g + kc, 0:n], kc == 0, kc == 1), reads=["wpl", "dd"], writes=[pk(dc // 2)])
                for dc in range(NCH):
                    xs = x[:, dc, c0:c0 + n]
                    A("dve", STT(xs, yps(dc, n), vcol(V_PSC + 8 * i + dc), xs, ALU.mult, ALU.add),
                      reads=[pk(dc // 2), "vecs"] + xkeys(c0, n, [dc]), writes=xkeys(c0, n, [dc]))

            prev_n = None
            for (c0, n) in split_groups(start_blk * 128, OWN1):
                last = (c0 + n == OWN1)
                m = n + 16
                if prev_n is None:
                    norm_group(c0 - 16, m, gbase, sq, rstd, lambda ch, m=m: xw[:, ch, 0:m], ["xw"])
                else:
                    A("dve", TCOPY(xw[:, :, 0:16], xw[:, :, prev_n:prev_n + 16]), reads=["xw"], writes=["xw"])
                    norm_group(c0, n, gbase, sq, rstd, lambda ch, n=n: xw[:, ch, 16:16 + n], ["xw"])
                prev_n = n
                A("dve", TT(sA[:, :, 1:m], xw[:, :, 1:m], xw[:, :, 0:m - 1], ALU.add), reads=["xw"], writes=["sA"])
                A("dve", TT(sB[:, 2:8, 3:m], sA[:, 2:8, 3:m], sA[:, 2:8, 1:m - 2], ALU.add), reads=["sA"], writes=["sB"])
                A("dve", TT(sA[:, 4:8, 7:m], sB[:, 4:8, 7:m], sB[:, 4:8, 3:m - 4], ALU.add), reads=["sB"], writes=["sA"])
                A("dve", TT(sB[:, 6:8, 15:m], sA[:, 6:8, 15:m], sA[:, 6:8, 7:m - 8], ALU.add), reads=["sA"], writes=["sB"])
                for g in range(4):
                    src = sA if g in (0, 2) else sB
                    A("dve", STT(dd[:, 2 * g:2 * g + 2, 0:n], src[:, 2 * g:2 * g + 2, 16:16 + n], 1.0 / POOL_W[g], xw[:, 2 * g:2 * g + 2, 16:16 + n], ALU.mult, ALU.subtract),
                      reads=["sA", "sB", "xw"], writes=["dd"])
                    if c0 <= HALO < c0 + n:
                        o = HALO - c0
                        for cc in range(2):
                            ch = 2 * g + cc
                            A("dve", TT(tmp[:, :], src[:, ch, 16 + o:16 + o + 128], invc[:, g, :], ALU.mult), reads=["sA", "sB", "invc"], writes=["tmp"])
                            A("dve", TT(dd[:, ch, o:o + 128], tmp[:, :], xw[:, ch, 16 + o:16 + o + 128], ALU.subtract), reads=["tmp", "xw"], writes=["dd"])
                if last:
                    A("sp", DMA(o_xnT[i].rearrange("p (a b) -> p a b", a=8), xw[:, :, m - 15:m]), reads=["xw"], writes=[("o_xn", i)], dma="outs", batch=None)
                    out_ops.append(("o_xn", i))
                pool_out(c0, n)
            c0, n = OWN1, 128
            norm_group(c0, n, gbase, sq, rstd, lambda ch: ep[:, ch, :, 15:23], ["ep"], view=b16)
            A("dve", TT(eA[:, :, :, 1:23], ep[:, :, :, 1:23], ep[:, :, :, 0:22], ALU.add), reads=["ep"], writes=["eA"])
            A("dve", TT(eB[:, 2:8, :, 3:23], eA[:, 2:8, :, 3:23], eA[:, 2:8, :, 1:21], ALU.add), reads=["eA"], writes=["eB"])
            A("dve", TT(eA[:, 4:8, :, 7:23], eB[:, 4:8, :, 7:23], eB[:, 4:8, :, 3:19], ALU.add), reads=["eB"], writes=["eA"])
            A("dve", TT(eB[:, 6:8, :, 15:23], eA[:, 6:8, :, 15:23], eA[:, 6:8, :, 7:15], ALU.add), reads=["eA"], writes=["eB"])
            for g in range(4):
                src = eA if g in (0, 2) else eB
                for cc in range(2):
                    ch = 2 * g + cc
                    A("dve", STT(b16(dd[:, ch, 0:128]), src[:, ch, :, 15:23], 1.0 / POOL_W[g], ep[:, ch, :, 15:23], ALU.mult, ALU.subtract), reads=["eA", "eB", "ep"], writes=["dd"])
            A("sp", DMA(o_xnT_s[i].rearrange("p (a b t) -> p a b t", a=8, b=16), ep[:, :, :, 15:23]), reads=["ep"], writes=[("o_xns", i)], dma="outs", batch=None)
            out_ops.append(("o_xns", i))
            pool_out(c0, n)

        for L in range(n_layers):
            if L >= 1:
                xs = x[:, :, 0:HALO]
                A("dve", TS(xs, xs, vcol(V_CMASK), ALU.mult), reads=xkeys(0, HALO) + ["vecs"], writes=xkeys(0, HALO))
            if L % 2 == 0:
                mixer_even(L)
            else:
                mixer_odd(L)
            ffn(L)

        for ch in range(NCH):
            A("sp", DMA(yT[ch * 128:(ch + 1) * 128, :], x[:, ch, HALO:TOK]), reads=xkeys(HALO, NOUT, [ch]), writes=[("o_y", ch)], dma="outy")
            out_ops.append(("o_y", ch))
        A("sp", None, reads=list(dict.fromkeys(out_ops)))
        S.emit()
    return nc


_PROG = {}


def _alibi_slopes():
    return np.array([2.0 ** (-8.0 * (h + 1) / 8) for h in range(8)], dtype=np.float64)


def _const_inputs():
    cm = np.zeros((128, 5, 128), np.float32)
    cm[:, 0] = 1.0 / 1024
    cm[:, 1] = 1.0 / 512
    cm[0:64, 2, 0:64] = 1.0 / 64
    cm[64:128, 2, 64:128] = 1.0 / 64
    cm[:, 3] = 1.0
    cm[:, 4] = np.eye(128, dtype=np.float32)
    sl = _alibi_slopes()
    s = np.arange(128)[:, None]
    q = np.arange(128)[None, :]
    em = np.zeros((128, 3, 2, 4, 128), np.float64)
    for kv in range(2):
        for j in range(4):
            h = kv * 4 + j
            dprev = 128 + q - s
            em[:, 0, kv, j] = np.where(dprev < 128, np.exp(-sl[h] * dprev), 0.0)
            dcur = q - s
            em[:, 1, kv, j] = np.where(dcur >= 0, np.exp(-sl[h] * dcur), 0.0)
    em[:, 2] = em[:, 0]
    es = np.zeros((128, 2, 4, 8), np.float64)
    en = np.zeros((8, 2, 4, 8), np.float64)
    ii = np.arange(8)[None, :]
    for kv in range(2):
        for j in range(4):
            h = kv * 4 + j
            d1 = 128 + ii - np.arange(128)[:, None]
            es[:, kv, j] = np.where(d1 < 128, np.exp(-sl[h] * d1), 0.0)
            d2 = ii - np.arange(8)[:, None]
            en[:, kv, j] = np.where(d2 >= 0, np.exp(-sl[h] * d2), 0.0)
    return cm.reshape(128, 640), em.astype(np.float32), es.astype(np.float32).reshape(128, 64), en.astype(np.float32).reshape(8, 64)


def _prep_shared(norm_mix, norm_ffn, w_in, q_norm, k_norm, sinks, w_dw, b_dw, conv_norm_g, conv_norm_b, w_out,
                 w_pool, pool_scale, w_gate, w_up, w_down):
    f = np.float32
    vecs = np.zeros((128, NV), f)
    vecs[:, V_NMIX:V_NMIX + 32] = np.asarray(norm_mix, f).reshape(4, 8, 128).transpose(2, 0, 1).reshape(128, 32)
    vecs[:, V_NFFN:V_NFFN + 32] = np.asarray(norm_ffn, f).reshape(4, 8, 128).transpose(2, 0, 1).reshape(128, 32)
    vecs[:, V_BDW:V_BDW + 8] = np.asarray(b_dw, f).reshape(2, 4, 128).transpose(2, 0, 1).reshape(128, 8)
    vecs[:, V_CNG:V_CNG + 8] = np.asarray(conv_norm_g, f).reshape(2, 4, 128).transpose(2, 0, 1).reshape(128, 8)
    vecs[:, V_CNB:V_CNB + 8] = np.asarray(conv_norm_b, f).reshape(2, 4, 128).transpose(2, 0, 1).reshape(128, 8)
    vecs[:, V_PSC:V_PSC + 16] = np.asarray(pool_scale, f).reshape(2, 8, 128).transpose(2, 0, 1).reshape(128, 16)
    pidx = np.arange(128)
    vecs[:, V_QN:V_QN + 2] = np.asarray(q_norm, f)[:, pidx % 64].T
    vecs[:, V_KN:V_KN + 2] = np.asarray(k_norm, f)[:, pidx % 64].T
    sk = np.asarray(sinks, f)
    for i in range(2):
        for j in range(4):
            vecs[:, V_SINK + 4 * i + j] = sk[i, (pidx // 64) * 4 + j]
    w_in = np.asarray(w_in, f)
    cols = list(range(1024))
    for j in range(4):
        for kv in range(2):
            h = kv * 4 + j
            cols.extend(range(1024 + h * 64, 1024 + (h + 1) * 64))
    cols.extend(range(1536, 1792))
    w_in_p = w_in[:, :, cols]
    w_in_l = np.ascontiguousarray(w_in_p.reshape(2, 8, 128, DIN).transpose(0, 2, 1, 3)).reshape(2, 128, 8 * DIN)
    w_out = np.asarray(w_out, f)
    rows = list(range(512))
    for j in range(4):
        for kv in range(2):
            h = kv * 4 + j
            rows.extend(range(512 + h * 64, 512 + (h + 1) * 64))
    w_out_l = np.ascontiguousarray(w_out[:, rows, :].reshape(2, 8, 128, D).transpose(0, 2, 1, 3)).reshape(2, 128, 8 * D)
    w_dwT = np.ascontiguousarray(np.asarray(w_dw, f).reshape(2, 31, 4, 128).transpose(0, 3, 2, 1)).reshape(2, 128, 4 * 31)
    w_pool_l = np.ascontiguousarray(np.asarray(w_pool, f).reshape(2, 4, 2, 128, 256).transpose(0, 3, 1, 2, 4)).reshape(2, 128, 2048)
    wg = np.asarray(w_gate, f).reshape(4, 8, 128, NFC, 128).transpose(0, 3, 2, 1, 4).reshape(4, NFC, 128, 1024)
    wu = np.asarray(w_up, f).reshape(4, 8, 128, NFC, 128).transpose(0, 3, 2, 1, 4).reshape(4, NFC, 128, 1024)
    wd = np.asarray(w_down, f).reshape(4, NFC, 128, 1024)
    w_ffn = np.ascontiguousarray(np.concatenate([wg, wu, wd], axis=3))
    return dict(vecs=vecs, w_in=w_in_l, w_out=w_out_l, w_dwT=w_dwT, w_pool=w_pool_l, w_ffn=w_ffn)


def kernel(x_prompt, x_sample, cache_conv, cache_k, cache_v, state_pool, norm_mix, norm_ffn, w_in, q_norm, k_norm,
           sinks, w_dw, b_dw, conv_norm_g, conv_norm_b, w_out, w_pool, pool_scale, w_gate, w_up, w_down, _n_layers=4):
    f = np.float32
    x_prompt = np.asarray(x_prompt, f)
    x_sample = np.asarray(x_sample, f)
    cache_conv = np.asarray(cache_conv, f)
    cache_k = np.asarray(cache_k, f).reshape(2, 128, 128, 128)
    cache_v = np.asarray(cache_v, f).reshape(2, 128, 128, 128)
    state_pool = np.asarray(state_pool, f)
    shared = _prep_shared(norm_mix, norm_ffn, w_in, q_norm, k_norm, sinks, w_dw, b_dw, conv_norm_g, conv_norm_b,
                          w_out, w_pool, pool_scale, w_gate, w_up, w_down)
    cmat, emask, esamp, esnew = _const_inputs()
    if _n_layers not in _PROG:
        _PROG[_n_layers] = build_program(_n_layers)
    nc = _PROG[_n_layers]
    in_maps = []
    for c in range(8):
        s, half = c // 2, c % 2
        xt = np.zeros((TOK, D), f)
        t0 = half * 2048
        if half == 1:
            xt[0:HALO] = x_prompt[s, t0 - HALO:t0]
        xt[HALO:OWN1] = x_prompt[s, t0:t0 + 2048]
        xt[OWN1:TOK] = x_sample[16 * c:16 * c + 16].reshape(128, D)
        bs = slice(16 * c, 16 * c + 16)
        vecs = shared["vecs"].copy()
        vecs[:, V_CMASK] = float(half)
        em = emask[:, 0:2]
        invc = np.zeros((128, 4, 128), f)
        for g, w in enumerate(POOL_W):
            if half == 0:
                invc[:, g, :] = 1.0 / np.minimum(w, np.arange(128) + 1)
            else:
                invc[:, g, :] = 1.0 / w
        m = dict(
            xT=np.ascontiguousarray(xt.T), vecs=vecs, cmat=cmat, emask=np.ascontiguousarray(em).reshape(128, 2048), esamp=esamp, esnew=esnew,
            invcnt=invc.reshape(128, 512), w_in=shared["w_in"], w_out=shared["w_out"], w_dwT=shared["w_dwT"],
            w_pool=shared["w_pool"], w_ffn=shared["w_ffn"],
            cconvT=np.ascontiguousarray(cache_conv[:, bs].reshape(2, 16, 30, 4, 128).transpose(0, 4, 3, 1, 2)).reshape(2, 128, 1920),
            ckT=np.ascontiguousarray(cache_k[:, bs].transpose(0, 3, 1, 2)).reshape(2, 128, 2048),
            cv_nat=np.ascontiguousarray(cache_v[:, bs]), ck_nat=np.ascontiguousarray(cache_k[:, bs]),
            cconv_nat=np.ascontiguousarray(cache_conv[:, bs]), cpool_nat=np.ascontiguousarray(state_pool[:, bs]),
            cpoolT=np.ascontiguousarray(state_pool[:, bs].reshape(2, 16, 15, 8, 128).transpose(0, 4, 3, 1, 2)).reshape(2, 128, 1920),
        )
        in_maps.append(m)
    res = run_bass_kernel_spmd(nc, in_maps, core_ids=list(range(8)))
    return _assemble(res.results)


def _assemble(R):
    f = np.float32
    y_prompt = np.zeros((4, 4096, D), f)
    y_sample = np.zeros((128, 8, D), f)
    conv_p = np.zeros((2, 4, 30, 512), f)
    k_p = np.zeros((2, 4, 128, 2, 64), f)
    v_p = np.zeros((2, 4, 128, 2, 64), f)
    pool_p = np.zeros((2, 4, 15, D), f)
    conv_s = np.zeros((2, 128, 30, 512), f)
    k_s = np.zeros((2, 128, 128, 2, 64), f)
    v_s = np.zeros((2, 128, 128, 2, 64), f)
    pool_s = np.zeros((2, 128, 15, D), f)
    for c in range(8):
        r = R[c]
        s, half = c // 2, c % 2
        yt = r["yT"].T
        y_prompt[s, half * 2048:(half + 1) * 2048] = yt[0:2048]
        y_sample[16 * c:16 * c + 16] = yt[2048:].reshape(16, 8, D)
        bs = slice(16 * c, 16 * c + 16)
        if half == 1:
            conv_p[:, s] = r["o_gluT"].reshape(2, 128, 4, 30).transpose(0, 3, 2, 1).reshape(2, 30, 512)
            k_p[:, s] = r["o_kT"].transpose(0, 2, 1).reshape(2, 128, 2, 64)
            v_p[:, s] = r["o_vtok"].reshape(2, 128, 2, 64)
            pool_p[:, s] = r["o_xnT"].reshape(2, 128, 8, 15).transpose(0, 3, 2, 1).reshape(2, 15, D)
        conv_s[:, bs, 0:22] = r["o_convs_old"]
        conv_s[:, bs, 22:30] = r["o_gluT_s"].reshape(2, 128, 4, 16, 8).transpose(0, 3, 4, 2, 1).reshape(2, 16, 8, 512)
        k_s[:, bs, 0:120] = r["o_ks_old"].reshape(2, 16, 120, 2, 64)
        k_s[:, bs, 120:128] = r["o_kT_s"].transpose(0, 2, 1).reshape(2, 16, 8, 2, 64)
        v_s[:, bs, 0:120] = r["o_vs_old"].reshape(2, 16, 120, 2, 64)
        v_s[:, bs, 120:128] = r["o_vtok_s"].reshape(2, 16, 8, 2, 64)
        pool_s[:, bs, 0:7] = r["o_pools_old"]
        pool_s[:, bs, 7:15] = r["o_xnT_s"].reshape(2, 128, 8, 16, 8).transpose(0, 3, 4, 2, 1).reshape(2, 16, 8, D)
    return (y_prompt, y_sample, conv_p, k_p, v_p, pool_p, conv_s, k_s, v_s, pool_s)
```

```python
import numpy as np
from contextlib import ExitStack
import concourse.bass as bass
import concourse.mybir as mybir
from concourse.bass_utils import run_bass_kernel_spmd

F32 = mybir.dt.float32
BF16 = mybir.dt.bfloat16
AF = mybir.ActivationFunctionType
ALU = mybir.AluOpType

ENGS = ("pe", "act", "dve", "pool", "sp")

D = 1024
NCH = 8
DFF = 2816
NFC = 22
DIN = 1792
NBLK = 20
TOK = NBLK * 128
HALO = 384
OWN1 = 2432
NOUT = TOK - HALO
RMS_EPS = 1e-6
LN_EPS = 1e-5
POOL_W = (2, 4, 8, 16)
SBUF_BASE = 16512
SBUF_LIMIT = 229312

V_NMIX = 0
V_NFFN = 32
V_BDW = 64
V_CNG = 72
V_CNB = 80
V_PSC = 88
V_QN = 104
V_KN = 106
V_SINK = 108
V_CMASK = 116
NV = 120


class Op:
    __slots__ = ("eng", "fn", "deps", "signal", "tsem", "tick", "dma", "batch", "known", "waits", "idx")


class Sched:
    def __init__(self, nc, stack):
        self.nc = nc
        self.stack = stack
        self.all = []
        self.last_w = {}
        self.readers = {}
        self.esem = {e: stack.enter_context(nc.semaphore("s_" + e)) for e in ENGS}
        self.dsem = {}
        self.dcount = {}
        self.sems = dict(self.esem)
        self.phase_deps = []
        self._cap = None
        self.nalloc = 0

    PERSIST = ("x", "vecs", "cmat", "identf", "onesf", "small", "P")

    def _persistent(self, k):
        b = self._bname(k)
        return b in self.PERSIST or b.startswith("o_")

    def new_phase(self):
        retired = {}
        for k in [k for k in self.last_w if not self._persistent(k)]:
            w = self.last_w.pop(k)
            retired[w.idx] = w
        for k in [k for k in self.readers if not self._persistent(k)]:
            for r in self.readers.pop(k).values():
                retired[r.idx] = r
        for r in self.phase_deps:
            retired[r.idx] = r
        keep = {}
        for r in retired.values():
            key = r.eng if r.dma is None else ("dma", r.idx)
            if key not in keep or keep[key].idx < r.idx:
                keep[key] = r
        self.phase_deps = list(keep.values())

    def buf(self, name, shape, dtype, off):
        size = int(np.prod(shape[1:])) * (2 if dtype == BF16 else 4)
        assert off + size <= SBUF_LIMIT, (name, off, size)
        self.nalloc += 1
        return self.nc.alloc_sbuf_tensor_at("%s_%d" % (name, self.nalloc), list(shape), dtype, offset=off)

    @staticmethod
    def _bname(k):
        return k if isinstance(k, str) else k[0]

    def capture(self, f):
        self._cap = []
        f()
        lst, self._cap = self._cap, None
        return lst

    def replay_merged(self, la, lb):
        i = j = 0
        while i < len(la) or j < len(lb):
            if j >= len(lb) or (i < len(la) and i * len(lb) <= j * len(la)):
                self.add(*la[i])
                i += 1
            else:
                self.add(*lb[j])
                j += 1

    def add(self, eng, fn, reads=(), writes=(), dma=None, batch=0):
        if self._cap is not None:
            self._cap.append((eng, fn, list(reads), list(writes), dma, batch))
            return None
        op = Op()
        op.eng, op.fn, op.signal, op.dma, op.batch = eng, fn, False, dma, batch
        op.idx = len(self.all)
        deps = {}
        for k in reads:
            w = self.last_w.get(k)
            if w is not None:
                deps[w.idx] = w
        for k in writes:
            w = self.last_w.get(k)
            if w is not None:
                deps[w.idx] = w
            rd = self.readers.get(k)
            if rd:
                for r in rd.values():
                    deps[r.idx] = r
        if self.phase_deps:
            for k in list(reads) + list(writes):
                if not self._persistent(k):
                    for r in self.phase_deps:
                        deps[r.idx] = r
                    break
        op.deps = list(deps.values())
        if fn is not None:
            for k in reads:
                rd = self.readers.setdefault(k, {})
                rd[eng if dma is None else ("dma", op.idx)] = op
            for k in writes:
                self.last_w[k] = op
                self.readers[k] = {}
        if dma is not None:
            if dma not in self.dsem:
                h = self.stack.enter_context(self.nc.semaphore("d_" + dma))
                self.dsem[dma] = h
                self.sems[dma] = h
                self.dcount[dma] = {}
            c = self.dcount[dma]
            if batch is None:
                batch = op.batch = len(c) + 1
            c[batch] = c.get(batch, 0) + 1
        for p in op.deps:
            if p.dma is None and not (p.eng == "pe" and eng == "pe" and dma is None):
                p.signal = True
        self.all.append(op)
        return op

    def finalize(self):
        count = {e: 0 for e in ENGS}
        seen = {e: {} for e in ENGS}
        bend = {}
        for name, c in self.dcount.items():
            tot = 0
            bend[name] = {}
            for b in sorted(c):
                tot += 16 * c[b]
                bend[name][b] = tot
        started = {}
        self.per_eng = {e: [] for e in ENGS}
        for op in self.all:
            E = op.eng
            sE = seen[E]
            waits = {}
            for p in sorted(op.deps, key=lambda p: -p.idx):
                if p.dma is None and p.eng == "pe" and E == "pe" and op.dma is None:
                    continue
                if p.dma is not None and p.dma == op.dma and p.batch == op.batch:
                    raise AssertionError("intra-batch DMA dependency on " + str(p.dma))
                if sE.get(p.tsem, 0) >= p.tick:
                    continue
                waits[p.tsem] = max(waits.get(p.tsem, 0), p.tick)
                for k, v in p.known.items():
                    if sE.get(k, 0) < v:
                        sE[k] = v
            if op.dma is not None:
                name = op.dma
                prev = started.get(name)
                if prev is not None and prev != op.batch:
                    assert op.batch > prev, (name, prev, op.batch)
                    v = bend[name][prev]
                    if sE.get(name, 0) < v:
                        waits[name] = max(waits.get(name, 0), v)
                        sE[name] = v
                started[name] = op.batch
                op.tsem, op.tick = name, bend[name][op.batch]
            elif op.signal:
                count[E] += 1
                op.tsem, op.tick = E, count[E]
            else:
                op.tsem, op.tick = E, count[E] + 1
            op.waits = list(waits.items())
            kn = dict(sE)
            if op.dma is not None or op.signal:
                kn[op.tsem] = max(kn.get(op.tsem, 0), op.tick)
            op.known = kn
            self.per_eng[E].append(op)

    def emit_engine(self, name, eng):
        for op in self.per_eng[name]:
            for (k, v) in op.waits:
                eng.wait_ge(self.sems[k], v)
            if op.fn is None:
                continue
            inst = op.fn(eng)
            if op.dma is not None:
                inst.then_inc(self.dsem[op.dma], 16)
            elif op.signal:
                inst.then_inc(self.esem[name], 1)

    def emit(self):
        self.finalize()
        with self.nc.Block() as block:
            @block.tensor
            def _(e):
                self.emit_engine("pe", e)

            @block.scalar
            def _(e):
                self.emit_engine("act", e)

            @block.vector
            def _(e):
                self.emit_engine("dve", e)

            @block.gpsimd
            def _(e):
                self.emit_engine("pool", e)

            @block.sync
            def _(e):
                self.emit_engine("sp", e)


class Arena:
    def __init__(self, base, limit=SBUF_LIMIT):
        self.p = base
        self.limit = limit

    def take(self, shape, dtype):
        size = int(np.prod(shape[1:])) * (2 if dtype == BF16 else 4)
        off = self.p
        self.p = (off + size + 63) // 64 * 64
        assert self.p <= self.limit, ("arena overflow", self.p)
        return off


def xkeys(c0, n, chs=range(NCH)):
    return [("x", ch, b) for ch in chs for b in range(c0 // 128, (c0 + n + 127) // 128)]


def MM(out, lhsT, rhs, start=True, stop=True):
    return lambda e: e.matmul(out, lhsT=lhsT, rhs=rhs, start=start, stop=stop)


def TR(out, in_, identity):
    return lambda e: e.transpose(out=out, in_=in_, identity=identity)


def ACT(out, in_, func, bias=None, scale=None):
    kw = {}
    if bias is not None:
        kw["bias"] = bias
    if scale is not None:
        kw["scale"] = scale
    return lambda e: e.activation(out=out, in_=in_, func=func, **kw)


def ACOPY(out, in_):
    return lambda e: e.copy(out=out, in_=in_)


def TT(out, in0, in1, op):
    return lambda e: e.tensor_tensor(out=out, in0=in0, in1=in1, op=op)


def STT(out, in0, scalar, in1, op0, op1):
    return lambda e: e.scalar_tensor_tensor(out=out, in0=in0, scalar=scalar, in1=in1, op0=op0, op1=op1)


def TS(out, in0, scalar1, op0, scalar2=None, op1=None):
    if op1 is None:
        return lambda e: e.tensor_scalar(out=out, in0=in0, scalar1=scalar1, scalar2=None, op0=op0)
    return lambda e: e.tensor_scalar(out=out, in0=in0, scalar1=scalar1, scalar2=scalar2, op0=op0, op1=op1)


def TCOPY(out, in_):
    return lambda e: e.tensor_copy(out=out, in_=in_)


def RECIP(out, in_):
    return lambda e: e.reciprocal(out=out, in_=in_)


def MSET(ap, v):
    return lambda e: e.memset(ap, v)


def TRED(out, in_, op):
    return lambda e: e.tensor_reduce(out=out, in_=in_, axis=mybir.AxisListType.X, op=op)


def DMA(out, in_):
    return lambda e: e.dma_start(out=out, in_=in_)


def build_program(n_layers=4):
    nc = bass.Bass("TRN2", target_bir_lowering=False)

    def din(name, shape):
        return nc.dram_tensor(name, list(shape), F32, kind="ExternalInput").ap()

    def dout(name, shape):
        return nc.dram_tensor(name, list(shape), F32, kind="ExternalOutput").ap()

    xT = din("xT", [D, TOK])
    vecs_d = din("vecs", [128, NV])
    cmat_d = din("cmat", [128, 5 * 128])
    emask_d = din("emask", [128, 2 * 2 * 512])
    esamp_d = din("esamp", [128, 2 * 32])
    esnew_d = din("esnew", [8, 2 * 32])
    invcnt_d = din("invcnt", [128, 4 * 128])
    w_in_d = din("w_in", [2, 128, NCH * DIN])
    w_out_d = din("w_out", [2, 128, NCH * D])
    w_dw_d = din("w_dwT", [2, 128, 4 * 31])
    w_pool_d = din("w_pool", [2, 128, 4 * 2 * 256])
    w_ffn_d = din("w_ffn", [4, NFC, 128, 3072])
    cconvT_d = din("cconvT", [2, 128, 4 * 16 * 30])
    ckT_d = din("ckT", [2, 128, 16 * 128])
    cv_d = din("cv_nat", [2, 16, 128, 128])
    ck_d = din("ck_nat", [2, 16, 128, 128])
    cconv_d = din("cconv_nat", [2, 16, 30, 512])
    cpool_d = din("cpool_nat", [2, 16, 15, 1024])
    cpoolT_d = din("cpoolT", [2, 128, 8 * 16 * 15])

    yT = dout("yT", [D, NOUT])
    o_gluT = dout("o_gluT", [2, 128, 4 * 30])
    o_kT = dout("o_kT", [2, 128, 128])
    o_vtok = dout("o_vtok", [2, 128, 128])
    o_xnT = dout("o_xnT", [2, 128, 8 * 15])
    o_convs_old = dout("o_convs_old", [2, 16, 22, 512])
    o_gluT_s = dout("o_gluT_s", [2, 128, 4 * 128])
    o_ks_old = dout("o_ks_old", [2, 16, 120, 128])
    o_kT_s = dout("o_kT_s", [2, 128, 128])
    o_vs_old = dout("o_vs_old", [2, 16, 120, 128])
    o_vtok_s = dout("o_vtok_s", [2, 128, 128])
    o_pools_old = dout("o_pools_old", [2, 16, 7, 1024])
    o_xnT_s = dout("o_xnT_s", [2, 128, 8 * 128])

    with ExitStack() as st:
        S = Sched(nc, st)
        A = S.add
        P = [nc.alloc_psum_tensor("P%d" % i, [128, 512], F32) for i in range(8)]

        def pk(i):
            return ("P", i)

        x = S.buf("x", [128, NCH, TOK], F32, SBUF_BASE)
        CB = SBUF_BASE + 81920
        vecs = S.buf("vecs", [128, NV], F32, CB)
        cmat = S.buf("cmat", [128, 5, 128], BF16, CB + 512)
        identf = S.buf("identf", [128, 128], F32, CB + 512 + 1280)
        onesf = S.buf("onesf", [128, 128], F32, CB + 512 + 1280 + 512)
        small = S.buf("small", [128, 16], F32, CB + 512 + 1280 + 1024)
        ABASE = CB + 512 + 1280 + 1024 + 64
        ONES_RMS, ONES_LN, BDIAG, ONES1, IDENT = range(5)
        out_ops = []

        def vcol(c):
            return vecs[:, c:c + 1]

        def b16(ap):
            return ap.rearrange("p (b t) -> p b t", b=16)

        def two(ap):
            return ap.rearrange("p (a c) -> p a c", a=2)

        for ch in range(NCH):
            A("sp", DMA(x[:, ch, :], xT[ch * 128:(ch + 1) * 128, :]), writes=xkeys(0, TOK, [ch]), dma="xin")
        A("sp", DMA(vecs[:], vecs_d[:, :]), writes=["vecs"], dma="cst")
        A("pool", DMA(cmat[:].rearrange("p a b -> p (a b)"), cmat_d[:, :]), writes=["cmat"], dma="cst2")
        A("sp", DMA(identf[:], cmat_d[:, 4 * 128:5 * 128]), writes=["identf"], dma="cst")
        A("sp", DMA(onesf[:], cmat_d[:, 3 * 128:4 * 128]), writes=["onesf"], dma="cst")

        for i in range(2):
            A("sp", DMA(o_convs_old[i], cconv_d[i, :, 8:30, :]), writes=[("o_shift", 0, i)], dma="shift")
            A("sp", DMA(o_ks_old[i], ck_d[i, :, 8:128, :]), writes=[("o_shift", 1, i)], dma="shift")
            A("sp", DMA(o_vs_old[i], cv_d[i, :, 8:128, :]), writes=[("o_shift", 2, i)], dma="shift")
            A("sp", DMA(o_pools_old[i], cpool_d[i, :, 8:15, :]), writes=[("o_shift", 3, i)], dma="shift")
            out_ops.extend([("o_shift", q, i) for q in range(4)])

        def rstd_from(ps_ap, dst_ap, eps, rkeys, wkeys):
            A("act", ACT(dst_ap, ps_ap, AF.Ln, bias=epsap[eps], scale=1.0), reads=list(rkeys) + ["small"], writes=wkeys)
            A("act", ACT(dst_ap, dst_ap, AF.Exp, scale=-0.5), reads=wkeys, writes=wkeys)

        A("dve", MSET(small[:, 8:9], RMS_EPS), writes=["small"])
        A("dve", MSET(small[:, 9:10], LN_EPS), writes=["small"])
        epsap = {RMS_EPS: small[:, 8:9], LN_EPS: small[:, 9:10]}

        def norm_group(c0, n, gbase, sq, rstd, dst_fn, dst_keys, view=None, bank=4, sqk="sq", rk="rstd"):
            A("act", ACT(sq[:, :, 0:n], x[:, :, c0:c0 + n], AF.Square), reads=xkeys(c0, n), writes=[sqk])
            for ch in range(NCH):
                A("pe", MM(P[bank][:, 0:n], cmat[:, ONES_RMS, :], sq[:, ch, 0:n], ch == 0, ch == NCH - 1), reads=[sqk, "cmat"], writes=[pk(bank)])
            rstd_from(P[bank][:, 0:n], rstd[:, 0:n], RMS_EPS, [pk(bank)], [rk])
            for ch in range(NCH):
                i0, i1 = x[:, ch, c0:c0 + n], rstd[:, 0:n]
                if view is not None:
                    i0, i1 = view(i0), view(i1)
                A("dve", STT(dst_fn(ch), i0, vcol(gbase + ch), i1, ALU.mult, ALU.mult),
                  reads=xkeys(c0, n, [ch]) + [rk, "vecs"], writes=dst_keys)

        def yps(dc, n):
            return P[dc // 2][:, (dc % 2) * 256:(dc % 2) * 256 + n]

        def add_y_to_x(c0, n):
            for b in range(4):
                xs = x[:, 2 * b:2 * b + 2, c0:c0 + n]
                A("dve", TT(xs, two(P[b][:, :])[:, :, 0:n], xs, ALU.add),
                  reads=[pk(b)] + xkeys(c0, n, [2 * b, 2 * b + 1]), writes=xkeys(c0, n, [2 * b, 2 * b + 1]))

        def split_groups(c_start, c_end):
            gs = []
            c = c_end
            while c > c_start:
                n = min(256, c - c_start)
                gs.append((c - n, n))
                c -= n
            return gs[::-1]

        def ffn(L):
            start_blk = (1, 1, 2, 3)[L]
            groups = split_groups(start_blk * 128, TOK)
            halves = [groups[:5], groups[5:]]
            S.new_phase()
            ar = Arena(ABASE)
            h = S.buf("h", [128, NCH, TOK], BF16, ar.take([128, NCH, TOK], BF16))
            wp = [S.buf("wp%d" % s, [128, 4, 3072], BF16, ar.take([128, 4, 3072], BF16)) for s in range(2)]
            sg = [S.buf("sg%d" % s, [128, 256], F32, ar.take([128, 256], F32)) for s in range(2)]
            abuf = [S.buf("a%d" % s, [128, 4, 256], BF16, ar.take([128, 4, 256], BF16)) for s in range(2)]
            sqs = [S.buf("sq%d" % s, [128, NCH, 256], BF16, ar.take([128, NCH, 256], BF16)) for s in range(2)]
            rstds = [S.buf("rstd%d" % s, [128, 256], F32, ar.take([128, 256], F32)) for s in range(2)]
            pieces = [(f, min(4, NFC - f)) for f in range(0, NFC, 4)]
            gbase = V_NFFN + L * 8
            bcount = [0]

            def load_piece(pi, slot):
                f0, nf = pieces[pi]
                bcount[0] += 1
                A("pool", DMA(wp[slot][:, 0:nf, :], w_ffn_d[L, f0:f0 + nf].rearrange("f p n -> p f n")),
                  writes=["wp%d" % slot], dma="wffn%d_%d" % (L, slot), batch=bcount[0])

            def norm_h(c0, n, bank, st):
                norm_group(c0, n, gbase, sqs[st], rstds[st], lambda ch: h[:, ch, c0:c0 + n], [("h", c0)], bank=bank, sqk="sq%d" % st, rk="rstd%d" % st)

            def emit_Y(slot, nf, c0, n, aslot):
                for dc in range(NCH):
                    for fl in range(nf):
                        A("pe", MM(yps(dc, n), wp[slot][:, fl, 2048 + dc * 128:2048 + (dc + 1) * 128], abuf[aslot][:, fl, 0:n], fl == 0, fl == nf - 1),
                          reads=["wp%d" % slot, "a%d" % aslot], writes=[pk(dc // 2)])
                add_y_to_x(c0, n)

            cnt = 0
            for hi, half in enumerate(halves):
                if not half:
                    continue
                other = halves[1] if hi == 0 else []
                load_piece(0, 0)
                if hi == 0:
                    norm_h(half[0][0], half[0][1], 3, 0)
                for pi, (f0, nf) in enumerate(pieces):
                    slot = pi % 2
                    if pi + 1 < len(pieces):
                        load_piece(pi + 1, 1 - slot)
                    wk = "wp%d" % slot
                    prev = None
                    for gi, (c0, n) in enumerate(half):
                        if hi == 0 and pi == 0 and gi + 1 < len(half):
                            norm_h(half[gi + 1][0], half[gi + 1][1], 3, (gi + 1) % 2)
                        if hi == 0 and pi == len(pieces) - 1 and gi < len(other):
                            norm_h(other[gi][0], other[gi][1], 7, gi % 2)
                        aslot = gi % 2
                        for fl in range(nf):
                            bank = 4 + fl
                            for kc in range(NCH):
                                A("pe", MM(P[bank][:, 0:n], wp[slot][:, fl, kc * 128:(kc + 1) * 128], h[:, kc, c0:c0 + n], kc == 0, kc == NCH - 1),
                                  reads=[wk, ("h", c0)], writes=[pk(bank)])
                            for kc in range(NCH):
                                A("pe", MM(P[bank][:, 256:256 + n], wp[slot][:, fl, 1024 + kc * 128:1024 + (kc + 1) * 128], h[:, kc, c0:c0 + n], kc == 0, kc == NCH - 1),
                                  reads=[wk, ("h", c0)], writes=[pk(bank)])
                            ss = cnt % 2
                            cnt += 1
                            A("act", ACT(sg[ss][:, 0:n], P[bank][:, 0:n], AF.Silu), reads=[pk(bank)], writes=["sg%d" % ss])
                            A("dve", TT(abuf[aslot][:, fl, 0:n], sg[ss][:, 0:n], P[bank][:, 256:256 + n], ALU.mult),
                              reads=["sg%d" % ss, pk(bank)], writes=["a%d" % aslot])
                        if prev is not None:
                            emit_Y(*prev)
                        prev = (slot, nf, c0, n, aslot)
                    emit_Y(*prev)

        def mixer_even(L):
            i = L // 2
            start_blk = (0, 0, 1, 0)[L]
            S.new_phase()
            ar = Arena(ABASE)

            def mk(name, shape, dt):
                return S.buf(name, shape, dt, ar.take(shape, dt))

            w_in = mk("w_in", [128, NCH, DIN], BF16)
            w_out = mk("w_out", [128, NCH, D], BF16)
            diag = mk("diag", [128, 4, 31, 128], BF16)
            wdw = mk("wdw", [128, 4, 32], F32)
            xn = mk("xn", [128, NCH, 256], BF16)
            sq = mk("sq", [128, NCH, 256], BF16)
            rstd = mk("rstd", [128, 512], F32)
            tmpA = mk("tmpA", [128, 4, 256], F32)
            sqq = mk("sqq", [128, 2, 256], BF16)
            rstdq = mk("rstdq", [128, 512], F32)
            denr = rstdq
            pexp2 = mk("pexp", [128, 2, 512], F32)
            pexp = pexp2[:, 0, :]
            pTb = mk("pT", [128, 2, 512], BF16)
            rhso = mk("rhso", [128, NCH, 256], BF16)
            qT = mk("qT", [128, 4, 256], BF16)
            o32 = mk("o32", [128, 4, 128], F32)
            k32 = mk("k32", [128, 128], F32)
            v32 = mk("v32", [128, 128], F32)
            esamp = mk("esamp", [128, 2, 32], F32)
            esnew = mk("esnew", [8, 2, 32], F32)
            kTs = mk("kTs", [128, 128], BF16)
            vnew = mk("vnew", [8, 16, 128], BF16)
            rbase = ar.p
            emask = mk("emask", [128, 2, 2, 512], F32)
            kT = mk("kT", [128, 5 * 128], BF16)
            vall = mk("vall", [128, 5, 128], BF16)
            glu = mk("glu", [128, 4, 288], BF16)
            rend = ar.p
            ar.p = rbase
            ext = mk("ext", [128, 4, 16, 38], BF16)
            ckT = mk("ckT", [128, 16, 128], BF16)
            cv = mk("cv", [128, 16, 128], BF16)
            assert ar.p <= rend, (ar.p, rend)
            meanb = two(rstd[:, :])
            vbf = sq[:, 0:4, :]
            sq2 = sq[:, 4:8, :]
            alias_keys = ["emask", "glu"] + [("kT", b) for b in range(-1, 19)] + [("vall", b) for b in range(-1, 19)]

            def kslot(b):
                return (b + 1) % 5

            wsem, csem = "wmix%d" % L, "cmix%d" % L
            A("pool", DMA(w_in[:].rearrange("p a b -> p (a b)"), w_in_d[i]), writes=["w_in"], dma=wsem)
            A("pool", DMA(w_out[:].rearrange("p a b -> p (a b)"), w_out_d[i]), writes=["w_out"], dma=wsem)
            A("sp", DMA(wdw[:, :, 0:31], w_dw_d[i].rearrange("p (a b) -> p a b", a=4)), writes=["wdw"], dma=csem)
            A("sp", DMA(emask[:].rearrange("p a b c -> p (a b c)"), emask_d[:, :]), writes=["emask"], dma=csem)
            A("sp", DMA(esamp[:].rearrange("p a b -> p (a b)"), esamp_d[:, :]), writes=["esamp"], dma=csem)
            A("sp", DMA(esnew[:].rearrange("p a b -> p (a b)"), esnew_d[:, :]), writes=["esnew"], dma=csem)
            for ch in range(4):
                for j in range(31):
                    A("dve", TS(diag[:, ch, j, :], cmat[:, IDENT, :], wdw[:, ch, j:j + 1], ALU.mult), reads=["cmat", "wdw"], writes=[("diag", ch, j)])
            A("dve", MSET(kT[:, :], 0.0), writes=[("kT", b) for b in range(-1, 19)])
            A("dve", MSET(vall[:].rearrange("p a b -> p (a b)"), 0.0), writes=[("vall", b) for b in range(-1, 19)])
            A("dve", MSET(glu[:].rearrange("p a b -> p (a b)"), 0.0), writes=["glu"])
            qn, kn = vcol(V_QN + i), vcol(V_KN + i)
            A("dve", TT(small[:, 0:1], qn, kn, ALU.mult), reads=["vecs"], writes=["small"])
            A("dve", TS(small[:, 1:2], small[:, 0:1], -1.0, ALU.mult), reads=["small"], writes=["small"])
            A("dve", TT(small[:, 0:1], small[:, 0:1], small[:, 1:2], ALU.max), reads=["small"], writes=["small"])
            A("pe", TR(P[7][0:1, 0:128], small[:, 0:1], identf[:]), reads=["small", "identf"], writes=[pk(7)])
            A("dve", TRED(small[0:1, 2:3], P[7][0:1, 0:128], ALU.max), reads=[pk(7)], writes=["small"])
            A("pe", MM(P[7][:, 128:129], onesf[0:1, :], small[0:1, 2:3]), reads=["small", "onesf"], writes=[pk(7)])
            A("dve", TS(small[:, 3:4], P[7][:, 128:129], -8.0, ALU.mult), reads=[pk(7)], writes=["small"])
            negM = small[:, 3:4]
            A("act", ACT(small[:, 4:8], vecs[:, V_SINK + 4 * i:V_SINK + 4 * i + 4], AF.Exp, bias=negM, scale=1.0), reads=["small", "vecs"], writes=["small"])
            sinkexp = small[:, 4:8]

            gbase = V_NMIX + L * 8
            pgroups = split_groups(start_blk * 128, OWN1)
            allgroups = [(c0, n, False) for (c0, n) in pgroups] + [(OWN1, 128, True)]
            spi = [0]

            def abank():
                b = 6 + (spi[0] % 2)
                spi[0] += 1
                return b

            def proj(dst_ps, col0, n, keyw):
                for kc in range(NCH):
                    A("pe", MM(dst_ps, w_in[:, kc, col0:col0 + 128], xn[:, kc, 0:n], kc == 0, kc == NCH - 1), reads=["w_in", "xn"], writes=[keyw])

            def chain_conv(c0, n, samp, last):
                for ch in range(4):
                    bank = abank()
                    proj(P[bank][:, 0:n], ch * 128, n, pk(bank))
                    proj(P[bank][:, 256:256 + n], 512 + ch * 128, n, pk(bank))
                    A("act", ACT(tmpA[:, ch, 0:n], P[bank][:, 256:256 + n], AF.Sigmoid), reads=[pk(bank)], writes=[("tmpA", ch)])
                    if samp:
                        A("dve", TT(ext[:, ch, :, 30:38], b16(P[bank][:, 0:128]), b16(tmpA[:, ch, 0:128]), ALU.mult), reads=[pk(bank), ("tmpA", ch)], writes=["ext"])
                    else:
                        A("dve", TT(glu[:, ch, 32:32 + n], P[bank][:, 0:n], tmpA[:, ch, 0:n], ALU.mult), reads=[pk(bank), ("tmpA", ch)], writes=["glu"])
                    if samp or last:
                        n0 = n - 128
                        A("dve", TT(o32[:, ch, :], P[bank][:, n0:n0 + 128], tmpA[:, ch, n0:n0 + 128], ALU.mult), reads=[pk(bank), ("tmpA", ch)], writes=["o32"])
                if samp:
                    A("sp", DMA(o_gluT_s[i], o32[:].rearrange("p a b -> p (a b)")), reads=["o32"], writes=[("o_glus", i)], dma="outs", batch=None)
                    out_ops.append(("o_glus", i))
                elif last:
                    A("sp", DMA(o_gluT[i].rearrange("p (a b) -> p a b", a=4), o32[:, :, 98:128]), reads=["o32"], writes=[("o_glu", i)], dma="outs", batch=None)
                    out_ops.append(("o_glu", i))
                for ch in range(4):
                    cps = P[ch // 2][:, (ch % 2) * 256:(ch % 2) * 256 + n]
                    for j in range(31):
                        rhs = ext[:, ch, :, j:j + 8] if samp else glu[:, ch, 2 + j:2 + j + n]
                        A("pe", MM(cps, diag[:, ch, j, :], rhs, j == 0, j == 30), reads=[("diag", ch, j), "ext" if samp else "glu"], writes=[pk(ch // 2)])
                    bcol = vcol(V_BDW + 4 * i + ch)
                    A("act", ACT(vbf[:, ch, 0:n], cps, AF.Identity, bias=bcol, scale=1.0), reads=[pk(ch // 2), "vecs"], writes=["sq"])
                    A("act", ACT(sq2[:, ch, 0:n], cps, AF.Square, bias=bcol, scale=1.0), reads=[pk(ch // 2), "vecs"], writes=["sq"])
                if not samp:
                    A("pool", TCOPY(glu[:, :, 0:32], glu[:, :, n:n + 32]), reads=["glu"], writes=["glu"])
                for ch in range(4):
                    A("pe", MM(P[7][:, 0:n], cmat[:, ONES_LN, :], vbf[:, ch, 0:n], ch == 0, ch == 3), reads=["sq", "cmat"], writes=[pk(7)])
                for ch in range(4):
                    A("pe", MM(P[7][:, 256:256 + n], cmat[:, ONES_LN, :], sq2[:, ch, 0:n], ch == 0, ch == 3), reads=["sq", "cmat"], writes=[pk(7)])
                A("act", ACOPY(meanb[:, 0, 0:n], P[7][:, 0:n]), reads=[pk(7)], writes=["rstd"])
                A("dve", STT(meanb[:, 1, 0:n], meanb[:, 0, 0:n], -1.0, meanb[:, 0, 0:n], ALU.mult, ALU.mult), reads=["rstd"], writes=["rstd"])
                A("dve", TT(meanb[:, 1, 0:n], P[7][:, 256:256 + n], meanb[:, 1, 0:n], ALU.add), reads=[pk(7), "rstd"], writes=["rstd"])
                rstd_from(meanb[:, 1, 0:n], meanb[:, 1, 0:n], LN_EPS, ["rstd"], ["rstd"])
                for ch in range(4):
                    cps = P[ch // 2][:, (ch % 2) * 256:(ch % 2) * 256 + n]
                    bcol = vcol(V_BDW + 4 * i + ch)
                    A("dve", STT(tmpA[:, ch, 0:n], cps, bcol, meanb[:, 0, 0:n], ALU.add, ALU.subtract), reads=[pk(ch // 2), "vecs", "rstd"], writes=[("tmpA", ch)])
                    A("dve", TT(tmpA[:, ch, 0:n], tmpA[:, ch, 0:n], meanb[:, 1, 0:n], ALU.mult), reads=[("tmpA", ch), "rstd"], writes=[("tmpA", ch)])
                    A("act", ACT(rhso[:, ch, 0:n], tmpA[:, ch, 0:n], AF.Silu, bias=vcol(V_CNB + 4 * i + ch), scale=vcol(V_CNG + 4 * i + ch)),
                      reads=[("tmpA", ch), "vecs"], writes=[("rhso", ch)])

            def chain_attn(c0, n, samp, last):
                nb = n // 128
                for pair in range(2):
                    bank = 5
                    for jj in range(2):
                        proj(P[bank][:, jj * 256:jj * 256 + n], 1024 + (pair * 2 + jj) * 128, n, pk(bank))
                    qps = two(P[bank][:, :])[:, :, 0:n]
                    A("act", ACT(sqq[:, 0:2, 0:n], qps, AF.Square), reads=[pk(bank)], writes=["sqq"])
                    for jj in range(2):
                        A("pe", MM(P[4][:, jj * 256:jj * 256 + n], cmat[:, BDIAG, :], sqq[:, jj, 0:n]), reads=["sqq", "cmat"], writes=[pk(4)])
                    r3 = two(rstdq[:, :])[:, :, 0:n]
                    rstd_from(two(P[4][:, :])[:, :, 0:n], r3, RMS_EPS, [pk(4)], ["rstdq"])
                    A("dve", STT(qT[:, 2 * pair:2 * pair + 2, 0:n], qps, qn, r3, ALU.mult, ALU.mult), reads=[pk(bank), "rstdq", "vecs"], writes=["qT"])
                bank = 5
                proj(P[bank][:, 0:n], 1536, n, pk(bank))
                A("act", ACT(sqq[:, 0, 0:n], P[bank][:, 0:n], AF.Square), reads=[pk(bank)], writes=["sqq"])
                A("pe", MM(P[4][:, 0:n], cmat[:, BDIAG, :], sqq[:, 0, 0:n]), reads=["sqq", "cmat"], writes=[pk(4)])
                rstd_from(P[4][:, 0:n], rstdq[:, 0:n], RMS_EPS, [pk(4)], ["rstdq"])
                if samp:
                    A("dve", STT(kTs[:, 0:128], P[bank][:, 0:n], kn, rstdq[:, 0:n], ALU.mult, ALU.mult), reads=[pk(bank), "rstdq", "vecs"], writes=["kTs"])
                else:
                    for bl in range(nb):
                        blk = c0 // 128 + bl
                        sl = kslot(blk)
                        A("dve", STT(kT[:, sl * 128:(sl + 1) * 128], P[bank][:, bl * 128:(bl + 1) * 128], kn, rstdq[:, bl * 128:(bl + 1) * 128], ALU.mult, ALU.mult),
                          reads=[pk(bank), "rstdq", "vecs"], writes=[("kT", blk), ("kT", blk - 5)])
                if samp or last:
                    n0 = n - 128
                    A("dve", STT(k32[:, :], P[bank][:, n0:n0 + 128], kn, rstdq[:, n0:n0 + 128], ALU.mult, ALU.mult), reads=[pk(bank), "rstdq", "vecs"], writes=["k32"])
                    okk = ("o_k", samp, i)
                    A("sp", DMA((o_kT_s if samp else o_kT)[i], k32[:, :]), reads=["k32"], writes=[okk], dma="outs", batch=None)
                    out_ops.append(okk)
                for bl in range(nb):
                    blk = c0 // 128 + bl
                    bank = 5
                    for kc in range(NCH):
                        A("pe", MM(P[bank][:, 0:128], xn[:, kc, bl * 128:(bl + 1) * 128], w_in[:, kc, 1664:1792], kc == 0, kc == NCH - 1), reads=["w_in", "xn"], writes=[pk(bank)])
                    if not samp:
                        A("act", ACOPY(vall[:, kslot(blk), :], P[bank][:, 0:128]), reads=[pk(bank)], writes=[("vall", blk), ("vall", blk - 5)])
                    if samp or (last and bl == nb - 1):
                        A("act", ACOPY(v32[:, :], P[bank][:, 0:128]), reads=[pk(bank)], writes=["v32"])
                        ovk = ("o_v", samp, i)
                        A("sp", DMA((o_vtok_s if samp else o_vtok)[i], v32[:, :]), reads=["v32"], writes=[ovk], dma="outs", batch=None)
                        out_ops.append(ovk)
                if samp:
                    for r in range(4):
                        bank = 5
                        for bb in range(4):
                            b = r * 4 + bb
                            for kc in range(NCH):
                                A("pe", MM(P[bank][0:8, bb * 128:(bb + 1) * 128], xn[:, kc, b * 8:(b + 1) * 8], w_in[:, kc, 1664:1792], kc == 0, kc == NCH - 1), reads=["w_in", "xn"], writes=[pk(bank)])
                        A("act", ACOPY(vnew[0:8, r * 4:(r + 1) * 4, :], P[bank][0:8, :].rearrange("p (a b) -> p a b", a=4)), reads=[pk(bank)], writes=["vnew"])
                if not samp:
                    steps = [(bl, kv, kb) for bl in range(nb) for kv in range(2) for kb in range(2)]
                    ns = len(steps)

                    def st1(k):
                        bl, kv, kb = steps[k]
                        blk = c0 // 128 + bl
                        kblk = blk - 1 + kb
                        ks = kslot(kblk)
                        ps_ = slice(kv * 64, (kv + 1) * 64)
                        sb = (5, 4)[k % 2]
                        A("pe", MM(P[sb][:, :], kT[ps_, ks * 128:(ks + 1) * 128], qT[ps_, :, bl * 128:(bl + 1) * 128]), reads=[("kT", kblk), "qT"], writes=[pk(sb)])

                    def st2(k):
                        sb = (5, 4)[k % 2]
                        A("act", ACT(pexp2[:, k % 2, :], P[sb][:, :], AF.Exp, bias=negM, scale=0.125), reads=[pk(sb), "small"], writes=[("pexp", k % 2)])

                    def st3(k):
                        bl, kv, kb = steps[k]
                        blk = c0 // 128 + bl
                        pslot = pTb[:, k % 2, :]
                        pkey = ("pT", k % 2)
                        A("dve", TT(pslot, pexp2[:, k % 2, :], emask[:, kb, kv, :], ALU.mult), reads=[("pexp", k % 2), "emask"], writes=[pkey])
                        if blk == 3 and kb == 0:
                            A("dve", TS(pslot, pslot, vcol(V_CMASK), ALU.mult), reads=[pkey, "vecs"], writes=[pkey])

                    def st4(k):
                        bl, kv, kb = steps[k]
                        blk = c0 // 128 + bl
                        kblk = blk - 1 + kb
                        ks = kslot(kblk)
                        ps_ = slice(kv * 64, (kv + 1) * 64)
                        pslot = pTb[:, k % 2, :]
                        pkey = ("pT", k % 2)
                        A("pe", MM(P[2][ps_, :], vall[:, ks, ps_], pslot, kb == 0, kb == 1), reads=[("vall", kblk), pkey], writes=[pk(2)])
                        A("pe", MM(P[3][ps_, :], cmat[:, ONES1, 0:64], pslot, kb == 0, kb == 1), reads=["cmat", pkey], writes=[pk(3)])
                        if kv == 1 and kb == 1:
                            qo = bl * 128
                            for j in range(4):
                                A("dve", TS(denr[:, j * 128:(j + 1) * 128], P[3][:, j * 128:(j + 1) * 128], sinkexp[:, j:j + 1], ALU.add), reads=[pk(3), "small"], writes=["rstdq"])
                            A("dve", RECIP(denr[:, :], denr[:, :]), reads=["rstdq"], writes=["rstdq"])
                            A("dve", TT(rhso[:, 4:8, qo:qo + 128], P[2][:, :].rearrange("p (a c) -> p a c", a=4), denr[:, :].rearrange("p (a c) -> p a c", a=4), ALU.mult),
                              reads=[pk(2), "rstdq"], writes=[("rhso", 4 + j) for j in range(4)])

                    for t in range(ns + 3):
                        if t < ns:
                            st1(t)
                        if 0 <= t - 1 < ns:
                            st2(t - 1)
                        if 0 <= t - 2 < ns:
                            st3(t - 2)
                        if 0 <= t - 3 < ns:
                            st4(t - 3)
                else:
                    for kv in range(2):
                        ps_ = slice(kv * 64, (kv + 1) * 64)
                        for b in range(16):
                            A("pe", MM(P[5][:, b * 32:(b + 1) * 32], ckT[ps_, b, :], qT[ps_, :, b * 8:(b + 1) * 8]), reads=["ckT", "qT"], writes=[pk(5)])
                        for b in range(16):
                            A("pe", MM(P[4][0:8, b * 32:(b + 1) * 32], kTs[ps_, b * 8:(b + 1) * 8], qT[ps_, :, b * 8:(b + 1) * 8]), reads=["kTs", "qT"], writes=[pk(4)])
                        A("act", ACT(pexp, P[5][:, :], AF.Exp, bias=negM, scale=0.125), reads=[pk(5), "small"], writes=[("pexp", 0)])
                        A("act", ACT(denr[0:8, :], P[4][0:8, :], AF.Exp, bias=small[0:8, 3:4], scale=0.125), reads=[pk(4), "small"], writes=["rstdq"])
                        p1 = pTb[:, 0, :]
                        p2 = pTb[0:8, 1, :]
                        k1, k2 = ("pT", 0), ("pT", 1)
                        A("dve", TT(b16(p1), b16(pexp), esamp[:, kv:kv + 1, :].to_broadcast([128, 16, 32]), ALU.mult), reads=[("pexp", 0), "esamp"], writes=[k1])
                        A("dve", TT(b16(p2), b16(denr[0:8, :]), esnew[0:8, kv:kv + 1, :].to_broadcast([8, 16, 32]), ALU.mult), reads=["rstdq", "esnew"], writes=[k2])
                        for b in range(16):
                            cs = slice(b * 32, (b + 1) * 32)
                            A("pe", MM(P[2][ps_, cs], cv[:, b, ps_], p1[:, cs], True, False), reads=["cv", k1], writes=[pk(2)])
                            A("pe", MM(P[2][ps_, cs], vnew[0:8, b, ps_], p2[:, cs], False, True), reads=["vnew", k2], writes=[pk(2)])
                            A("pe", MM(P[3][ps_, cs], cmat[:, ONES1, 0:64], p1[:, cs], True, False), reads=["cmat", k1], writes=[pk(3)])
                            A("pe", MM(P[3][ps_, cs], cmat[0:8, ONES1, 0:64], p2[:, cs], False, True), reads=["cmat", k2], writes=[pk(3)])
                    den4 = denr[:, :].rearrange("p (b j t) -> p j b t", b=16, j=4)
                    d34 = P[3][:, :].rearrange("p (b j t) -> p j b t", b=16, j=4)
                    o24 = P[2][:, :].rearrange("p (b j t) -> p j b t", b=16, j=4)
                    for j in range(4):
                        A("dve", TS(den4[:, j], d34[:, j], sinkexp[:, j:j + 1], ALU.add), reads=[pk(3), "small"], writes=["rstdq"])
                    A("dve", RECIP(denr[:, :], denr[:, :]), reads=["rstdq"], writes=["rstdq"])
                    for j in range(4):
                        A("dve", TT(b16(rhso[:, 4 + j, 0:128]), o24[:, j], den4[:, j], ALU.mult), reads=[pk(2), "rstdq"], writes=[("rhso", 4 + j)])

            for (c0, n, samp) in allgroups:
                last = (c0 + n == OWN1) and not samp
                if samp:
                    A("pool", None, writes=alias_keys)
                    A("pool", DMA(ext[:, :, :, 0:30], cconvT_d[i].rearrange("p (a b c) -> p a b c", a=4, b=16)), writes=["ext"], dma="csamp%d" % L)
                    A("pool", DMA(ckT[:].rearrange("p a b -> p (a b)"), ckT_d[i]), writes=["ckT"], dma="csamp%d" % L)
                    A("pool", DMA(cv[:], cv_d[i].rearrange("b k f -> k b f")), writes=["cv"], dma="csamp%d" % L)
                norm_group(c0, n, gbase, sq, rstd, lambda ch, n=n: xn[:, ch, 0:n], ["xn"])
                la = S.capture(lambda: chain_conv(c0, n, samp, last))
                lb = S.capture(lambda: chain_attn(c0, n, samp, last))
                S.replay_merged(la, lb)
                for dc in range(NCH):
                    for kc in range(NCH):
                        A("pe", MM(yps(dc, n), w_out[:, kc, dc * 128:(dc + 1) * 128], rhso[:, kc, 0:n], kc == 0, kc == NCH - 1), reads=["w_out", ("rhso", kc)], writes=[pk(dc // 2)])
                add_y_to_x(c0, n)

        def mixer_odd(L):
            i = L // 2
            start_blk = (0, 1, 0, 2)[L]
            S.new_phase()
            ar = Arena(ABASE)

            def mk(name, shape, dt):
                return S.buf(name, shape, dt, ar.take(shape, dt))

            wpl = mk("wpl", [128, 4, 2, 256], BF16)
            invc = mk("invc", [128, 4, 128], F32)
            sq = mk("sq", [128, NCH, 272], BF16)
            rstd = mk("rstd", [128, 272], F32)
            xw = mk("xw", [128, NCH, 272], F32)
            sA = mk("sA", [128, NCH, 272], F32)
            sB = mk("sB", [128, NCH, 272], F32)
            dd = mk("dd", [128, NCH, 256], BF16)
            tmp = mk("tmp", [128, 128], F32)
            ep = mk("ep", [128, NCH, 16, 23], F32)
            eA = mk("eA", [128, NCH, 16, 23], F32)
            eB = mk("eB", [128, NCH, 16, 23], F32)
            A("pool", DMA(wpl[:].rearrange("p a b c -> p (a b c)"), w_pool_d[i]), writes=["wpl"], dma="wmix%d" % L)
            A("sp", DMA(invc[:].rearrange("p a b -> p (a b)"), invcnt_d[:, :]), writes=["invc"], dma="cmix%d" % L)
            A("sp", DMA(ep[:, :, :, 0:15], cpoolT_d[i].rearrange("p (a b c) -> p a b c", a=8, b=16)), writes=["ep"], dma="cmix%d" % L)
            gbase = V_NMIX + L * 8

            def pool_out(c0, n):
                for g in range(4):
                    for oc in range(2):
                        dc = 2 * g + oc
                        for kc in range(2):
                            A("pe", MM(yps(dc, n), wpl[:, g, kc, oc * 128:(oc + 1) * 128], dd[:, 2 * g + kc, 0:n], kc == 0, kc == 1), reads=["wpl", "dd"], writes=[pk(dc // 2)])
                for dc in range(NCH):
                    xs = x[:, dc, c0:c0 + n]
                    A("dve", STT(xs, yps(dc, n), vcol(V_PSC + 8 * i + dc), xs, ALU.mult, ALU.add),
                      reads=[pk(dc // 2), "vecs"] + xkeys(c0, n, [dc]), writes=xkeys(c0, n, [dc]))

            prev_n = None
            for (c0, n) in split_groups(start_blk * 128, OWN1):
                last = (c0 + n == OWN1)
                m = n + 16
                if prev_n is None:
                    norm_group(c0 - 16, m, gbase, sq, rstd, lambda ch, m=m: xw[:, ch, 0:m], ["xw"])
                else:
                    A("dve", TCOPY(xw[:, :, 0:16], xw[:, :, prev_n:prev_n + 16]), reads=["xw"], writes=["xw"])
                    norm_group(c0, n, gbase, sq, rstd, lambda ch, n=n: xw[:, ch, 16:16 + n], ["xw"])
                prev_n = n
                A("dve", TT(sA[:, :, 1:m], xw[:, :, 1:m], xw[:, :, 0:m - 1], ALU.add), reads=["xw"], writes=["sA"])
                A("dve", TT(sB[:, 2:8, 3:m], sA[:, 2:8, 3:m], sA[:, 2:8, 1:m - 2], ALU.add), reads=["sA"], writes=["sB"])
                A("dve", TT(sA[:, 4:8, 7:m], sB[:, 4:8, 7:m], sB[:, 4:8, 3:m - 4], ALU.add), reads=["sB"], writes=["sA"])
                A("dve", TT(sB[:, 6:8, 15:m], sA[:, 6:8, 15:m], sA[:, 6:8, 7:m - 8], ALU.add), reads=["sA"], writes=["sB"])
                for g in range(4):
                    src = sA if g in (0, 2) else sB
                    A("dve", STT(dd[:, 2 * g:2 * g + 2, 0:n], src[:, 2 * g:2 * g + 2, 16:16 + n], 1.0 / POOL_W[g], xw[:, 2 * g:2 * g + 2, 16:16 + n], ALU.mult, ALU.subtract),
                      reads=["sA", "sB", "xw"], writes=["dd"])
                    if c0 <= HALO < c0 + n:
                        o = HALO - c0
                        for cc in range(2):
                            ch = 2 * g + cc
                            A("dve", TT(tmp[:, :], src[:, ch, 16 + o:16 + o + 128], invc[:, g, :], ALU.mult), reads=["sA", "sB", "invc"], writes=["tmp"])
                            A("dve", TT(dd[:, ch, o:o + 128], tmp[:, :], xw[:, ch, 16 + o:16 + o + 128], ALU.subtract), reads=["tmp", "xw"], writes=["dd"])
                if last:
                    A("sp", DMA(o_xnT[i].rearrange("p (a b) -> p a b", a=8), xw[:, :, m - 15:m]), reads=["xw"], writes=[("o_xn", i)], dma="outs", batch=None)
                    out_ops.append(("o_xn", i))
                pool_out(c0, n)
            c0, n = OWN1, 128
            norm_group(c0, n, gbase, sq, rstd, lambda ch: ep[:, ch, :, 15:23], ["ep"], view=b16)
            A("dve", TT(eA[:, :, :, 1:23], ep[:, :, :, 1:23], ep[:, :, :, 0:22], ALU.add), reads=["ep"], writes=["eA"])
            A("dve", TT(eB[:, 2:8, :, 3:23], eA[:, 2:8, :, 3:23], eA[:, 2:8, :, 1:21], ALU.add), reads=["eA"], writes=["eB"])
            A("dve", TT(eA[:, 4:8, :, 7:23], eB[:, 4:8, :, 7:23], eB[:, 4:8, :, 3:19], ALU.add), reads=["eB"], writes=["eA"])
            A("dve", TT(eB[:, 6:8, :, 15:23], eA[:, 6:8, :, 15:23], eA[:, 6:8, :, 7:15], ALU.add), reads=["eA"], writes=["eB"])
            for g in range(4):
                src = eA if g in (0, 2) else eB
                for cc in range(2):
                    ch = 2 * g + cc
                    A("dve", STT(b16(dd[:, ch, 0:128]), src[:, ch, :, 15:23], 1.0 / POOL_W[g], ep[:, ch, :, 15:23], ALU.mult, ALU.subtract), reads=["eA", "eB", "ep"], writes=["dd"])
            A("sp", DMA(o_xnT_s[i].rearrange("p (a b t) -> p a b t", a=8, b=16), ep[:, :, :, 15:23]), reads=["ep"], writes=[("o_xns", i)], dma="outs", batch=None)
            out_ops.append(("o_xns", i))
            pool_out(c0, n)

        for L in range(n_layers):
            if L >= 1:
                xs = x[:, :, 0:HALO]
                A("dve", TS(xs, xs, vcol(V_CMASK), ALU.mult), reads=xkeys(0, HALO) + ["vecs"], writes=xkeys(0, HALO))
            if L % 2 == 0:
                mixer_even(L)
            else:
                mixer_odd(L)
            ffn(L)

        for ch in range(NCH):
            A("sp", DMA(yT[ch * 128:(ch + 1) * 128, :], x[:, ch, HALO:TOK]), reads=xkeys(HALO, NOUT, [ch]), writes=[("o_y", ch)], dma="outy")
            out_ops.append(("o_y", ch))
        A("sp", None, reads=list(dict.fromkeys(out_ops)))
        S.emit()
    return nc


_PROG = {}


def _alibi_slopes():
    return np.array([2.0 ** (-8.0 * (h + 1) / 8) for h in range(8)], dtype=np.float64)


def _const_inputs():
    cm = np.zeros((128, 5, 128), np.float32)
    cm[:, 0] = 1.0 / 1024
    cm[:, 1] = 1.0 / 512
    cm[0:64, 2, 0:64] = 1.0 / 64
    cm[64:128, 2, 64:128] = 1.0 / 64
    cm[:, 3] = 1.0
    cm[:, 4] = np.eye(128, dtype=np.float32)
    sl = _alibi_slopes()
    s = np.arange(128)[:, None]
    q = np.arange(128)[None, :]
    em = np.zeros((128, 3, 2, 4, 128), np.float64)
    for kv in range(2):
        for j in range(4):
            h = kv * 4 + j
            dprev = 128 + q - s
            em[:, 0, kv, j] = np.where(dprev < 128, np.exp(-sl[h] * dprev), 0.0)
            dcur = q - s
            em[:, 1, kv, j] = np.where(dcur >= 0, np.exp(-sl[h] * dcur), 0.0)
    em[:, 2] = em[:, 0]
    es = np.zeros((128, 2, 4, 8), np.float64)
    en = np.zeros((8, 2, 4, 8), np.float64)
    ii = np.arange(8)[None, :]
    for kv in range(2):
        for j in range(4):
            h = kv * 4 + j
            d1 = 128 + ii - np.arange(128)[:, None]
            es[:, kv, j] = np.where(d1 < 128, np.exp(-sl[h] * d1), 0.0)
            d2 = ii - np.arange(8)[:, None]
            en[:, kv, j] = np.where(d2 >= 0, np.exp(-sl[h] * d2), 0.0)
    return cm.reshape(128, 640), em.astype(np.float32), es.astype(np.float32).reshape(128, 64), en.astype(np.float32).reshape(8, 64)


def _prep_shared(norm_mix, norm_ffn, w_in, q_norm, k_norm, sinks, w_dw, b_dw, conv_norm_g, conv_norm_b, w_out,
                 w_pool, pool_scale, w_gate, w_up, w_down):
    f = np.float32
    vecs = np.zeros((128, NV), f)
    vecs[:, V_NMIX:V_NMIX + 32] = np.asarray(norm_mix, f).reshape(4, 8, 128).transpose(2, 0, 1).reshape(128, 32)
    vecs[:, V_NFFN:V_NFFN + 32] = np.asarray(norm_ffn, f).reshape(4, 8, 128).transpose(2, 0, 1).reshape(128, 32)
    vecs[:, V_BDW:V_BDW + 8] = np.asarray(b_dw, f).reshape(2, 4, 128).transpose(2, 0, 1).reshape(128, 8)
    vecs[:, V_CNG:V_CNG + 8] = np.asarray(conv_norm_g, f).reshape(2, 4, 128).transpose(2, 0, 1).reshape(128, 8)
    vecs[:, V_CNB:V_CNB + 8] = np.asarray(conv_norm_b, f).reshape(2, 4, 128).transpose(2, 0, 1).reshape(128, 8)
    vecs[:, V_PSC:V_PSC + 16] = np.asarray(pool_scale, f).reshape(2, 8, 128).transpose(2, 0, 1).reshape(128, 16)
    pidx = np.arange(128)
    vecs[:, V_QN:V_QN + 2] = np.asarray(q_norm, f)[:, pidx % 64].T
    vecs[:, V_KN:V_KN + 2] = np.asarray(k_norm, f)[:, pidx % 64].T
    sk = np.asarray(sinks, f)
    for i in range(2):
        for j in range(4):
            vecs[:, V_SINK + 4 * i + j] = sk[i, (pidx // 64) * 4 + j]
    w_in = np.asarray(w_in, f)
    cols = list(range(1024))
    for j in range(4):
        for kv in range(2):
            h = kv * 4 + j
            cols.extend(range(1024 + h * 64, 1024 + (h + 1) * 64))
    cols.extend(range(1536, 1792))
    w_in_p = w_in[:, :, cols]
    w_in_l = np.ascontiguousarray(w_in_p.reshape(2, 8, 128, DIN).transpose(0, 2, 1, 3)).reshape(2, 128, 8 * DIN)
    w_out = np.asarray(w_out, f)
    rows = list(range(512))
    for j in range(4):
        for kv in range(2):
            h = kv * 4 + j
            rows.extend(range(512 + h * 64, 512 + (h + 1) * 64))
    w_out_l = np.ascontiguousarray(w_out[:, rows, :].reshape(2, 8, 128, D).transpose(0, 2, 1, 3)).reshape(2, 128, 8 * D)
    w_dwT = np.ascontiguousarray(np.asarray(w_dw, f).reshape(2, 31, 4, 128).transpose(0, 3, 2, 1)).reshape(2, 128, 4 * 31)
    w_pool_l = np.ascontiguousarray(np.asarray(w_pool, f).reshape(2, 4, 2, 128, 256).transpose(0, 3, 1, 2, 4)).reshape(2, 128, 2048)
    wg = np.asarray(w_gate, f).reshape(4, 8, 128, NFC, 128).transpose(0, 3, 2, 1, 4).reshape(4, NFC, 128, 1024)
    wu = np.asarray(w_up, f).reshape(4, 8, 128, NFC, 128).transpose(0, 3, 2, 1, 4).reshape(4, NFC, 128, 1024)
    wd = np.asarray(w_down, f).reshape(4, NFC, 128, 1024)
    w_ffn = np.ascontiguousarray(np.concatenate([wg, wu, wd], axis=3))
    return dict(vecs=vecs, w_in=w_in_l, w_out=w_out_l, w_dwT=w_dwT, w_pool=w_pool_l, w_ffn=w_ffn)


def kernel(x_prompt, x_sample, cache_conv, cache_k, cache_v, state_pool, norm_mix, norm_ffn, w_in, q_norm, k_norm,
           sinks, w_dw, b_dw, conv_norm_g, conv_norm_b, w_out, w_pool, pool_scale, w_gate, w_up, w_down, _n_layers=4):
    f = np.float32
    x_prompt = np.asarray(x_prompt, f)
    x_sample = np.asarray(x_sample, f)
    cache_conv = np.asarray(cache_conv, f)
    cache_k = np.asarray(cache_k, f).reshape(2, 128, 128, 128)
    cache_v = np.asarray(cache_v, f).reshape(2, 128, 128, 128)
    state_pool = np.asarray(state_pool, f)
    shared = _prep_shared(norm_mix, norm_ffn, w_in, q_norm, k_norm, sinks, w_dw, b_dw, conv_norm_g, conv_norm_b,
                          w_out, w_pool, pool_scale, w_gate, w_up, w_down)
    cmat, emask, esamp, esnew = _const_inputs()
    if _n_layers not in _PROG:
        _PROG[_n_layers] = build_program(_n_layers)
    nc = _PROG[_n_layers]
    in_maps = []
    for c in range(8):
        s, half = c // 2, c % 2
        xt = np.zeros((TOK, D), f)
        t0 = half * 2048
        if half == 1:
            xt[0:HALO] = x_prompt[s, t0 - HALO:t0]
        xt[HALO:OWN1] = x_prompt[s, t0:t0 + 2048]
        xt[OWN1:TOK] = x_sample[16 * c:16 * c + 16].reshape(128, D)
        bs = slice(16 * c, 16 * c + 16)
        vecs = shared["vecs"].copy()
        vecs[:, V_CMASK] = float(half)
        em = emask[:, 0:2]
        invc = np.zeros((128, 4, 128), f)
        for g, w in enumerate(POOL_W):
            if half == 0:
                invc[:, g, :] = 1.0 / np.minimum(w, np.arange(128) + 1)
            else:
                invc[:, g, :] = 1.0 / w
        m = dict(
            xT=np.ascontiguousarray(xt.T), vecs=vecs, cmat=cmat, emask=np.ascontiguousarray(em).reshape(128, 2048), esamp=esamp, esnew=esnew,
            invcnt=invc.reshape(128, 512), w_in=shared["w_in"], w_out=shared["w_out"], w_dwT=shared["w_dwT"],
            w_pool=shared["w_pool"], w_ffn=shared["w_ffn"],
            cconvT=np.ascontiguousarray(cache_conv[:, bs].reshape(2, 16, 30, 4, 128).transpose(0, 4, 3, 1, 2)).reshape(2, 128, 1920),
            ckT=np.ascontiguousarray(cache_k[:, bs].transpose(0, 3, 1, 2)).reshape(2, 128, 2048),
            cv_nat=np.ascontiguousarray(cache_v[:, bs]), ck_nat=np.ascontiguousarray(cache_k[:, bs]),
            cconv_nat=np.ascontiguousarray(cache_conv[:, bs]), cpool_nat=np.ascontiguousarray(state_pool[:, bs]),
            cpoolT=np.ascontiguousarray(state_pool[:, bs].reshape(2, 16, 15, 8, 128).transpose(0, 4, 3, 1, 2)).reshape(2, 128, 1920),
        )
        in_maps.append(m)
    res = run_bass_kernel_spmd(nc, in_maps, core_ids=list(range(8)))
    return _assemble(res.results)


def _assemble(R):
    f = np.float32
    y_prompt = np.zeros((4, 4096, D), f)
    y_sample = np.zeros((128, 8, D), f)
    conv_p = np.zeros((2, 4, 30, 512), f)
    k_p = np.zeros((2, 4, 128, 2, 64), f)
    v_p = np.zeros((2, 4, 128, 2, 64), f)
    pool_p = np.zeros((2, 4, 15, D), f)
    conv_s = np.zeros((2, 128, 30, 512), f)
    k_s = np.zeros((2, 128, 128, 2, 64), f)
    v_s = np.zeros((2, 128, 128, 2, 64), f)
    pool_s = np.zeros((2, 128, 15, D), f)
    for c in range(8):
        r = R[c]
        s, half = c // 2, c % 2
        yt = r["yT"].T
        y_prompt[s, half * 2048:(half + 1) * 2048] = yt[0:2048]
        y_sample[16 * c:16 * c + 16] = yt[2048:].reshape(16, 8, D)
        bs = slice(16 * c, 16 * c + 16)
        if half == 1:
            conv_p[:, s] = r["o_gluT"].reshape(2, 128, 4, 30).transpose(0, 3, 2, 1).reshape(2, 30, 512)
            k_p[:, s] = r["o_kT"].transpose(0, 2, 1).reshape(2, 128, 2, 64)
            v_p[:, s] = r["o_vtok"].reshape(2, 128, 2, 64)
            pool_p[:, s] = r["o_xnT"].reshape(2, 128, 8, 15).transpose(0, 3, 2, 1).reshape(2, 15, D)
        conv_s[:, bs, 0:22] = r["o_convs_old"]
        conv_s[:, bs, 22:30] = r["o_gluT_s"].reshape(2, 128, 4, 16, 8).transpose(0, 3, 4, 2, 1).reshape(2, 16, 8, 512)
        k_s[:, bs, 0:120] = r["o_ks_old"].reshape(2, 16, 120, 2, 64)
        k_s[:, bs, 120:128] = r["o_kT_s"].transpose(0, 2, 1).reshape(2, 16, 8, 2, 64)
        v_s[:, bs, 0:120] = r["o_vs_old"].reshape(2, 16, 120, 2, 64)
        v_s[:, bs, 120:128] = r["o_vtok_s"].reshape(2, 16, 8, 2, 64)
        pool_s[:, bs, 0:7] = r["o_pools_old"]
        pool_s[:, bs, 7:15] = r["o_xnT_s"].reshape(2, 128, 8, 16, 8).transpose(0, 3, 4, 2, 1).reshape(2, 16, 8, D)
    return (y_prompt, y_sample, conv_p, k_p, v_p, pool_p, conv_s, k_s, v_s, pool_s)
```

```python
import numpy as np
from contextlib import ExitStack
import concourse.bass as bass
import concourse.mybir as mybir
from concourse.bass_utils import run_bass_kernel_spmd

F32 = mybir.dt.float32
BF16 = mybir.dt.bfloat16
AF = mybir.ActivationFunctionType
ALU = mybir.AluOpType

ENGS = ("pe", "act", "dve", "pool", "sp")

D = 1024
NCH = 8
DFF = 2816
NFC = 22
DIN = 1792
NBLK = 20
TOK = NBLK * 128
HALO = 384
OWN1 = 2432
NOUT = TOK - HALO
RMS_EPS = 1e-6
LN_EPS = 1e-5
POOL_W = (2, 4, 8, 16)
SBUF_BASE = 16512
SBUF_LIMIT = 229312

V_NMIX = 0
V_NFFN = 32
V_BDW = 64
V_CNG = 72
V_CNB = 80
V_PSC = 88
V_QN = 104
V_KN = 106
V_SINK = 108
V_CMASK = 116
NV = 120


class Op:
    __slots__ = ("eng", "fn", "deps", "signal", "tsem", "tick", "dma", "batch", "known", "waits", "idx")


class Sched:
    def __init__(self, nc, stack):
        self.nc = nc
        self.stack = stack
        self.all = []
        self.last_w = {}
        self.readers = {}
        self.esem = {e: stack.enter_context(nc.semaphore("s_" + e)) for e in ENGS}
        self.dsem = {}
        self.dcount = {}
        self.sems = dict(self.esem)
        self.phase_deps = []
        self._cap = None
        self.nalloc = 0

    PERSIST = ("x", "vecs", "cmat", "identf", "onesf", "small", "P")

    def _persistent(self, k):
        b = self._bname(k)
        return b in self.PERSIST or b.startswith("o_")

    def new_phase(self):
        retired = {}
        for k in [k for k in self.last_w if not self._persistent(k)]:
            w = self.last_w.pop(k)
            retired[w.idx] = w
        for k in [k for k in self.readers if not self._persistent(k)]:
            for r in self.readers.pop(k).values():
                retired[r.idx] = r
        for r in self.phase_deps:
            retired[r.idx] = r
        keep = {}
        for r in retired.values():
            key = r.eng if r.dma is None else ("dma", r.idx)
            if key not in keep or keep[key].idx < r.idx:
                keep[key] = r
        self.phase_deps = list(keep.values())

    def buf(self, name, shape, dtype, off):
        size = int(np.prod(shape[1:])) * (2 if dtype == BF16 else 4)
        assert off + size <= SBUF_LIMIT, (name, off, size)
        self.nalloc += 1
        return self.nc.alloc_sbuf_tensor_at("%s_%d" % (name, self.nalloc), list(shape), dtype, offset=off)

    @staticmethod
    def _bname(k):
        return k if isinstance(k, str) else k[0]

    def capture(self, f):
        self._cap = []
        f()
        lst, self._cap = self._cap, None
        return lst

    def replay_merged(self, la, lb):
        i = j = 0
        while i < len(la) or j < len(lb):
            if j >= len(lb) or (i < len(la) and i * len(lb) <= j * len(la)):
                self.add(*la[i])
                i += 1
            else:
                self.add(*lb[j])
                j += 1

    def add(self, eng, fn, reads=(), writes=(), dma=None, batch=0):
        if self._cap is not None:
            self._cap.append((eng, fn, list(reads), list(writes), dma, batch))
            return None
        op = Op()
        op.eng, op.fn, op.signal, op.dma, op.batch = eng, fn, False, dma, batch
        op.idx = len(self.all)
        deps = {}
        for k in reads:
            w = self.last_w.get(k)
            if w is not None:
                deps[w.idx] = w
        for k in writes:
            w = self.last_w.get(k)
            if w is not None:
                deps[w.idx] = w
            rd = self.readers.get(k)
            if rd:
                for r in rd.values():
                    deps[r.idx] = r
        if self.phase_deps:
            for k in list(reads) + list(writes):
                if not self._persistent(k):
                    for r in self.phase_deps:
                        deps[r.idx] = r
                    break
        op.deps = list(deps.values())
        if fn is not None:
            for k in reads:
                rd = self.readers.setdefault(k, {})
                rd[eng if dma is None else ("dma", op.idx)] = op
            for k in writes:
                self.last_w[k] = op
                self.readers[k] = {}
        if dma is not None:
            if dma not in self.dsem:
                h = self.stack.enter_context(self.nc.semaphore("d_" + dma))
                self.dsem[dma] = h
                self.sems[dma] = h
                self.dcount[dma] = {}
            c = self.dcount[dma]
            if batch is None:
                batch = op.batch = len(c) + 1
            c[batch] = c.get(batch, 0) + 1
        for p in op.deps:
            if p.dma is None and not (p.eng == "pe" and eng == "pe" and dma is None):
                p.signal = True
        self.all.append(op)
        return op

    def finalize(self):
        count = {e: 0 for e in ENGS}
        seen = {e: {} for e in ENGS}
        bend = {}
        for name, c in self.dcount.items():
            tot = 0
            bend[name] = {}
            for b in sorted(c):
                tot += 16 * c[b]
                bend[name][b] = tot
        started = {}
        self.per_eng = {e: [] for e in ENGS}
        for op in self.all:
            E = op.eng
            sE = seen[E]
            waits = {}
            for p in sorted(op.deps, key=lambda p: -p.idx):
                if p.dma is None and p.eng == "pe" and E == "pe" and op.dma is None:
                    continue
                if p.dma is not None and p.dma == op.dma and p.batch == op.batch:
                    raise AssertionError("intra-batch DMA dependency on " + str(p.dma))
                if sE.get(p.tsem, 0) >= p.tick:
                    continue
                waits[p.tsem] = max(waits.get(p.tsem, 0), p.tick)
                for k, v in p.known.items():
                    if sE.get(k, 0) < v:
                        sE[k] = v
            if op.dma is not None:
                name = op.dma
                prev = started.get(name)
                if prev is not None and prev != op.batch:
                    assert op.batch > prev, (name, prev, op.batch)
                    v = bend[name][prev]
                    if sE.get(name, 0) < v:
                        waits[name] = max(waits.get(name, 0), v)
                        sE[name] = v
                started[name] = op.batch
                op.tsem, op.tick = name, bend[name][op.batch]
            elif op.signal:
                count[E] += 1
                op.tsem, op.tick = E, count[E]
            else:
                op.tsem, op.tick = E, count[E] + 1
            op.waits = list(waits.items())
            kn = dict(sE)
            if op.dma is not None or op.signal:
                kn[op.tsem] = max(kn.get(op.tsem, 0), op.tick)
            op.known = kn
            self.per_eng[E].append(op)

    def emit_engine(self, name, eng):
        for op in self.per_eng[name]:
            for (k, v) in op.waits:
                eng.wait_ge(self.sems[k], v)
            if op.fn is None:
                continue
            inst = op.fn(eng)
            if op.dma is not None:
                inst.then_inc(self.dsem[op.dma], 16)
            elif op.signal:
                inst.then_inc(self.esem[name], 1)

    def emit(self):
        self.finalize()
        with self.nc.Block() as block:
            @block.tensor
            def _(e):
                self.emit_engine("pe", e)

            @block.scalar
            def _(e):
                self.emit_engine("act", e)

            @block.vector
            def _(e):
                self.emit_engine("dve", e)

            @block.gpsimd
            def _(e):
                self.emit_engine("pool", e)

            @block.sync
            def _(e):
                self.emit_engine("sp", e)


class Arena:
    def __init__(self, base, limit=SBUF_LIMIT):
        self.p = base
        self.limit = limit

    def take(self, shape, dtype):
        size = int(np.prod(shape[1:])) * (2 if dtype == BF16 else 4)
        off = self.p
        self.p = (off + size + 63) // 64 * 64
        assert self.p <= self.limit, ("arena overflow", self.p)
        return off


def xkeys(c0, n, chs=range(NCH)):
    return [("x", ch, b) for ch in chs for b in range(c0 // 128, (c0 + n + 127) // 128)]


def MM(out, lhsT, rhs, start=True, stop=True):
    return lambda e: e.matmul(out, lhsT=lhsT, rhs=rhs, start=start, stop=stop)


def TR(out, in_, identity):
    return lambda e: e.transpose(out=out, in_=in_, identity=identity)


def ACT(out, in_, func, bias=None, scale=None):
    kw = {}
    if bias is not None:
        kw["bias"] = bias
    if scale is not None:
        kw["scale"] = scale
    return lambda e: e.activation(out=out, in_=in_, func=func, **kw)


def ACOPY(out, in_):
    return lambda e: e.copy(out=out, in_=in_)


def TT(out, in0, in1, op):
    return lambda e: e.tensor_tensor(out=out, in0=in0, in1=in1, op=op)


def STT(out, in0, scalar, in1, op0, op1):
    return lambda e: e.scalar_tensor_tensor(out=out, in0=in0, scalar=scalar, in1=in1, op0=op0, op1=op1)


def TS(out, in0, scalar1, op0, scalar2=None, op1=None):
    if op1 is None:
        return lambda e: e.tensor_scalar(out=out, in0=in0, scalar1=scalar1, scalar2=None, op0=op0)
    return lambda e: e.tensor_scalar(out=out, in0=in0, scalar1=scalar1, scalar2=scalar2, op0=op0, op1=op1)


def TCOPY(out, in_):
    return lambda e: e.tensor_copy(out=out, in_=in_)


def RECIP(out, in_):
    return lambda e: e.reciprocal(out=out, in_=in_)


def MSET(ap, v):
    return lambda e: e.memset(ap, v)


def TRED(out, in_, op):
    return lambda e: e.tensor_reduce(out=out, in_=in_, axis=mybir.AxisListType.X, op=op)


def DMA(out, in_):
    return lambda e: e.dma_start(out=out, in_=in_)


def build_program(n_layers=4):
    nc = bass.Bass("TRN2", target_bir_lowering=False)

    def din(name, shape):
        return nc.dram_tensor(name, list(shape), F32, kind="ExternalInput").ap()

    def dout(name, shape):
        return nc.dram_tensor(name, list(shape), F32, kind="ExternalOutput").ap()

    xT = din("xT", [D, TOK])
    vecs_d = din("vecs", [128, NV])
    cmat_d = din("cmat", [128, 5 * 128])
    emask_d = din("emask", [128, 2 * 2 * 512])
    esamp_d = din("esamp", [128, 2 * 32])
    esnew_d = din("esnew", [8, 2 * 32])
    invcnt_d = din("invcnt", [128, 4 * 128])
    w_in_d = din("w_in", [2, 128, NCH * DIN])
    w_out_d = din("w_out", [2, 128, NCH * D])
    w_dw_d = din("w_dwT", [2, 128, 4 * 31])
    w_pool_d = din("w_pool", [2, 128, 4 * 2 * 256])
    w_ffn_d = din("w_ffn", [4, NFC, 128, 3072])
    cconvT_d = din("cconvT", [2, 128, 4 * 16 * 30])
    ckT_d = din("ckT", [2, 128, 16 * 128])
    cv_d = din("cv_nat", [2, 16, 128, 128])
    ck_d = din("ck_nat", [2, 16, 128, 128])
    cconv_d = din("cconv_nat", [2, 16, 30, 512])
    cpool_d = din("cpool_nat", [2, 16, 15, 1024])
    cpoolT_d = din("cpoolT", [2, 128, 8 * 16 * 15])

    yT = dout("yT", [D, NOUT])
    o_gluT = dout("o_gluT", [2, 128, 4 * 30])
    o_kT = dout("o_kT", [2, 128, 128])
    o_vtok = dout("o_vtok", [2, 128, 128])
    o_xnT = dout("o_xnT", [2, 128, 8 * 15])
    o_convs_old = dout("o_convs_old", [2, 16, 22, 512])
    o_gluT_s = dout("o_gluT_s", [2, 128, 4 * 128])
    o_ks_old = dout("o_ks_old", [2, 16, 120, 128])
    o_kT_s = dout("o_kT_s", [2, 128, 128])
    o_vs_old = dout("o_vs_old", [2, 16, 120, 128])
    o_vtok_s = dout("o_vtok_s", [2, 128, 128])
    o_pools_old = dout("o_pools_old", [2, 16, 7, 1024])
    o_xnT_s = dout("o_xnT_s", [2, 128, 8 * 128])

    with ExitStack() as st:
        S = Sched(nc, st)
        A = S.add
        P = [nc.alloc_psum_tensor("P%d" % i, [128, 512], F32) for i in range(8)]

        def pk(i):
            return ("P", i)

        x = S.buf("x", [128, NCH, TOK], F32, SBUF_BASE)
        CB = SBUF_BASE + 81920
        vecs = S.buf("vecs", [128, NV], F32, CB)
        cmat = S.buf("cmat", [128, 5, 128], BF16, CB + 512)
        identf = S.buf("identf", [128, 128], F32, CB + 512 + 1280)
        onesf = S.buf("onesf", [128, 128], F32, CB + 512 + 1280 + 512)
        small = S.buf("small", [128, 16], F32, CB + 512 + 1280 + 1024)
        ABASE = CB + 512 + 1280 + 1024 + 64
        ONES_RMS, ONES_LN, BDIAG, ONES1, IDENT = range(5)
        out_ops = []

        def vcol(c):
            return vecs[:, c:c + 1]

        def b16(ap):
            return ap.rearrange("p (b t) -> p b t", b=16)

        def two(ap):
            return ap.rearrange("p (a c) -> p a c", a=2)

        for ch in range(NCH):
            A("sp", DMA(x[:, ch, :], xT[ch * 128:(ch + 1) * 128, :]), writes=xkeys(0, TOK, [ch]), dma="xin")
        A("sp", DMA(vecs[:], vecs_d[:, :]), writes=["vecs"], dma="cst")
        A("pool", DMA(cmat[:].rearrange("p a b -> p (a b)"), cmat_d[:, :]), writes=["cmat"], dma="cst2")
        A("sp", DMA(identf[:], cmat_d[:, 4 * 128:5 * 128]), writes=["identf"], dma="cst")
        A("sp", DMA(onesf[:], cmat_d[:, 3 * 128:4 * 128]), writes=["onesf"], dma="cst")

        for i in range(2):
            A("sp", DMA(o_convs_old[i], cconv_d[i, :, 8:30, :]), writes=[("o_shift", 0, i)], dma="shift")
            A("sp", DMA(o_ks_old[i], ck_d[i, :, 8:128, :]), writes=[("o_shift", 1, i)], dma="shift")
            A("sp", DMA(o_vs_old[i], cv_d[i, :, 8:128, :]), writes=[("o_shift", 2, i)], dma="shift")
            A("sp", DMA(o_pools_old[i], cpool_d[i, :, 8:15, :]), writes=[("o_shift", 3, i)], dma="shift")
            out_ops.extend([("o_shift", q, i) for q in range(4)])

        def rstd_from(ps_ap, dst_ap, eps, rkeys, wkeys):
            A("act", ACT(dst_ap, ps_ap, AF.Ln, bias=epsap[eps], scale=1.0), reads=list(rkeys) + ["small"], writes=wkeys)
            A("act", ACT(dst_ap, dst_ap, AF.Exp, scale=-0.5), reads=wkeys, writes=wkeys)

        A("dve", MSET(small[:, 8:9], RMS_EPS), writes=["small"])
        A("dve", MSET(small[:, 9:10], LN_EPS), writes=["small"])
        epsap = {RMS_EPS: small[:, 8:9], LN_EPS: small[:, 9:10]}

        def norm_group(c0, n, gbase, sq, rstd, dst_fn, dst_keys, view=None, bank=4, sqk="sq", rk="rstd"):
            A("act", ACT(sq[:, :, 0:n], x[:, :, c0:c0 + n], AF.Square), reads=xkeys(c0, n), writes=[sqk])
            for ch in range(NCH):
                A("pe", MM(P[bank][:, 0:n], cmat[:, ONES_RMS, :], sq[:, ch, 0:n], ch == 0, ch == NCH - 1), reads=[sqk, "cmat"], writes=[pk(bank)])
            rstd_from(P[bank][:, 0:n], rstd[:, 0:n], RMS_EPS, [pk(bank)], [rk])
            for ch in range(NCH):
                i0, i1 = x[:, ch, c0:c0 + n], rstd[:, 0:n]
                if view is not None:
                    i0, i1 = view(i0), view(i1)
                A("dve", STT(dst_fn(ch), i0, vcol(gbase + ch), i1, ALU.mult, ALU.mult),
                  reads=xkeys(c0, n, [ch]) + [rk, "vecs"], writes=dst_keys)

        def yps(dc, n):
            return P[dc // 2][:, (dc % 2) * 256:(dc % 2) * 256 + n]

        def add_y_to_x(c0, n):
            for b in range(4):
                xs = x[:, 2 * b:2 * b + 2, c0:c0 + n]
                A("dve", TT(xs, two(P[b][:, :])[:, :, 0:n], xs, ALU.add),
                  reads=[pk(b)] + xkeys(c0, n, [2 * b, 2 * b + 1]), writes=xkeys(c0, n, [2 * b, 2 * b + 1]))

        def split_groups(c_start, c_end):
            gs = []
            c = c_end
            while c > c_start:
                n = min(256, c - c_start)
                gs.append((c - n, n))
                c -= n
            return gs[::-1]

        def ffn(L):
            start_blk = (1, 1, 2, 3)[L]
            groups = split_groups(start_blk * 128, TOK)
            halves = [groups[:5], groups[5:]]
            S.new_phase()
            ar = Arena(ABASE)
            h = S.buf("h", [128, NCH, TOK], BF16, ar.take([128, NCH, TOK], BF16))
            wp = [S.buf("wp%d" % s, [128, 4, 3072], BF16, ar.take([128, 4, 3072], BF16)) for s in range(2)]
            sg = [S.buf("sg%d" % s, [128, 256], F32, ar.take([128, 256], F32)) for s in range(2)]
            abuf = [S.buf("a%d" % s, [128, 4, 256], BF16, ar.take([128, 4, 256], BF16)) for s in range(2)]
            sqs = [S.buf("sq%d" % s, [128, NCH, 256], BF16, ar.take([128, NCH, 256], BF16)) for s in range(2)]
            rstds = [S.buf("rstd%d" % s, [128, 256], F32, ar.take([128, 256], F32)) for s in range(2)]
            pieces = [(f, min(4, NFC - f)) for f in range(0, NFC, 4)]
            gbase = V_NFFN + L * 8
            bcount = [0]

            def load_piece(pi, slot):
                f0, nf = pieces[pi]
                bcount[0] += 1
                A("pool", DMA(wp[slot][:, 0:nf, :], w_ffn_d[L, f0:f0 + nf].rearrange("f p n -> p f n")),
                  writes=["wp%d" % slot], dma="wffn%d_%d" % (L, slot), batch=bcount[0])

            def norm_h(c0, n, bank, st):
                norm_group(c0, n, gbase, sqs[st], rstds[st], lambda ch: h[:, ch, c0:c0 + n], [("h", c0)], bank=bank, sqk="sq%d" % st, rk="rstd%d" % st)

            def emit_Y(slot, nf, c0, n, aslot):
                for dc in range(NCH):
                    for fl in range(nf):
                        A("pe", MM(yps(dc, n), wp[slot][:, fl, 2048 + dc * 128:2048 + (dc + 1) * 128], abuf[aslot][:, fl, 0:n], fl == 0, fl == nf - 1),
                          reads=["wp%d" % slot, "a%d" % aslot], writes=[pk(dc // 2)])
                add_y_to_x(c0, n)

            cnt = 0
            for hi, half in enumerate(halves):
                if not half:
                    continue
                other = halves[1] if hi == 0 else []
                load_piece(0, 0)
                if hi == 0:
                    norm_h(half[0][0], half[0][1], 3, 0)
                for pi, (f0, nf) in enumerate(pieces):
                    slot = pi % 2
                    if pi + 1 < len(pieces):
                        load_piece(pi + 1, 1 - slot)
                    wk = "wp%d" % slot
                    prev = None
                    for gi, (c0, n) in enumerate(half):
                        if hi == 0 and pi == 0 and gi + 1 < len(half):
                            norm_h(half[gi + 1][0], half[gi + 1][1], 3, (gi + 1) % 2)
                        if hi == 0 and pi == len(pieces) - 1 and gi < len(other):
                            norm_h(other[gi][0], other[gi][1], 7, gi % 2)
                        aslot = gi % 2
                        for fl in range(nf):
                            bank = 4 + fl
                            for kc in range(NCH):
                                A("pe", MM(P[bank][:, 0:n], wp[slot][:, fl, kc * 128:(kc + 1) * 128], h[:, kc, c0:c0 + n], kc == 0, kc == NCH - 1),
                                  reads=[wk, ("h", c0)], writes=[pk(bank)])
                            for kc in range(NCH):
                                A("pe", MM(P[bank][:, 256:256 + n], wp[slot][:, fl, 1024 + kc * 128:1024 + (kc + 1) * 128], h[:, kc, c0:c0 + n], kc == 0, kc == NCH - 1),
                                  reads=[wk, ("h", c0)], writes=[pk(bank)])
                            ss = cnt % 2
                            cnt += 1
                            A("act", ACT(sg[ss][:, 0:n], P[bank][:, 0:n], AF.Silu), reads=[pk(bank)], writes=["sg%d" % ss])
                            A("dve", TT(abuf[aslot][:, fl, 0:n], sg[ss][:, 0:n], P[bank][:, 256:256 + n], ALU.mult),
                              reads=["sg%d" % ss, pk(bank)], writes=["a%d" % aslot])
                        if prev is not None:
                            emit_Y(*prev)
                        prev = (slot, nf, c0, n, aslot)
                    emit_Y(*prev)

        def mixer_even(L):
            i = L // 2
            start_blk = (0, 0, 1, 0)[L]
            S.new_phase()
            ar = Arena(ABASE)

            def mk(name, shape, dt):
                return S.buf(name, shape, dt, ar.take(shape, dt))

            w_in = mk("w_in", [128, NCH, DIN], BF16)
            w_out = mk("w_out", [128, NCH, D], BF16)
            diag = mk("diag", [128, 4, 31, 128], BF16)
            wdw = mk("wdw", [128, 4, 32], F32)
            xn = mk("xn", [128, NCH, 256], BF16)
            sq = mk("sq", [128, NCH, 256], BF16)
            rstd = mk("rstd", [128, 512], F32)
            tmpA = mk("tmpA", [128, 4, 256], F32)
            sqq = mk("sqq", [128, 2, 256], BF16)
            rstdq = mk("rstdq", [128, 512], F32)
            denr = rstdq
            pexp2 = mk("pexp", [128, 2, 512], F32)
            pexp = pexp2[:, 0, :]
            pTb = mk("pT", [128, 2, 512], BF16)
            rhso = mk("rhso", [128, NCH, 256], BF16)
            qT = mk("qT", [128, 4, 256], BF16)
            o32 = mk("o32", [128, 4, 128], F32)
            k32 = mk("k32", [128, 128], F32)
            v32 = mk("v32", [128, 128], F32)
            esamp = mk("esamp", [128, 2, 32], F32)
            esnew = mk("esnew", [8, 2, 32], F32)
            kTs = mk("kTs", [128, 128], BF16)
            vnew = mk("vnew", [8, 16, 128], BF16)
            rbase = ar.p
            emask = mk("emask", [128, 2, 2, 512], F32)
            kT = mk("kT", [128, 5 * 128], BF16)
            vall = mk("vall", [128, 5, 128], BF16)
            glu = mk("glu", [128, 4, 288], BF16)
            rend = ar.p
            ar.p = rbase
            ext = mk("ext", [128, 4, 16, 38], BF16)
            ckT = mk("ckT", [128, 16, 128], BF16)
            cv = mk("cv", [128, 16, 128], BF16)
            assert ar.p <= rend, (ar.p, rend)
            meanb = two(rstd[:, :])
            vbf = sq[:, 0:4, :]
            sq2 = sq[:, 4:8, :]
            alias_keys = ["emask", "glu"] + [("kT", b) for b in range(-1, 19)] + [("vall", b) for b in range(-1, 19)]

            def kslot(b):
                return (b + 1) % 5

            wsem, csem = "wmix%d" % L, "cmix%d" % L
            A("pool", DMA(w_in[:].rearrange("p a b -> p (a b)"), w_in_d[i]), writes=["w_in"], dma=wsem)
            A("pool", DMA(w_out[:].rearrange("p a b -> p (a b)"), w_out_d[i]), writes=["w_out"], dma=wsem)
            A("sp", DMA(wdw[:, :, 0:31], w_dw_d[i].rearrange("p (a b) -> p a b", a=4)), writes=["wdw"], dma=csem)
            A("sp", DMA(emask[:].rearrange("p a b c -> p (a b c)"), emask_d[:, :]), writes=["emask"], dma=csem)
            A("sp", DMA(esamp[:].rearrange("p a b -> p (a b)"), esamp_d[:, :]), writes=["esamp"], dma=csem)
            A("sp", DMA(esnew[:].rearrange("p a b -> p (a b)"), esnew_d[:, :]), writes=["esnew"], dma=csem)
            for ch in range(4):
                for j in range(31):
                    A("dve", TS(diag[:, ch, j, :], cmat[:, IDENT, :], wdw[:, ch, j:j + 1], ALU.mult), reads=["cmat", "wdw"], writes=[("diag", ch, j)])
            A("dve", MSET(kT[:, :], 0.0), writes=[("kT", b) for b in range(-1, 19)])
            A("dve", MSET(vall[:].rearrange("p a b -> p (a b)"), 0.0), writes=[("vall", b) for b in range(-1, 19)])
            A("dve", MSET(glu[:].rearrange("p a b -> p (a b)"), 0.0), writes=["glu"])
            qn, kn = vcol(V_QN + i), vcol(V_KN + i)
            A("dve", TT(small[:, 0:1], qn, kn, ALU.mult), reads=["vecs"], writes=["small"])
            A("dve", TS(small[:, 1:2], small[:, 0:1], -1.0, ALU.mult), reads=["small"], writes=["small"])
            A("dve", TT(small[:, 0:1], small[:, 0:1], small[:, 1:2], ALU.max), reads=["small"], writes=["small"])
            A("pe", TR(P[7][0:1, 0:128], small[:, 0:1], identf[:]), reads=["small", "identf"], writes=[pk(7)])
            A("dve", TRED(small[0:1, 2:3], P[7][0:1, 0:128], ALU.max), reads=[pk(7)], writes=["small"])
            A("pe", MM(P[7][:, 128:129], onesf[0:1, :], small[0:1, 2:3]), reads=["small", "onesf"], writes=[pk(7)])
            A("dve", TS(small[:, 3:4], P[7][:, 128:129], -8.0, ALU.mult), reads=[pk(7)], writes=["small"])
            negM = small[:, 3:4]
            A("act", ACT(small[:, 4:8], vecs[:, V_SINK + 4 * i:V_SINK + 4 * i + 4], AF.Exp, bias=negM, scale=1.0), reads=["small", "vecs"], writes=["small"])
            sinkexp = small[:, 4:8]

            gbase = V_NMIX + L * 8
            pgroups = split_groups(start_blk * 128, OWN1)
            allgroups = [(c0, n, False) for (c0, n) in pgroups] + [(OWN1, 128, True)]
            spi = [0]

            def abank():
                b = 6 + (spi[0] % 2)
                spi[0] += 1
                return b

            def proj(dst_ps, col0, n, keyw):
                for kc in range(NCH):
                    A("pe", MM(dst_ps, w_in[:, kc, col0:col0 + 128], xn[:, kc, 0:n], kc == 0, kc == NCH - 1), reads=["w_in", "xn"], writes=[keyw])

            def chain_conv(c0, n, samp, last):
                for ch in range(4):
                    bank = abank()
                    proj(P[bank][:, 0:n], ch * 128, n, pk(bank))
                    proj(P[bank][:, 256:256 + n], 512 + ch * 128, n, pk(bank))
                    A("act", ACT(tmpA[:, ch, 0:n], P[bank][:, 256:256 + n], AF.Sigmoid), reads=[pk(bank)], writes=[("tmpA", ch)])
                    if samp:
                        A("dve", TT(ext[:, ch, :, 30:38], b16(P[bank][:, 0:128]), b16(tmpA[:, ch, 0:128]), ALU.mult), reads=[pk(bank), ("tmpA", ch)], writes=["ext"])
                    else:
                        A("dve", TT(glu[:, ch, 32:32 + n], P[bank][:, 0:n], tmpA[:, ch, 0:n], ALU.mult), reads=[pk(bank), ("tmpA", ch)], writes=["glu"])
                    if samp or last:
                        n0 = n - 128
                        A("dve", TT(o32[:, ch, :], P[bank][:, n0:n0 + 128], tmpA[:, ch, n0:n0 + 128], ALU.mult), reads=[pk(bank), ("tmpA", ch)], writes=["o32"])
                if samp:
                    A("sp", DMA(o_gluT_s[i], o32[:].rearrange("p a b -> p (a b)")), reads=["o32"], writes=[("o_glus", i)], dma="outs", batch=None)
                    out_ops.append(("o_glus", i))
                elif last:
                    A("sp", DMA(o_gluT[i].rearrange("p (a b) -> p a b", a=4), o32[:, :, 98:128]), reads=["o32"], writes=[("o_glu", i)], dma="outs", batch=None)
                    out_ops.append(("o_glu", i))
                for ch in range(4):
                    cps = P[ch // 2][:, (ch % 2) * 256:(ch % 2) * 256 + n]
                    for j in range(31):
                        rhs = ext[:, ch, :, j:j + 8] if samp else glu[:, ch, 2 + j:2 + j + n]
                        A("pe", MM(cps, diag[:, ch, j, :], rhs, j == 0, j == 30), reads=[("diag", ch, j), "ext" if samp else "glu"], writes=[pk(ch // 2)])
                    bcol = vcol(V_BDW + 4 * i + ch)
                    A("act", ACT(vbf[:, ch, 0:n], cps, AF.Identity, bias=bcol, scale=1.0), reads=[pk(ch // 2), "vecs"], writes=["sq"])
                    A("act", ACT(sq2[:, ch, 0:n], cps, AF.Square, bias=bcol, scale=1.0), reads=[pk(ch // 2), "vecs"], writes=["sq"])
                if not samp:
                    A("pool", TCOPY(glu[:, :, 0:32], glu[:, :, n:n + 32]), reads=["glu"], writes=["glu"])
                for ch in range(4):
                    A("pe", MM(P[7][:, 0:n], cmat[:, ONES_LN, :], vbf[:, ch, 0:n], ch == 0, ch == 3), reads=["sq", "cmat"], writes=[pk(7)])
                for ch in range(4):
                    A("pe", MM(P[7][:, 256:256 + n], cmat[:, ONES_LN, :], sq2[:, ch, 0:n], ch == 0, ch == 3), reads=["sq", "cmat"], writes=[pk(7)])
                A("act", ACOPY(meanb[:, 0, 0:n], P[7][:, 0:n]), reads=[pk(7)], writes=["rstd"])
                A("dve", STT(meanb[:, 1, 0:n], meanb[:, 0, 0:n], -1.0, meanb[:, 0, 0:n], ALU.mult, ALU.mult), reads=["rstd"], writes=["rstd"])
                A("dve", TT(meanb[:, 1, 0:n], P[7][:, 256:256 + n], meanb[:, 1, 0:n], ALU.add), reads=[pk(7), "rstd"], writes=["rstd"])
                rstd_from(meanb[:, 1, 0:n], meanb[:, 1, 0:n], LN_EPS, ["rstd"], ["rstd"])
                for ch in range(4):
                    cps = P[ch // 2][:, (ch % 2) * 256:(ch % 2) * 256 + n]
                    bcol = vcol(V_BDW + 4 * i + ch)
                    A("dve", STT(tmpA[:, ch, 0:n], cps, bcol, meanb[:, 0, 0:n], ALU.add, ALU.subtract), reads=[pk(ch // 2), "vecs", "rstd"], writes=[("tmpA", ch)])
                    A("dve", TT(tmpA[:, ch, 0:n], tmpA[:, ch, 0:n], meanb[:, 1, 0:n], ALU.mult), reads=[("tmpA", ch), "rstd"], writes=[("tmpA", ch)])
                    A("act", ACT(rhso[:, ch, 0:n], tmpA[:, ch, 0:n], AF.Silu, bias=vcol(V_CNB + 4 * i + ch), scale=vcol(V_CNG + 4 * i + ch)),
                      reads=[("tmpA", ch), "vecs"], writes=[("rhso", ch)])

            def chain_attn(c0, n, samp, last):
                nb = n // 128
                for pair in range(2):
                    bank = 5
                    for jj in range(2):
                        proj(P[bank][:, jj * 256:jj * 256 + n], 1024 + (pair * 2 + jj) * 128, n, pk(bank))
                    qps = two(P[bank][:, :])[:, :, 0:n]
                    A("act", ACT(sqq[:, 0:2, 0:n], qps, AF.Square), reads=[pk(bank)], writes=["sqq"])
                    for jj in range(2):
                        A("pe", MM(P[4][:, jj * 256:jj * 256 + n], cmat[:, BDIAG, :], sqq[:, jj, 0:n]), reads=["sqq", "cmat"], writes=[pk(4)])
                    r3 = two(rstdq[:, :])[:, :, 0:n]
                    rstd_from(two(P[4][:, :])[:, :, 0:n], r3, RMS_EPS, [pk(4)], ["rstdq"])
                    A("dve", STT(qT[:, 2 * pair:2 * pair + 2, 0:n], qps, qn, r3, ALU.mult, ALU.mult), reads=[pk(bank), "rstdq", "vecs"], writes=["qT"])
                bank = 5
                proj(P[bank][:, 0:n], 1536, n, pk(bank))
                A("act", ACT(sqq[:, 0, 0:n], P[bank][:, 0:n], AF.Square), reads=[pk(bank)], writes=["sqq"])
                A("pe", MM(P[4][:, 0:n], cmat[:, BDIAG, :], sqq[:, 0, 0:n]), reads=["sqq", "cmat"], writes=[pk(4)])
                rstd_from(P[4][:, 0:n], rstdq[:, 0:n], RMS_EPS, [pk(4)], ["rstdq"])
                if samp:
                    A("dve", STT(kTs[:, 0:128], P[bank][:, 0:n], kn, rstdq[:, 0:n], ALU.mult, ALU.mult), reads=[pk(bank), "rstdq", "vecs"], writes=["kTs"])
                else:
                    for bl in range(nb):
                        blk = c0 // 128 + bl
                        sl = kslot(blk)
                        A("dve", STT(kT[:, sl * 128:(sl + 1) * 128], P[bank][:, bl * 128:(bl + 1) * 128], kn, rstdq[:, bl * 128:(bl + 1) * 128], ALU.mult, ALU.mult),
                          reads=[pk(bank), "rstdq", "vecs"], writes=[("kT", blk), ("kT", blk - 5)])
                if samp or last:
                    n0 = n - 128
                    A("dve", STT(k32[:, :], P[bank][:, n0:n0 + 128], kn, rstdq[:, n0:n0 + 128], ALU.mult, ALU.mult), reads=[pk(bank), "rstdq", "vecs"], writes=["k32"])
                    okk = ("o_k", samp, i)
                    A("sp", DMA((o_kT_s if samp else o_kT)[i], k32[:, :]), reads=["k32"], writes=[okk], dma="outs", batch=None)
                    out_ops.append(okk)
                for bl in range(nb):
                    blk = c0 // 128 + bl
                    bank = 5
                    for kc in range(NCH):
                        A("pe", MM(P[bank][:, 0:128], xn[:, kc, bl * 128:(bl + 1) * 128], w_in[:, kc, 1664:1792], kc == 0, kc == NCH - 1), reads=["w_in", "xn"], writes=[pk(bank)])
                    if not samp:
                        A("act", ACOPY(vall[:, kslot(blk), :], P[bank][:, 0:128]), reads=[pk(bank)], writes=[("vall", blk), ("vall", blk - 5)])
                    if samp or (last and bl == nb - 1):
                        A("act", ACOPY(v32[:, :], P[bank][:, 0:128]), reads=[pk(bank)], writes=["v32"])
                        ovk = ("o_v", samp, i)
                        A("sp", DMA((o_vtok_s if samp else o_vtok)[i], v32[:, :]), reads=["v32"], writes=[ovk], dma="outs", batch=None)
                        out_ops.append(ovk)
                if samp:
                    for r in range(4):
                        bank = 5
                        for bb in range(4):
                            b = r * 4 + bb
                            for kc in range(NCH):
                                A("pe", MM(P[bank][0:8, bb * 128:(bb + 1) * 128], xn[:, kc, b * 8:(b + 1) * 8], w_in[:, kc, 1664:1792], kc == 0, kc == NCH - 1), reads=["w_in", "xn"], writes=[pk(bank)])
                        A("act", ACOPY(vnew[0:8, r * 4:(r + 1) * 4, :], P[bank][0:8, :].rearrange("p (a b) -> p a b", a=4)), reads=[pk(bank)], writes=["vnew"])
                if not samp:
                    steps = [(bl, kv, kb) for bl in range(nb) for kv in range(2) for kb in range(2)]
                    ns = len(steps)

                    def st1(k):
                        bl, kv, kb = steps[k]
                        blk = c0 // 128 + bl
                        kblk = blk - 1 + kb
                        ks = kslot(kblk)
                        ps_ = slice(kv * 64, (kv + 1) * 64)
                        sb = (5, 4)[k % 2]
                        A("pe", MM(P[sb][:, :], kT[ps_, ks * 128:(ks + 1) * 128], qT[ps_, :, bl * 128:(bl + 1) * 128]), reads=[("kT", kblk), "qT"], writes=[pk(sb)])

                    def st2(k):
                        sb = (5, 4)[k % 2]
                        A("act", ACT(pexp2[:, k % 2, :], P[sb][:, :], AF.Exp, bias=negM, scale=0.125), reads=[pk(sb), "small"], writes=[("pexp", k % 2)])

                    def st3(k):
                        bl, kv, kb = steps[k]
                        blk = c0 // 128 + bl
                        pslot = pTb[:, k % 2, :]
                        pkey = ("pT", k % 2)
                        A("dve", TT(pslot, pexp2[:, k % 2, :], emask[:, kb, kv, :], ALU.mult), reads=[("pexp", k % 2), "emask"], writes=[pkey])
                        if blk == 3 and kb == 0:
                            A("dve", TS(pslot, pslot, vcol(V_CMASK), ALU.mult), reads=[pkey, "vecs"], writes=[pkey])

                    def st4(k):
                        bl, kv, kb = steps[k]
                        blk = c0 // 128 + bl
                        kblk = blk - 1 + kb
                        ks = kslot(kblk)
                        ps_ = slice(kv * 64, (kv + 1) * 64)
                        pslot = pTb[:, k % 2, :]
                        pkey = ("pT", k % 2)
                        A("pe", MM(P[2][ps_, :], vall[:, ks, ps_], pslot, kb == 0, kb == 1), reads=[("vall", kblk), pkey], writes=[pk(2)])
                        A("pe", MM(P[3][ps_, :], cmat[:, ONES1, 0:64], pslot, kb == 0, kb == 1), reads=["cmat", pkey], writes=[pk(3)])
                        if kv == 1 and kb == 1:
                            qo = bl * 128
                            for j in range(4):
                                A("dve", TS(denr[:, j * 128:(j + 1) * 128], P[3][:, j * 128:(j + 1) * 128], sinkexp[:, j:j + 1], ALU.add), reads=[pk(3), "small"], writes=["rstdq"])
                            A("dve", RECIP(denr[:, :], denr[:, :]), reads=["rstdq"], writes=["rstdq"])
                            A("dve", TT(rhso[:, 4:8, qo:qo + 128], P[2][:, :].rearrange("p (a c) -> p a c", a=4), denr[:, :].rearrange("p (a c) -> p a c", a=4), ALU.mult),
                              reads=[pk(2), "rstdq"], writes=[("rhso", 4 + j) for j in range(4)])

                    for t in range(ns + 3):
                        if t < ns:
                            st1(t)
                        if 0 <= t - 1 < ns:
                            st2(t - 1)
                        if 0 <= t - 2 < ns:
                            st3(t - 2)
                        if 0 <= t - 3 < ns:
                            st4(t - 3)
                else:
                    for kv in range(2):
                        ps_ = slice(kv * 64, (kv + 1) * 64)
                        for b in range(16):
                            A("pe", MM(P[5][:, b * 32:(b + 1) * 32], ckT[ps_, b, :], qT[ps_, :, b * 8:(b + 1) * 8]), reads=["ckT", "qT"], writes=[pk(5)])
                        for b in range(16):
                            A("pe", MM(P[4][0:8, b * 32:(b + 1) * 32], kTs[ps_, b * 8:(b + 1) * 8], qT[ps_, :, b * 8:(b + 1) * 8]), reads=["kTs", "qT"], writes=[pk(4)])
                        A("act", ACT(pexp, P[5][:, :], AF.Exp, bias=negM, scale=0.125), reads=[pk(5), "small"], writes=[("pexp", 0)])
                        A("act", ACT(denr[0:8, :], P[4][0:8, :], AF.Exp, bias=small[0:8, 3:4], scale=0.125), reads=[pk(4), "small"], writes=["rstdq"])
                        p1 = pTb[:, 0, :]
                        p2 = pTb[0:8, 1, :]
                        k1, k2 = ("pT", 0), ("pT", 1)
                        A("dve", TT(b16(p1), b16(pexp), esamp[:, kv:kv + 1, :].to_broadcast([128, 16, 32]), ALU.mult), reads=[("pexp", 0), "esamp"], writes=[k1])
                        A("dve", TT(b16(p2), b16(denr[0:8, :]), esnew[0:8, kv:kv + 1, :].to_broadcast([8, 16, 32]), ALU.mult), reads=["rstdq", "esnew"], writes=[k2])
                        for b in range(16):
                            cs = slice(b * 32, (b + 1) * 32)
                            A("pe", MM(P[2][ps_, cs], cv[:, b, ps_], p1[:, cs], True, False), reads=["cv", k1], writes=[pk(2)])
                            A("pe", MM(P[2][ps_, cs], vnew[0:8, b, ps_], p2[:, cs], False, True), reads=["vnew", k2], writes=[pk(2)])
                            A("pe", MM(P[3][ps_, cs], cmat[:, ONES1, 0:64], p1[:, cs], True, False), reads=["cmat", k1], writes=[pk(3)])
                            A("pe", MM(P[3][ps_, cs], cmat[0:8, ONES1, 0:64], p2[:, cs], False, True), reads=["cmat", k2], writes=[pk(3)])
                    den4 = denr[:, :].rearrange("p (b j t) -> p j b t", b=16, j=4)
                    d34 = P[3][:, :].rearrange("p (b j t) -> p j b t", b=16, j=4)
                    o24 = P[2][:, :].rearrange("p (b j t) -> p j b t", b=16, j=4)
                    for j in range(4):
                        A("dve", TS(den4[:, j], d34[:, j], sinkexp[:, j:j + 1], ALU.add), reads=[pk(3), "small"], writes=["rstdq"])
                    A("dve", RECIP(denr[:, :], denr[:, :]), reads=["rstdq"], writes=["rstdq"])
                    for j in range(4):
                        A("dve", TT(b16(rhso[:, 4 + j, 0:128]), o24[:, j], den4[:, j], ALU.mult), reads=[pk(2), "rstdq"], writes=[("rhso", 4 + j)])

            def do_norm(c0, n):
                norm_group(c0, n, gbase, sq, rstd, lambda ch: xn[:, ch, 0:n], ["xn"])

            do_norm(allgroups[0][0], allgroups[0][1])
            for gi, (c0, n, samp) in enumerate(allgroups):
                last = (c0 + n == OWN1) and not samp
                if samp:
                    A("pool", None, writes=alias_keys)
                    A("pool", DMA(ext[:, :, :, 0:30], cconvT_d[i].rearrange("p (a b c) -> p a b c", a=4, b=16)), writes=["ext"], dma="csamp%d" % L)
                    A("pool", DMA(ckT[:].rearrange("p a b -> p (a b)"), ckT_d[i]), writes=["ckT"], dma="csamp%d" % L)
                    A("pool", DMA(cv[:], cv_d[i].rearrange("b k f -> k b f")), writes=["cv"], dma="csamp%d" % L)
                la = S.capture(lambda: chain_conv(c0, n, samp, last))
                lb = S.capture(lambda: chain_attn(c0, n, samp, last))
                S.replay_merged(la, lb)
                if gi + 1 < len(allgroups):
                    do_norm(allgroups[gi + 1][0], allgroups[gi + 1][1])
                for dc in range(NCH):
                    for kc in range(NCH):
                        A("pe", MM(yps(dc, n), w_out[:, kc, dc * 128:(dc + 1) * 128], rhso[:, kc, 0:n], kc == 0, kc == NCH - 1), reads=["w_out", ("rhso", kc)], writes=[pk(dc // 2)])
                add_y_to_x(c0, n)

        def mixer_odd(L):
            i = L // 2
            start_blk = (0, 1, 0, 2)[L]
            S.new_phase()
            ar = Arena(ABASE)

            def mk(name, shape, dt):
                return S.buf(name, shape, dt, ar.take(shape, dt))

            wpl = mk("wpl", [128, 4, 2, 256], BF16)
            invc = mk("invc", [128, 4, 128], F32)
            sq = mk("sq", [128, NCH, 272], BF16)
            rstd = mk("rstd", [128, 272], F32)
            xw = mk("xw", [128, NCH, 272], F32)
            sA = mk("sA", [128, NCH, 272], F32)
            sB = mk("sB", [128, NCH, 272], F32)
            dd = mk("dd", [128, NCH, 256], BF16)
            tmp = mk("tmp", [128, 128], F32)
            ep = mk("ep", [128, NCH, 16, 23], F32)
            eA = mk("eA", [128, NCH, 16, 23], F32)
            eB = mk("eB", [128, NCH, 16, 23], F32)
            A("pool", DMA(wpl[:].rearrange("p a b c -> p (a b c)"), w_pool_d[i]), writes=["wpl"], dma="wmix%d" % L)
            A("sp", DMA(invc[:].rearrange("p a b -> p (a b)"), invcnt_d[:, :]), writes=["invc"], dma="cmix%d" % L)
            A("sp", DMA(ep[:, :, :, 0:15], cpoolT_d[i].rearrange("p (a b c) -> p a b c", a=8, b=16)), writes=["ep"], dma="cmix%d" % L)
            gbase = V_NMIX + L * 8

            def pool_out(c0, n):
                for g in range(4):
                    for oc in range(2):
                        dc = 2 * g + oc
                        for kc in range(2):
                            A("pe", MM(yps(dc, n), wpl[:, g, kc, oc * 128:(oc + 1) * 128], dd[:, 2 * g + kc, 0:n], kc == 0, kc == 1), reads=["wpl", "dd"], writes=[pk(dc // 2)])
                for dc in range(NCH):
                    xs = x[:, dc, c0:c0 + n]
                    A("dve", STT(xs, yps(dc, n), vcol(V_PSC + 8 * i + dc), xs, ALU.mult, ALU.add),
                      reads=[pk(dc // 2), "vecs"] + xkeys(c0, n, [dc]), writes=xkeys(c0, n, [dc]))

            prev_n = None
            for (c0, n) in split_groups(start_blk * 128, OWN1):
                last = (c0 + n == OWN1)
                m = n + 16
                if prev_n is None:
                    norm_group(c0 - 16, m, gbase, sq, rstd, lambda ch, m=m: xw[:, ch, 0:m], ["xw"])
                else:
                    A("dve", TCOPY(xw[:, :, 0:16], xw[:, :, prev_n:prev_n + 16]), reads=["xw"], writes=["xw"])
                    norm_group(c0, n, gbase, sq, rstd, lambda ch, n=n: xw[:, ch, 16:16 + n], ["xw"])
                prev_n = n
                A("dve", TT(sA[:, :, 1:m], xw[:, :, 1:m], xw[:, :, 0:m - 1], ALU.add), reads=["xw"], writes=["sA"])
                A("dve", TT(sB[:, 2:8, 3:m], sA[:, 2:8, 3:m], sA[:, 2:8, 1:m - 2], ALU.add), reads=["sA"], writes=["sB"])
                A("dve", TT(sA[:, 4:8, 7:m], sB[:, 4:8, 7:m], sB[:, 4:8, 3:m - 4], ALU.add), reads=["sB"], writes=["sA"])
                A("dve", TT(sB[:, 6:8, 15:m], sA[:, 6:8, 15:m], sA[:, 6:8, 7:m - 8], ALU.add), reads=["sA"], writes=["sB"])
                for g in range(4):
                    src = sA if g in (0, 2) else sB
                    A("dve", STT(dd[:, 2 * g:2 * g + 2, 0:n], src[:, 2 * g:2 * g + 2, 16:16 + n], 1.0 / POOL_W[g], xw[:, 2 * g:2 * g + 2, 16:16 + n], ALU.mult, ALU.subtract),
                      reads=["sA", "sB", "xw"], writes=["dd"])
                    if c0 <= HALO < c0 + n:
                        o = HALO - c0
                        for cc in range(2):
                            ch = 2 * g + cc
                            A("dve", TT(tmp[:, :], src[:, ch, 16 + o:16 + o + 128], invc[:, g, :], ALU.mult), reads=["sA", "sB", "invc"], writes=["tmp"])
                            A("dve", TT(dd[:, ch, o:o + 128], tmp[:, :], xw[:, ch, 16 + o:16 + o + 128], ALU.subtract), reads=["tmp", "xw"], writes=["dd"])
                if last:
                    A("sp", DMA(o_xnT[i].rearrange("p (a b) -> p a b", a=8), xw[:, :, m - 15:m]), reads=["xw"], writes=[("o_xn", i)], dma="outs", batch=None)
                    out_ops.append(("o_xn", i))
                pool_out(c0, n)
            c0, n = OWN1, 128
            norm_group(c0, n, gbase, sq, rstd, lambda ch: ep[:, ch, :, 15:23], ["ep"], view=b16)
            A("dve", TT(eA[:, :, :, 1:23], ep[:, :, :, 1:23], ep[:, :, :, 0:22], ALU.add), reads=["ep"], writes=["eA"])
            A("dve", TT(eB[:, 2:8, :, 3:23], eA[:, 2:8, :, 3:23], eA[:, 2:8, :, 1:21], ALU.add), reads=["eA"], writes=["eB"])
            A("dve", TT(eA[:, 4:8, :, 7:23], eB[:, 4:8, :, 7:23], eB[:, 4:8, :, 3:19], ALU.add), reads=["eB"], writes=["eA"])
            A("dve", TT(eB[:, 6:8, :, 15:23], eA[:, 6:8, :, 15:23], eA[:, 6:8, :, 7:15], ALU.add), reads=["eA"], writes=["eB"])
            for g in range(4):
                src = eA if g in (0, 2) else eB
                for cc in range(2):
                    ch = 2 * g + cc
                    A("dve", STT(b16(dd[:, ch, 0:128]), src[:, ch, :, 15:23], 1.0 / POOL_W[g], ep[:, ch, :, 15:23], ALU.mult, ALU.subtract), reads=["eA", "eB", "ep"], writes=["dd"])
            A("sp", DMA(o_xnT_s[i].rearrange("p (a b t) -> p a b t", a=8, b=16), ep[:, :, :, 15:23]), reads=["ep"], writes=[("o_xns", i)], dma="outs", batch=None)
            out_ops.append(("o_xns", i))
            pool_out(c0, n)

        for L in range(n_layers):
            if L >= 1:
                xs = x[:, :, 0:HALO]
                A("dve", TS(xs, xs, vcol(V_CMASK), ALU.mult), reads=xkeys(0, HALO) + ["vecs"], writes=xkeys(0, HALO))
            if L % 2 == 0:
                mixer_even(L)
            else:
                mixer_odd(L)
            ffn(L)

        for ch in range(NCH):
            A("sp", DMA(yT[ch * 128:(ch + 1) * 128, :], x[:, ch, HALO:TOK]), reads=xkeys(HALO, NOUT, [ch]), writes=[("o_y", ch)], dma="outy")
            out_ops.append(("o_y", ch))
        A("sp", None, reads=list(dict.fromkeys(out_ops)))
        S.emit()
    return nc


_PROG = {}


def _alibi_slopes():
    return np.array([2.0 ** (-8.0 * (h + 1) / 8) for h in range(8)], dtype=np.float64)


def _const_inputs():
    cm = np.zeros((128, 5, 128), np.float32)
    cm[:, 0] = 1.0 / 1024
    cm[:, 1] = 1.0 / 512
    cm[0:64, 2, 0:64] = 1.0 / 64
    cm[64:128, 2, 64:128] = 1.0 / 64
    cm[:, 3] = 1.0
    cm[:, 4] = np.eye(128, dtype=np.float32)
    sl = _alibi_slopes()
    s = np.arange(128)[:, None]
    q = np.arange(128)[None, :]
    em = np.zeros((128, 3, 2, 4, 128), np.float64)
    for kv in range(2):
        for j in range(4):
            h = kv * 4 + j
            dprev = 128 + q - s
            em[:, 0, kv, j] = np.where(dprev < 128, np.exp(-sl[h] * dprev), 0.0)
            dcur = q - s
            em[:, 1, kv, j] = np.where(dcur >= 0, np.exp(-sl[h] * dcur), 0.0)
    em[:, 2] = em[:, 0]
    es = np.zeros((128, 2, 4, 8), np.float64)
    en = np.zeros((8, 2, 4, 8), np.float64)
    ii = np.arange(8)[None, :]
    for kv in range(2):
        for j in range(4):
            h = kv * 4 + j
            d1 = 128 + ii - np.arange(128)[:, None]
            es[:, kv, j] = np.where(d1 < 128, np.exp(-sl[h] * d1), 0.0)
            d2 = ii - np.arange(8)[:, None]
            en[:, kv, j] = np.where(d2 >= 0, np.exp(-sl[h] * d2), 0.0)
    return cm.reshape(128, 640), em.astype(np.float32), es.astype(np.float32).reshape(128, 64), en.astype(np.float32).reshape(8, 64)


def _prep_shared(norm_mix, norm_ffn, w_in, q_norm, k_norm, sinks, w_dw, b_dw, conv_norm_g, conv_norm_b, w_out,
                 w_pool, pool_scale, w_gate, w_up, w_down):
    f = np.float32
    vecs = np.zeros((128, NV), f)
    vecs[:, V_NMIX:V_NMIX + 32] = np.asarray(norm_mix, f).reshape(4, 8, 128).transpose(2, 0, 1).reshape(128, 32)
    vecs[:, V_NFFN:V_NFFN + 32] = np.asarray(norm_ffn, f).reshape(4, 8, 128).transpose(2, 0, 1).reshape(128, 32)
    vecs[:, V_BDW:V_BDW + 8] = np.asarray(b_dw, f).reshape(2, 4, 128).transpose(2, 0, 1).reshape(128, 8)
    vecs[:, V_CNG:V_CNG + 8] = np.asarray(conv_norm_g, f).reshape(2, 4, 128).transpose(2, 0, 1).reshape(128, 8)
    vecs[:, V_CNB:V_CNB + 8] = np.asarray(conv_norm_b, f).reshape(2, 4, 128).transpose(2, 0, 1).reshape(128, 8)
    vecs[:, V_PSC:V_PSC + 16] = np.asarray(pool_scale, f).reshape(2, 8, 128).transpose(2, 0, 1).reshape(128, 16)
    pidx = np.arange(128)
    vecs[:, V_QN:V_QN + 2] = np.asarray(q_norm, f)[:, pidx % 64].T
    vecs[:, V_KN:V_KN + 2] = np.asarray(k_norm, f)[:, pidx % 64].T
    sk = np.asarray(sinks, f)
    for i in range(2):
        for j in range(4):
            vecs[:, V_SINK + 4 * i + j] = sk[i, (pidx // 64) * 4 + j]
    w_in = np.asarray(w_in, f)
    cols = list(range(1024))
    for j in range(4):
        for kv in range(2):
            h = kv * 4 + j
            cols.extend(range(1024 + h * 64, 1024 + (h + 1) * 64))
    cols.extend(range(1536, 1792))
    w_in_p = w_in[:, :, cols]
    w_in_l = np.ascontiguousarray(w_in_p.reshape(2, 8, 128, DIN).transpose(0, 2, 1, 3)).reshape(2, 128, 8 * DIN)
    w_out = np.asarray(w_out, f)
    rows = list(range(512))
    for j in range(4):
        for kv in range(2):
            h = kv * 4 + j
            rows.extend(range(512 + h * 64, 512 + (h + 1) * 64))
    w_out_l = np.ascontiguousarray(w_out[:, rows, :].reshape(2, 8, 128, D).transpose(0, 2, 1, 3)).reshape(2, 128, 8 * D)
    w_dwT = np.ascontiguousarray(np.asarray(w_dw, f).reshape(2, 31, 4, 128).transpose(0, 3, 2, 1)).reshape(2, 128, 4 * 31)
    w_pool_l = np.ascontiguousarray(np.asarray(w_pool, f).reshape(2, 4, 2, 128, 256).transpose(0, 3, 1, 2, 4)).reshape(2, 128, 2048)
    wg = np.asarray(w_gate, f).reshape(4, 8, 128, NFC, 128).transpose(0, 3, 2, 1, 4).reshape(4, NFC, 128, 1024)
    wu = np.asarray(w_up, f).reshape(4, 8, 128, NFC, 128).transpose(0, 3, 2, 1, 4).reshape(4, NFC, 128, 1024)
    wd = np.asarray(w_down, f).reshape(4, NFC, 128, 1024)
    w_ffn = np.ascontiguousarray(np.concatenate([wg, wu, wd], axis=3))
    return dict(vecs=vecs, w_in=w_in_l, w_out=w_out_l, w_dwT=w_dwT, w_pool=w_pool_l, w_ffn=w_ffn)


def kernel(x_prompt, x_sample, cache_conv, cache_k, cache_v, state_pool, norm_mix, norm_ffn, w_in, q_norm, k_norm,
           sinks, w_dw, b_dw, conv_norm_g, conv_norm_b, w_out, w_pool, pool_scale, w_gate, w_up, w_down, _n_layers=4):
    f = np.float32
    x_prompt = np.asarray(x_prompt, f)
    x_sample = np.asarray(x_sample, f)
    cache_conv = np.asarray(cache_conv, f)
    cache_k = np.asarray(cache_k, f).reshape(2, 128, 128, 128)
    cache_v = np.asarray(cache_v, f).reshape(2, 128, 128, 128)
    state_pool = np.asarray(state_pool, f)
    shared = _prep_shared(norm_mix, norm_ffn, w_in, q_norm, k_norm, sinks, w_dw, b_dw, conv_norm_g, conv_norm_b,
                          w_out, w_pool, pool_scale, w_gate, w_up, w_down)
    cmat, emask, esamp, esnew = _const_inputs()
    if _n_layers not in _PROG:
        _PROG[_n_layers] = build_program(_n_layers)
    nc = _PROG[_n_layers]
    in_maps = []
    for c in range(8):
        s, half = c // 2, c % 2
        xt = np.zeros((TOK, D), f)
        t0 = half * 2048
        if half == 1:
            xt[0:HALO] = x_prompt[s, t0 - HALO:t0]
        xt[HALO:OWN1] = x_prompt[s, t0:t0 + 2048]
        xt[OWN1:TOK] = x_sample[16 * c:16 * c + 16].reshape(128, D)
        bs = slice(16 * c, 16 * c + 16)
        vecs = shared["vecs"].copy()
        vecs[:, V_CMASK] = float(half)
        em = emask[:, 0:2]
        invc = np.zeros((128, 4, 128), f)
        for g, w in enumerate(POOL_W):
            if half == 0:
                invc[:, g, :] = 1.0 / np.minimum(w, np.arange(128) + 1)
            else:
                invc[:, g, :] = 1.0 / w
        m = dict(
            xT=np.ascontiguousarray(xt.T), vecs=vecs, cmat=cmat, emask=np.ascontiguousarray(em).reshape(128, 2048), esamp=esamp, esnew=esnew,
            invcnt=invc.reshape(128, 512), w_in=shared["w_in"], w_out=shared["w_out"], w_dwT=shared["w_dwT"],
            w_pool=shared["w_pool"], w_ffn=shared["w_ffn"],
            cconvT=np.ascontiguousarray(cache_conv[:, bs].reshape(2, 16, 30, 4, 128).transpose(0, 4, 3, 1, 2)).reshape(2, 128, 1920),
            ckT=np.ascontiguousarray(cache_k[:, bs].transpose(0, 3, 1, 2)).reshape(2, 128, 2048),
            cv_nat=np.ascontiguousarray(cache_v[:, bs]), ck_nat=np.ascontiguousarray(cache_k[:, bs]),
            cconv_nat=np.ascontiguousarray(cache_conv[:, bs]), cpool_nat=np.ascontiguousarray(state_pool[:, bs]),
            cpoolT=np.ascontiguousarray(state_pool[:, bs].reshape(2, 16, 15, 8, 128).transpose(0, 4, 3, 1, 2)).reshape(2, 128, 1920),
        )
        in_maps.append(m)
    res = run_bass_kernel_spmd(nc, in_maps, core_ids=list(range(8)))
    return _assemble(res.results)


def _assemble(R):
    f = np.float32
    y_prompt = np.zeros((4, 4096, D), f)
    y_sample = np.zeros((128, 8, D), f)
    conv_p = np.zeros((2, 4, 30, 512), f)
    k_p = np.zeros((2, 4, 128, 2, 64), f)
    v_p = np.zeros((2, 4, 128, 2, 64), f)
    pool_p = np.zeros((2, 4, 15, D), f)
    conv_s = np.zeros((2, 128, 30, 512), f)
    k_s = np.zeros((2, 128, 128, 2, 64), f)
    v_s = np.zeros((2, 128, 128, 2, 64), f)
    pool_s = np.zeros((2, 128, 15, D), f)
    for c in range(8):
        r = R[c]
        s, half = c // 2, c % 2
        yt = r["yT"].T
        y_prompt[s, half * 2048:(half + 1) * 2048] = yt[0:2048]
        y_sample[16 * c:16 * c + 16] = yt[2048:].reshape(16, 8, D)
        bs = slice(16 * c, 16 * c + 16)
        if half == 1:
            conv_p[:, s] = r["o_gluT"].reshape(2, 128, 4, 30).transpose(0, 3, 2, 1).reshape(2, 30, 512)
            k_p[:, s] = r["o_kT"].transpose(0, 2, 1).reshape(2, 128, 2, 64)
            v_p[:, s] = r["o_vtok"].reshape(2, 128, 2, 64)
            pool_p[:, s] = r["o_xnT"].reshape(2, 128, 8, 15).transpose(0, 3, 2, 1).reshape(2, 15, D)
        conv_s[:, bs, 0:22] = r["o_convs_old"]
        conv_s[:, bs, 22:30] = r["o_gluT_s"].reshape(2, 128, 4, 16, 8).transpose(0, 3, 4, 2, 1).reshape(2, 16, 8, 512)
        k_s[:, bs, 0:120] = r["o_ks_old"].reshape(2, 16, 120, 2, 64)
        k_s[:, bs, 120:128] = r["o_kT_s"].transpose(0, 2, 1).reshape(2, 16, 8, 2, 64)
        v_s[:, bs, 0:120] = r["o_vs_old"].reshape(2, 16, 120, 2, 64)
        v_s[:, bs, 120:128] = r["o_vtok_s"].reshape(2, 16, 8, 2, 64)
        pool_s[:, bs, 0:7] = r["o_pools_old"]
        pool_s[:, bs, 7:15] = r["o_xnT_s"].reshape(2, 128, 8, 16, 8).transpose(0, 3, 4, 2, 1).reshape(2, 16, 8, D)
    return (y_prompt, y_sample, conv_p, k_p, v_p, pool_p, conv_s, k_s, v_s, pool_s)
```
